# Optimizing a Trainium2 kernel written in Bass

```python
import jax, jax.numpy as jnp
from jax import lax
import numpy as np


D_MODEL = 1024
BATCH = 2
SEQ = 8192
DEPTH = 2

POOL_WINDOWS = (2, 4, 8, 16)
POOL_GROUPS = len(POOL_WINDOWS)
POOL_WIDTH = D_MODEL
POOL_GROUP = POOL_WIDTH // POOL_GROUPS
SGU_CHUNK = 128
SGU_WIDTH = D_MODEL
SGU_HEADS = 8
SGU_HEAD_DIM = SGU_WIDTH // SGU_HEADS
N_BRANCHES = 2
IN_WIDTH = POOL_WIDTH + 2 * SGU_WIDTH + N_BRANCHES * D_MODEL
D_FF = 2816
CONV_WIDTH = 3
PLE_DIM = 256
EPS = 1e-6

kernel_name = "hybrid_pool_sgu_convffn_ple"


def rmsnorm(x, g):
    xf = x.astype(jnp.float32)
    y = xf * lax.rsqrt(jnp.mean(xf * xf, axis=-1, keepdims=True) + EPS)
    return (y * g.astype(jnp.float32)).astype(x.dtype)


def pool_mixer(h, w_pool, pool_scale):
    T = h.shape[1]
    hf = h.astype(jnp.float32)
    c = jnp.cumsum(hf, axis=1)
    t = jnp.arange(T)
    outs = []
    for gi, w in enumerate(POOL_WINDOWS):
        sl = slice(gi * POOL_GROUP, (gi + 1) * POOL_GROUP)
        cg = c[..., sl]
        prev = jnp.pad(cg, ((0, 0), (w, 0), (0, 0)))[:, :T]
        cnt = jnp.minimum(t + 1, w).astype(jnp.float32)[None, :, None]
        outs.append((cg - prev) / cnt - hf[..., sl])
    pooled = jnp.stack(outs, axis=2).astype(h.dtype)
    y = jnp.einsum('btgc,gcd->btgd', pooled, w_pool)
    return y.reshape(h.shape) * pool_scale


def spatial_gating(z_uv, sgu_norm, w_spatial, b_spatial):
    B, T, _ = z_uv.shape
    z = jax.nn.gelu(z_uv, approximate=False)
    u, v = z[..., :SGU_WIDTH], z[..., SGU_WIDTH:]
    v = rmsnorm(v, sgu_norm)
    nc = T // SGU_CHUNK
    v = v.reshape(B, nc, SGU_CHUNK, SGU_HEADS, SGU_HEAD_DIM)
    mask = jnp.tril(jnp.ones((SGU_CHUNK, SGU_CHUNK), dtype=w_spatial.dtype))
    ws = w_spatial * mask[None]
    mixed = jnp.einsum('hts,bnshd->bnthd', ws, v)
    mixed = mixed + jnp.transpose(b_spatial)[None, None, :, :, None]
    return u * mixed.reshape(B, T, SGU_WIDTH)


def conv_ffn(h, w_up, conv_w, conv_b, w_down):
    T = h.shape[1]
    up = h @ w_up
    up_pad = jnp.pad(up, ((0, 0), (CONV_WIDTH - 1, 0), (0, 0)))
    conv = conv_b
    for k in range(CONV_WIDTH):
        conv = conv + conv_w[k] * up_pad[:, k:k + T]
    a, b = conv[..., :D_FF], conv[..., D_FF:]
    return (jax.nn.gelu(a, approximate=False) * b) @ w_down


def setup_inputs(seed: int = 0) -> dict:
    key = jax.random.key(seed)
    ks = jax.random.split(key, 24)
    f32 = jnp.float32
    L, D = DEPTH, D_MODEL

    def nrm(k, shape, scale):
        return jax.random.normal(k, shape, f32) * scale

    def gain(k, shape):
        return 1.0 + 0.05 * jax.random.normal(k, shape, f32)

    return {
        "x": nrm(ks[0], (BATCH, SEQ, D), 1.0),
        "p": nrm(ks[1], (DEPTH, BATCH, SEQ, PLE_DIM), 1.0),
        "mix_norm": gain(ks[2], (L, D)),
        "w_in": nrm(ks[3], (L, D, IN_WIDTH), D ** -0.5),
        "w_pool": nrm(ks[4], (L, POOL_GROUPS, POOL_GROUP, POOL_GROUP), POOL_GROUP ** -0.5),
        "pool_scale": gain(ks[5], (L, POOL_WIDTH)),
        "sgu_norm": gain(ks[6], (L, SGU_WIDTH)),
        "w_spatial": nrm(ks[7], (L, SGU_HEADS, SGU_CHUNK, SGU_CHUNK), 0.5 * SGU_CHUNK ** -0.5),
        "b_spatial": gain(ks[8], (L, SGU_HEADS, SGU_CHUNK)),
        "w_branch_a": nrm(ks[9], (L, POOL_WIDTH, D), POOL_WIDTH ** -0.5),
        "w_branch_b": nrm(ks[10], (L, SGU_WIDTH, D), SGU_WIDTH ** -0.5),
        "w_out": nrm(ks[11], (L, D, D), D ** -0.5),
        "ffn_norm": gain(ks[12], (L, D)),
        "w_up": nrm(ks[13], (L, D, 2 * D_FF), D ** -0.5),
        "conv_w": nrm(ks[14], (L, CONV_WIDTH, 2 * D_FF), CONV_WIDTH ** -0.5),
        "conv_b": nrm(ks[15], (L, 2 * D_FF), 0.02),
        "w_down": nrm(ks[16], (L, D_FF, D), D_FF ** -0.5),
        "ple_norm": gain(ks[17], (L, D)),
        "w_ple_gate": nrm(ks[18], (L, D, D), D ** -0.5),
        "w_ple": nrm(ks[19], (L, PLE_DIM, D), PLE_DIM ** -0.5),
        "final_norm": gain(ks[20], (D,)),
    }


def reference(x, p, mix_norm, w_in, w_pool, pool_scale, sgu_norm, w_spatial, b_spatial,
              w_branch_a, w_branch_b, w_out, ffn_norm, w_up, conv_w, conv_b, w_down,
              ple_norm, w_ple_gate, w_ple, final_norm):
    o_uv = POOL_WIDTH
    o_gate = POOL_WIDTH + 2 * SGU_WIDTH
    for i in range(DEPTH):
        h = rmsnorm(x, mix_norm[i])
        z = h @ w_in[i]
        z_pool = z[..., :o_uv]
        z_uv = z[..., o_uv:o_gate]
        z_gate = z[..., o_gate:]
        y_a = pool_mixer(z_pool, w_pool[i], pool_scale[i]) @ w_branch_a[i]
        y_b = spatial_gating(z_uv, sgu_norm[i], w_spatial[i], b_spatial[i]) @ w_branch_b[i]
        gates = jax.nn.sigmoid(z_gate.astype(jnp.float32)).astype(x.dtype)
        g_a, g_b = gates[..., :D_MODEL], gates[..., D_MODEL:]
        x = x + (g_a * y_a + g_b * y_b) @ w_out[i]
        h = rmsnorm(x, ffn_norm[i])
        x = x + conv_ffn(h, w_up[i], conv_w[i], conv_b[i], w_down[i])
        gate = jax.nn.sigmoid((rmsnorm(x, ple_norm[i]) @ w_ple_gate[i]).astype(jnp.float32)).astype(x.dtype)
        x = x + gate * (p[i] @ w_ple[i])
    return rmsnorm(x, final_norm)
```

```python
import os
import numpy as np
from contextlib import ExitStack
import concourse.bass as bass
import concourse.mybir as mybir
from concourse.bass_utils import run_bass_kernel_spmd

F32 = mybir.dt.float32
BF16 = mybir.dt.bfloat16
AF = mybir.ActivationFunctionType
ALU = mybir.AluOpType

D = 1024
DFF = 2816
NFF = 22
SEQ = 8192
TOK = 2048
HALO = 256
TIN = TOK + HALO
TS = 768
EPS = 1e-6
LW = 208
C_MIXN, C_FFNN, C_PLEN, C_PSC, C_CW, C_CB = 0, 8, 16, 24, 32, 164
C_FIN = 2 * LW
C_VALID = C_FIN + 8
C_EPS = C_VALID + 256
NCST = C_EPS + 8
CB_ONES, CB_BCUR, CB_BPREV, CB_BFIRST = 0, 128, 640, 1152
CB_ONE1 = 1664
NCSTB = 1792
ND = 8
SKIP = os.environ.get('KV_SKIP', '')
FFN_BLOCKS = [(0, 6), (6, 6), (12, 6), (18, 4)]


class Sched:
    def __init__(self, nc, es):
        self.nc = nc
        self.engs = {"pe": nc.tensor, "act": nc.scalar, "dve": nc.vector,
                     "pool": nc.gpsimd, "sp": nc.sync}
        self.sem = {}
        self.cnt = {}
        for e in self.engs:
            self.sem[e] = es.enter_context(nc.semaphore("s_" + e))
            self.cnt[e] = 0
        self.dsem = {}
        self.dcnt = {}
        self.drr = {}
        for q in ("pool", "sp"):
            self.dsem[q] = [es.enter_context(nc.semaphore("d_%s%d" % (q, i))) for i in range(ND)]
            self.dcnt[q] = [0] * ND
            self.drr[q] = 0
        self.seen = {e: {} for e in self.engs}
        self.lastw = {}
        self.readers = {}
        self.lastw_sub = {}
        self.readers_sub = {}
        self.nops = 0

    def _same(self, e, t, sub, tsub):
        if t[3] != self.cnt[e]:
            return False
        if sub is not None and tsub is not None and sub != tsub:
            return False
        return True

    def _deps(self, e, reads, writes, sub=None):
        deps = []
        for k in reads:
            w = self.lastw.get(k)
            if w is not None:
                if w[0] != e or w[0] == "dma":
                    deps.append(w)
                elif self._same(e, w, sub, self.lastw_sub.get(k)):
                    deps.append(w)
            if isinstance(k, tuple) and k[0] == "ps":
                for r in self.readers.get(k, {}).values():
                    if r[0] != e:
                        deps.append(r)
        for k in writes:
            w = self.lastw.get(k)
            if w is not None:
                if w[0] != e or w[0] == "dma":
                    deps.append(w)
                elif self._same(e, w, sub, self.lastw_sub.get(k)):
                    deps.append(w)
            for sk, r in self.readers.get(k, {}).items():
                if r[0] != e or r[0] == "dma":
                    deps.append(r)
                elif self._same(e, r, sub, self.readers_sub.get(k, {}).get(sk)):
                    deps.append(r)
        return deps

    def _wait(self, e, deps):
        best = {}
        for d in deps:
            if d[2] not in best or best[d[2]][3] < d[3]:
                best[d[2]] = d
        for d in best.values():
            if self.seen[e].get(d[2], 0) >= d[3]:
                continue
            self.engs[e].wait_ge(d[1], d[3])
            self.seen[e][d[2]] = d[3]

    def _reg(self, t, reads, writes, sub=None):
        for k in reads:
            self.readers.setdefault(k, {})[t[2]] = t
            self.readers_sub.setdefault(k, {})[t[2]] = sub
        for k in writes:
            self.lastw[k] = t
            self.lastw_sub[k] = sub
            self.readers[k] = {}
            self.readers_sub[k] = {}

    def op(self, e, fn, reads=(), writes=(), sub=None):
        self._wait(e, self._deps(e, reads, writes, sub))
        ins = fn()
        self.cnt[e] += 1
        ins.then_inc(self.sem[e], 1)
        t = (e, self.sem[e], "s_" + e, self.cnt[e])
        self._reg(t, reads, writes, sub)
        self.nops += 1
        return t

    def mm(self, grp, reads=(), writes=()):
        e = "pe"
        self._wait(e, self._deps(e, reads, writes))
        ins = None
        for (o, l, r, st, sp) in grp:
            ins = self.nc.tensor.matmul(o, l, r, start=st, stop=sp)
            self.nops += 1
        self.cnt[e] += 1
        ins.then_inc(self.sem[e], 1)
        t = (e, self.sem[e], "s_" + e, self.cnt[e])
        self._reg(t, reads, writes)
        return t

    def dma(self, q, out, in_, reads=(), writes=()):
        i = self.drr[q]
        self.drr[q] = (i + 1) % ND
        sem = self.dsem[q][i]
        key = "d_%s%d" % (q, i)
        deps = self._deps("dma", reads, writes)
        if self.dcnt[q][i] > 0:
            deps.append(("dma", sem, key, self.dcnt[q][i]))
        self._wait(q, deps)
        ins = self.engs[q].dma_start(out=out, in_=in_)
        self.dcnt[q][i] += 16
        ins.then_inc(sem, 16)
        t = ("dma", sem, key, self.dcnt[q][i])
        self._reg(t, reads, writes)
        self.nops += 1
        return t


class BufPool:
    def __init__(self, bufs, name):
        self.bufs = bufs
        self.name = name
        self.free = list(range(len(bufs)))

    def get(self):
        assert self.free, "pool %s exhausted" % self.name
        j = self.free.pop(0)
        return self.bufs[j], (self.name, j), j

    def put(self, j):
        assert j not in self.free
        self.free.append(j)


class Unit:
    def __init__(self, ap, keys):
        self.ap = ap
        self.keys = keys


def build_program():
    nc = bass.Bass("TRN2", target_bir_lowering=False)

    def dram(name, shape, kind="ExternalInput"):
        return nc.dram_tensor(name, shape, F32, kind=kind).ap()

    xin = dram("xT", [128, 8, TIN])
    pin = dram("pT", [2, 128, 2, TIN])
    w_in = dram("w_in", [2, 1024, 5120])
    w_pool = dram("w_pool", [2, 4, 256, 256])
    w_a = dram("w_branch_a", [2, 1024, 1024])
    w_b = dram("w_branch_b", [2, 1024, 1024])
    w_out = dram("w_out", [2, 1024, 1024])
    w_up = dram("w_up", [2, 1024, 2 * DFF])
    w_down = dram("w_down", [2, DFF, 1024])
    w_pg = dram("w_ple_gate", [2, 1024, 1024])
    w_ple = dram("w_ple", [2, 256, 1024])
    cst_in = dram("cst", [128, NCST])
    cstb_in = dram("cstb", [128, NCSTB])
    gsgu_in = dram("gsgu", [128, 2, 1024])
    bT_in = dram("bT", [128, 2, 1024])
    wsT_in = dram("wsT", [128, 2, 1024])
    mask_in = dram("mask", [128, 128])
    out = dram("outT", [128, 8, TOK], kind="ExternalOutput")

    with ExitStack() as es:
        S = Sched(nc, es)
        V = nc.vector
        G = nc.gpsimd
        A = nc.scalar

        def sb(name, shape, dt):
            return es.enter_context(nc.sbuf_tensor(name, shape, dt))

        xT = sb("xT_sb", [128, 8, TS], F32)
        hT = sb("hT_sb", [128, 8, TS], BF16)
        mixa = sb("mixa_sb", [128, 8, TS], BF16)
        cst = sb("cst_sb", [128, NCST], F32)
        cb = sb("cb_sb", [128, NCSTB], BF16)
        mask = sb("mask_sb", [128, 128], F32)
        gsgu = sb("gsgu_sb", [128, 2, 1024], F32)
        brow = sb("brow_sb", [128, 2, 1024], BF16)
        wsb = sb("wsb_sb", [128, 2, 8, 128], BF16)
        zpc = sb("zpc_sb", [128, 2, 1024], BF16)
        upc = sb("upc_sb", [128, 2, 2, NFF, 4], F32)
        bigp = BufPool([sb("pa%d" % i, [128, 8, 256], BF16) for i in range(3)], "pa")
        rsp = BufPool([sb("rs%d" % i, [128, 256], F32) for i in range(2)], "rs")
        zp_tm = sb("zp_tm", [128, 3, 1024], BF16)
        vgp = BufPool([sb("vg%d" % i, [128, 1024], F32) for i in range(2)], "vg")
        scr = BufPool([sb("sc%d" % i, [128, 2, 264], F32) for i in range(6)], "sc")
        ubuf = sb("u_sb", [128, 8, 256], F32)
        atp = BufPool([sb("at%d" % i, [128, 6, 256], BF16) for i in range(2)], "at")
        pTs = sb("pT_sb", [128, 2, TS], BF16)
        ssp = BufPool([sb("ss%d" % i, [128, 2], F32) for i in range(4)], "ss")
        ps = es.enter_context(nc.psum_tensor("ps", [128, 8, 512], F32))

        remaining = int(nc.sbuf_bytes_remaining)
        ring_pages = (remaining - 512) // 2048
        assert ring_pages >= 34, ring_pages
        wr = sb("wring", [128, ring_pages * 1024], BF16)

        bank_i = [0]

        def bank():
            b = bank_i[0] % 8
            bank_i[0] += 1
            return b

        def kp(w, l, c0, c1):
            return w[l, :, c0:c1].rearrange("(k p) c -> p k c", p=128)

        def ck(prefix, col0, n):
            return [(prefix, c) for c in range(col0 // 128, (col0 + n) // 128)]

        def q2(ap):
            return ap.rearrange("p (q t) -> p q t", q=2)

        for t3 in range(3):
            S.dma("sp", xT[:, :, t3 * 256:(t3 + 1) * 256], xin[:, :, t3 * 256:(t3 + 1) * 256],
                  writes=ck("x", t3 * 256, 256))
        S.dma("sp", cst[:], cst_in, writes=["cst"])
        S.dma("sp", mask[:], mask_in, writes=["mask"])
        S.dma("sp", gsgu[:], gsgu_in, writes=["gsgu"])
        S.dma("pool", cb[:], cstb_in, writes=["cb"])
        uflat = ubuf[:].rearrange("p a b -> p (a b)")
        S.dma("sp", uflat, wsT_in.rearrange("p l c -> p (l c)"), writes=["u"])
        for l in range(2):
            for h in range(8):
                S.op("dve", lambda l=l, h=h: V.tensor_tensor(
                    out=wsb[:, l, h, :], in0=uflat[:, l * 1024 + h * 128:l * 1024 + (h + 1) * 128],
                    in1=mask[:], op=ALU.mult), reads=["u", "mask"], writes=["wsb"])
        browf = brow[:].rearrange("p l c -> p (l c)")
        S.dma("sp", uflat, bT_in.rearrange("p l c -> p (l c)"), reads=["wsb"], writes=["u"])
        S.op("dve", lambda: V.memset(browf[0:64, :], 0.0), writes=["brow"])
        S.op("dve", lambda: V.tensor_copy(out=browf[0:1, :], in_=uflat[0:1, :]), reads=["u"], writes=["brow"])
        btmp = zp_tm[:, 0:2, :].rearrange("p a b -> p (a b)")
        S.op("dve", lambda: V.tensor_copy(out=btmp[32:33, :], in_=uflat[32:33, :]), reads=["u"],
             writes=[("zp", 0), ("zp", 1)])
        S.op("dve", lambda: V.tensor_tensor(out=uflat[32:33, :], in0=uflat[32:33, :], in1=btmp[32:33, :],
                                            op=ALU.subtract), reads=["u", ("zp", 0), ("zp", 1)], writes=["u"])
        S.op("dve", lambda: V.tensor_copy(out=browf[32:33, :], in_=uflat[32:33, :]), reads=["u"], writes=["brow"])
        S.op("dve", lambda: V.memset(zpc[:], 0.0), writes=[("zpc", 0), ("zpc", 1)])
        S.op("dve", lambda: V.memset(upc[:], 0.0),
             writes=[("upc", l, pr, j) for l in range(2) for pr in range(2) for j in range(NFF)])

        ones = cb[:, CB_ONES:CB_ONES + 128]
        one33 = cb[0:33, CB_ONE1:CB_ONE1 + 128]

        def Bm(off, g):
            return cb[:, off + g * 128: off + (g + 1) * 128]

        def emit_norm(gcol, col0, n, vcol=None, out_f32=None):
            sq, sqk, sqh = bigp.get()
            S.op("act", lambda: A.activation(out=sq[:, :, 0:n], in_=xT[:, :, col0:col0 + n], func=AF.Square),
                 reads=ck("x", col0, n), writes=[sqk])
            b = bank()
            S.mm([(ps[:, b, 0:n], ones, sq[:, k, 0:n], k == 0, k == 7) for k in range(8)],
                 reads=[sqk, "cb"], writes=[("ps", b)])
            bigp.put(sqh)
            rs, rsk, rsh = rsp.get()
            S.op("act", lambda: A.activation(out=rs[:, 0:n], in_=ps[:, b, 0:n], func=AF.Sqrt,
                                             bias=cst[:, C_EPS:C_EPS + 1], scale=1.0),
                 reads=[("ps", b), "cst"], writes=[rsk])
            S.op("dve", lambda: V.reciprocal(out=rs[:, 0:n], in_=rs[:, 0:n]), reads=[rsk], writes=[rsk])
            if vcol is not None:
                S.op("dve", lambda: V.tensor_tensor(out=rs[:, 0:n], in0=rs[:, 0:n],
                                                    in1=cst[:, C_VALID + vcol:C_VALID + vcol + n], op=ALU.mult),
                     reads=[rsk, "cst"], writes=[rsk])
            for k in range(8):
                if out_f32 is None:
                    o = hT[:, k, col0:col0 + n]
                    wk = ck("h", col0, n)
                else:
                    o = out_f32[:, k, 0:n]
                    wk = ["u"]
                if k < 5:
                    S.op("dve", lambda k=k, o=o: V.scalar_tensor_tensor(
                        out=o, in0=xT[:, k, col0:col0 + n], scalar=cst[:, gcol + k:gcol + k + 1],
                        in1=rs[:, 0:n], op0=ALU.mult, op1=ALU.mult),
                        reads=ck("x", col0, n) + [rsk, "cst"], writes=wk, sub=k)
                else:
                    tb, tbk, tbh = scr.get()
                    S.op("act", lambda k=k, tb=tb: A.activation(
                        out=tb[:, 0, 0:n], in_=xT[:, k, col0:col0 + n], func=AF.Identity,
                        scale=cst[:, gcol + k:gcol + k + 1]), reads=ck("x", col0, n) + ["cst"], writes=[tbk])
                    S.op("pool", lambda o=o, tb=tb: G.tensor_tensor(
                        out=o, in0=tb[:, 0, 0:n], in1=rs[:, 0:n], op=ALU.mult),
                        reads=[tbk, rsk], writes=wk, sub=k)
                    scr.put(tbh)
            rsp.put(rsh)

        zcount = [0]
        tcount = [0, 0]
        out_tickets = []
        bases = [-256, 512, 1280]
        phases = []
        ctl = {}
        pre_pa_norm = {}

        def add_phase(units, body):
            phases.append((units, body))

        def make_sl(s, l):
            base = bases[s]
            cl = l * LW
            if s == 0 and l == 1:
                tiles = [(128, 128), (256, 256), (512, 256)]
            else:
                tiles = [(0, 256), (256, 256), (512, 256)]
            if s == 0 and l == 0:
                next_tiles = [(128, 128), (256, 256), (512, 256)]
            else:
                next_tiles = tiles
            NT = len(tiles)
            tbase = tcount[l]
            tcount[l] += NT

            def vcol_of(col0):
                return col0 if (s == 0 and col0 < 256) else None

            first_chunk = tiles[0][0] // 128
            slots = {}
            st = {"ple_norm_done": set(), "pa_norm_done": set()}

            def pa_tile(U, col0, n, is_last):
                Wpool, wp, wa, Wga = U["Wpool"], U["wp"], U["wa"], U["Wga"]
                nch = n // 128
                for ci in range(nch):
                    c = col0 // 128 + ci
                    slot = zcount[0] % 3
                    zcount[0] += 1
                    slots[c] = slot
                    for cbk in range(2):
                        b = bank()
                        S.mm([(ps[:, b, :], hT[:, k, c * 128:(c + 1) * 128],
                               Wpool.ap[:, k, cbk * 512:(cbk + 1) * 512], k == 0, k == 7) for k in range(8)],
                             reads=[("h", c)] + Wpool.keys, writes=[("ps", b)])
                        S.op("act", lambda b=b, slot=slot, cbk=cbk: A.copy(
                            out=zp_tm[:, slot, cbk * 512:(cbk + 1) * 512], in_=ps[:, b, :]),
                            reads=[("ps", b)], writes=[("zp", slot)], sub=cbk)
                if is_last:
                    ctl["release"]("Wpool")
                gates = {}

                def gate(djp):
                    bg = bank()
                    S.mm([(ps[:, bg, q * n:(q + 1) * n], Wga.ap[:, k, (djp * 2 + q) * 128:(djp * 2 + q + 1) * 128],
                           hT[:, k, col0:col0 + n], k == 0, k == 7) for q in range(2) for k in range(8)],
                         reads=ck("h", col0, n) + Wga.keys, writes=[("ps", bg)])
                    sc, sck, sch = scr.get()
                    S.op("act", lambda: A.activation(out=sc[:, :, 0:n], in_=q2(ps[:, bg, 0:2 * n]), func=AF.Sigmoid),
                         reads=[("ps", bg)], writes=[sck])
                    gates[djp] = (sc, sck, sch)

                gate(0)
                gate(1)
                pl, plk, plh = bigp.get()
                for ci in range(nch):
                    c = col0 // 128 + ci
                    slot = slots[c]
                    if c == first_chunk:
                        prev_ap, prev_key = zpc[:, l, :], ("zpc", l)
                    else:
                        prev_ap, prev_key = zp_tm[:, slots[c - 1], :], ("zp", slots[c - 1])
                    boff = CB_BFIRST if (s == 0 and c == 2) else CB_BCUR
                    for jh in range(2):
                        b = bank()
                        grp = []
                        for jj in range(4):
                            j = jh * 4 + jj
                            g = j // 2
                            o = ps[:, b, jj * 128:(jj + 1) * 128]
                            grp.append((o, zp_tm[:, slot, j * 128:(j + 1) * 128], Bm(boff, g), True, False))
                            grp.append((o, prev_ap[:, j * 128:(j + 1) * 128], Bm(CB_BPREV, g), False, True))
                        S.mm(grp, reads=[("zp", slot), prev_key, "cb"], writes=[("ps", b)])
                        S.op("act", lambda b=b, jh=jh, ci=ci: A.copy(
                            out=pl[:, jh * 4:(jh + 1) * 4, ci * 128:(ci + 1) * 128],
                            in_=ps[:, b, :].rearrange("p (j t) -> p j t", j=4)),
                            reads=[("ps", b)], writes=[plk], sub=(ci, jh))
                gate(2)
                yp, ypk, yph = bigp.get()
                for djp in range(4):
                    b = bank()
                    grp = []
                    for q in range(2):
                        dj = djp * 2 + q
                        g = dj // 2
                        for cc in range(2):
                            grp.append((ps[:, b, q * n:(q + 1) * n],
                                        wp.ap[:, g, cc, (dj % 2) * 128:(dj % 2 + 1) * 128],
                                        pl[:, 2 * g + cc, 0:n], cc == 0, cc == 1))
                    S.mm(grp, reads=[plk] + wp.keys, writes=[("ps", b)])
                    for q in range(2):
                        dj = djp * 2 + q
                        S.op("act", lambda b=b, q=q, dj=dj: A.activation(
                            out=yp[:, dj, 0:n], in_=ps[:, b, q * n:(q + 1) * n], func=AF.Identity,
                            scale=cst[:, cl + C_PSC + dj:cl + C_PSC + dj + 1]),
                            reads=[("ps", b), "cst"], writes=[ypk], sub=dj)
                bigp.put(plh)
                gate(3)
                for djp in range(4):
                    sc, sck, sch = gates[djp]
                    by = bank()
                    S.mm([(ps[:, by, q * n:(q + 1) * n], wa.ap[:, k, (djp * 2 + q) * 128:(djp * 2 + q + 1) * 128],
                           yp[:, k, 0:n], k == 0, k == 7) for q in range(2) for k in range(8)],
                         reads=[ypk] + wa.keys, writes=[("ps", by)])
                    S.op("dve", lambda by=by, sc=sc, djp=djp: V.tensor_tensor(
                        out=mixa[:, 2 * djp:2 * djp + 2, col0:col0 + n], in0=sc[:, :, 0:n],
                        in1=q2(ps[:, by, 0:2 * n]), op=ALU.mult),
                        reads=[sck, ("ps", by)], writes=ck("ma", col0, n), sub=djp)
                    scr.put(sch)
                bigp.put(yph)

            def pa_body(U):
                if not pre_pa_norm.get((s, l)):
                    emit_norm(cl + C_MIXN, tiles[0][0], tiles[0][1], vcol_of(tiles[0][0]))
                for i, (col0, n) in enumerate(tiles):
                    if i + 1 < NT and tiles[i + 1] not in st["pa_norm_done"]:
                        emit_norm(cl + C_MIXN, tiles[i + 1][0], tiles[i + 1][1], vcol_of(tiles[i + 1][0]))
                    pa_tile(U, col0, n, i == NT - 1)
                last_c = (tiles[-1][0] + tiles[-1][1]) // 128 - 1
                S.op("act", lambda: A.copy(out=zpc[:, l, :], in_=zp_tm[:, slots[last_c], :]),
                     reads=[("zp", slots[last_c])], writes=[("zpc", l)])

            add_phase([("Wpool", kp(w_in, l, 0, 1024), [8, 1024]),
                       ("Wga", kp(w_in, l, 3072, 4096), [8, 1024]),
                       ("wp", w_pool[l].rearrange("g (c p) d -> p g c d", p=128), [4, 2, 256]),
                       ("wa", kp(w_a, l, 0, 1024), [8, 1024])], pa_body)

            def pb_tile(U, col0, n, is_last):
                Wu, Wv, wb, Wgb = U["Wu"], U["Wv"], U["wb"], U["Wgb"]
                nch = n // 128
                vslots = []
                vgs = []
                for ci in range(nch):
                    c = col0 // 128 + ci
                    slot = zcount[0] % 3
                    zcount[0] += 1
                    vslots.append(slot)
                    vg, vgk, vgh = vgp.get()
                    ss, ssk, ssh = ssp.get()
                    vgs.append((vg, vgk, vgh, ss, ssk, ssh, slot))
                    for cbk in range(2):
                        b = bank()
                        S.mm([(ps[:, b, :], hT[:, k, c * 128:(c + 1) * 128],
                               Wv.ap[:, k, cbk * 512:(cbk + 1) * 512], k == 0, k == 7) for k in range(8)],
                             reads=[("h", c)] + Wv.keys, writes=[("ps", b)])
                        S.op("act", lambda: A.activation(
                            out=vg[:, cbk * 512:(cbk + 1) * 512], in_=ps[:, b, :], func=AF.Gelu),
                            reads=[("ps", b)], writes=[vgk], sub=cbk)
                for (vg, vgk, vgh, ss, ssk, ssh, slot) in vgs:
                    S.op("dve", lambda: V.memset(ss[:], 0.0), writes=[ssk])
                    S.op("dve", lambda: V.scalar_tensor_tensor(
                        out=zp_tm[:, slot, :], in0=vg[:], scalar=1.0, in1=vg[:], op0=ALU.mult, op1=ALU.mult,
                        accum_out=ss[:, 0:1]), reads=[vgk, ssk], writes=[("zp", slot), ssk])
                for (vg, vgk, vgh, ss, ssk, ssh, slot) in vgs:
                    S.op("act", lambda: A.activation(
                        out=ss[:, 1:2], in_=ss[:, 0:1], func=AF.Sqrt, bias=cst[:, C_EPS:C_EPS + 1],
                        scale=1.0 / 1024.0), reads=[ssk, "cst"], writes=[ssk])
                for (vg, vgk, vgh, ss, ssk, ssh, slot) in vgs:
                    S.op("dve", lambda: V.reciprocal(out=ss[:, 1:2], in_=ss[:, 1:2]), reads=[ssk], writes=[ssk])
                for (vg, vgk, vgh, ss, ssk, ssh, slot) in vgs:
                    S.op("act", lambda: A.activation(out=vg[:], in_=vg[:], func=AF.Identity, scale=ss[:, 1:2]),
                         reads=[vgk, ssk], writes=[vgk])
                for (vg, vgk, vgh, ss, ssk, ssh, slot) in vgs:
                    S.op("pool", lambda: G.tensor_tensor(
                        out=zp_tm[:, slot, :], in0=vg[:], in1=gsgu[:, l, :], op=ALU.mult),
                        reads=[vgk, "gsgu"], writes=[("zp", slot)])
                    vgp.put(vgh)
                    ssp.put(ssh)
                if is_last:
                    ctl["release"]("Wv")
                for fp in range(4):
                    b = bank()
                    S.mm([(ps[:, b, q * n:(q + 1) * n], Wu.ap[:, k, (fp * 2 + q) * 128:(fp * 2 + q + 1) * 128],
                           hT[:, k, col0:col0 + n], k == 0, k == 7) for q in range(2) for k in range(8)],
                         reads=ck("h", col0, n) + Wu.keys, writes=[("ps", b)])
                    S.op("act", lambda b=b, fp=fp: A.activation(
                        out=ubuf[:, 2 * fp:2 * fp + 2, 0:n], in_=q2(ps[:, b, 0:2 * n]), func=AF.Gelu),
                        reads=[("ps", b)], writes=["u"], sub=fp)
                if is_last:
                    ctl["release"]("Wu")
                gt, gtk, gth = bigp.get()
                for ci in range(nch):
                    slot = vslots[ci]
                    for hh in range(2):
                        b = bank()
                        grp = []
                        for jj in range(4):
                            h = hh * 4 + jj
                            o = ps[:, b, jj * 128:(jj + 1) * 128]
                            grp.append((o, zp_tm[:, slot, h * 128:(h + 1) * 128], wsb[:, l, h, :], True, False))
                            grp.append((o, one33, brow[0:33, l, h * 128:(h + 1) * 128], False, True))
                        S.mm(grp, reads=[("zp", slot), "wsb", "brow", "cb"], writes=[("ps", b)])
                        S.op("dve", lambda b=b, hh=hh, ci=ci: V.tensor_tensor(
                            out=gt[:, hh * 4:(hh + 1) * 4, ci * 128:(ci + 1) * 128],
                            in0=ps[:, b, :].rearrange("p (j t) -> p j t", j=4),
                            in1=ubuf[:, hh * 4:(hh + 1) * 4, ci * 128:(ci + 1) * 128], op=ALU.mult),
                            reads=[("ps", b), "u"], writes=[gtk], sub=(ci, hh))
                gsc = {}
                for djp in range(4):
                    bg = bank()
                    S.mm([(ps[:, bg, q * n:(q + 1) * n], Wgb.ap[:, k, (djp * 2 + q) * 128:(djp * 2 + q + 1) * 128],
                           hT[:, k, col0:col0 + n], k == 0, k == 7) for q in range(2) for k in range(8)],
                         reads=ck("h", col0, n) + Wgb.keys, writes=[("ps", bg)])
                    sc, sck, sch = scr.get()
                    S.op("act", lambda bg=bg, sc=sc: A.activation(out=sc[:, :, 0:n], in_=q2(ps[:, bg, 0:2 * n]),
                                                                  func=AF.Sigmoid),
                         reads=[("ps", bg)], writes=[sck])
                    gsc[djp] = (sc, sck, sch)
                for djp in range(4):
                    sc, sck, sch = gsc[djp]
                    by = bank()
                    S.mm([(ps[:, by, q * n:(q + 1) * n], wb.ap[:, k, (djp * 2 + q) * 128:(djp * 2 + q + 1) * 128],
                           gt[:, k, 0:n], k == 0, k == 7) for q in range(2) for k in range(8)],
                         reads=[gtk] + wb.keys, writes=[("ps", by)])
                    S.op("dve", lambda by=by, sc=sc: V.tensor_tensor(
                        out=sc[:, :, 0:n], in0=sc[:, :, 0:n], in1=q2(ps[:, by, 0:2 * n]), op=ALU.mult),
                        reads=[sck, ("ps", by)], writes=[sck])
                    S.op("pool", lambda sc=sc, djp=djp: G.tensor_tensor(
                        out=mixa[:, 2 * djp:2 * djp + 2, col0:col0 + n], in0=sc[:, :, 0:n],
                        in1=mixa[:, 2 * djp:2 * djp + 2, col0:col0 + n], op=ALU.add),
                        reads=[sck] + ck("ma", col0, n), writes=ck("ma", col0, n), sub=djp)
                    scr.put(sch)
                bigp.put(gth)

            def pb_body(U):
                for i, (col0, n) in enumerate(tiles):
                    pb_tile(U, col0, n, i == NT - 1)

            add_phase([("Wv", kp(w_in, l, 2048, 3072), [8, 1024]),
                       ("Wu", kp(w_in, l, 1024, 2048), [8, 1024]),
                       ("Wgb", kp(w_in, l, 4096, 5120), [8, 1024]),
                       ("wb", kp(w_b, l, 0, 1024), [8, 1024])], pb_body)

            def pc_body(U):
                wo = U["wo"]
                for (col0, n) in tiles:
                    S.dma("pool", pTs[:, :, col0:col0 + n],
                          pin[l, :, :, base + 256 + col0: base + 256 + col0 + n],
                          writes=ck("pT", col0, n))

                def pc(col0, n):
                    for djp in range(4):
                        b = bank()
                        S.mm([(ps[:, b, q * n:(q + 1) * n], wo.ap[:, k, (djp * 2 + q) * 128:(djp * 2 + q + 1) * 128],
                               mixa[:, k, col0:col0 + n], k == 0, k == 7) for q in range(2) for k in range(8)],
                             reads=ck("ma", col0, n) + wo.keys, writes=[("ps", b)])
                        S.op("dve", lambda b=b, djp=djp: V.tensor_tensor(
                            out=xT[:, 2 * djp:2 * djp + 2, col0:col0 + n],
                            in0=xT[:, 2 * djp:2 * djp + 2, col0:col0 + n],
                            in1=q2(ps[:, b, 0:2 * n]), op=ALU.add),
                            reads=[("ps", b)] + ck("x", col0, n), writes=ck("x", col0, n), sub=djp)

                def nrm(col0, n):
                    emit_norm(cl + C_FFNN, col0, n, vcol_of(col0))

                pc(*tiles[0])
                for i in range(NT):
                    if i + 1 < NT:
                        pc(*tiles[i + 1])
                    nrm(*tiles[i])

            add_phase([("wo", kp(w_out, l, 0, 1024), [8, 1024])], pc_body)

            def make_ffn(bi, j0, cbn):
                last_block = (bi == len(FFN_BLOCKS) - 1)

                def up(U, col0, n, tpar, res):
                    upA, upB = U["upA"], U["upB"]
                    at, atk, ath = atp.get()
                    accs = {}

                    def stA(jl):
                        j = j0 + jl
                        b = bank()
                        grp = [(ps[:, b, 0:n], upA.ap[:, k, jl * 128:(jl + 1) * 128], hT[:, k, col0:col0 + n],
                                k == 0, k == 7) for k in range(8)]
                        grp += [(ps[:, b, n:2 * n], upB.ap[:, k, jl * 128:(jl + 1) * 128], hT[:, k, col0:col0 + n],
                                 k == 0, k == 7) for k in range(8)]
                        S.mm(grp, reads=ck("h", col0, n) + upA.keys + upB.keys, writes=[("ps", b)])
                        acc, acck, acch = scr.get()
                        accs[jl] = (acc, acck, acch, b)
                        if "memset" not in SKIP:
                            S.op("dve", lambda: V.memset(acc[:, :, n:n + 2], 0.0), writes=[acck, (acck, 0), (acck, 1)])
                        for q, f in ((0, j), (1, NFF + j)):
                            w2 = cst[:, cl + C_CW + 88 + f:cl + C_CW + 88 + f + 1]
                            bb = cst[:, cl + C_CB + f:cl + C_CB + f + 1]
                            S.op("act", lambda: A.activation(
                                out=acc[:, q, 0:n], in_=ps[:, b, q * n:q * n + n], func=AF.Identity, scale=w2, bias=bb),
                                reads=[("ps", b), "cst"], writes=[(acck, q)])

                    def stB(jl):
                        j = j0 + jl
                        acc, acck, acch, b = accs[jl]
                        for q, f in ((0, j), (1, NFF + j)):
                            w0 = cst[:, cl + C_CW + f:cl + C_CW + f + 1]
                            w1 = cst[:, cl + C_CW + 44 + f:cl + C_CW + 44 + f + 1]
                            S.op("dve", lambda: V.scalar_tensor_tensor(
                                out=acc[:, q, 1:n + 1], in0=ps[:, b, q * n:q * n + n], scalar=w1,
                                in1=acc[:, q, 1:n + 1], op0=ALU.mult, op1=ALU.add),
                                reads=[("ps", b), (acck, q), "cst"], writes=[(acck, q)])
                            S.op("dve", lambda: V.scalar_tensor_tensor(
                                out=acc[:, q, 2:n + 2], in0=ps[:, b, q * n:q * n + n], scalar=w0,
                                in1=acc[:, q, 2:n + 2], op0=ALU.mult, op1=ALU.add),
                                reads=[("ps", b), (acck, q), "cst"], writes=[(acck, q)])
                        if "add" not in SKIP:
                          S.op("pool", lambda: G.tensor_tensor(
                            out=acc[:, :, 0:2], in0=acc[:, :, 0:2],
                            in1=upc[:, l, tpar, j, :].rearrange("p (q t) -> p q t", q=2), op=ALU.add),
                            reads=[("upc", l, tpar, j), (acck, 0), (acck, 1)], writes=[(acck, 0), (acck, 1)],
                            sub="head")
                        if "copy" in SKIP:
                            pass
                        elif os.environ.get("KV_ACTCOPY"):
                            S.op("act", lambda: A.copy(
                                out=upc[:, l, 1 - tpar, j, :].rearrange("p (q t) -> p q t", q=2), in_=acc[:, :, n:n + 2]),
                                reads=[(acck, 0), (acck, 1)], writes=[("upc", l, 1 - tpar, j)], sub="tail")
                        else:
                            S.op("pool", lambda: G.tensor_copy(
                                out=upc[:, l, 1 - tpar, j, :].rearrange("p (q t) -> p q t", q=2), in_=acc[:, :, n:n + 2]),
                                reads=[(acck, 0), (acck, 1)], writes=[("upc", l, 1 - tpar, j)], sub="tail")

                    def stC(jl):
                        acc, acck, acch, b = accs[jl]
                        S.op("act", lambda: A.activation(out=acc[:, 0, 0:n], in_=acc[:, 0, 0:n], func=AF.Gelu),
                             reads=[(acck, 0)], writes=[(acck, 0)])
                        S.op("pool", lambda: G.tensor_tensor(
                            out=at[:, jl, 0:n], in0=acc[:, 0, 0:n], in1=acc[:, 1, 0:n], op=ALU.mult),
                            reads=[acck, (acck, 0), (acck, 1)], writes=[atk])
                        scr.put(acch)

                    for step in range(cbn + 2):
                        if step < cbn:
                            stA(step)
                        if 0 <= step - 1 < cbn:
                            stB(step - 1)
                        if 0 <= step - 2 < cbn:
                            stC(step - 2)
                        yield None
                    res.append((at, atk, ath))

                def down_parts(U, col0, n, atr):
                    dwn = U["dwn"]
                    at, atk, ath = atr
                    parts = []

                    def mk(djp):
                        def part():
                            b = bank()
                            S.mm([(ps[:, b, q * n:(q + 1) * n],
                                   dwn.ap[:, jl, (djp * 2 + q) * 128:(djp * 2 + q + 1) * 128],
                                   at[:, jl, 0:n], jl == 0, jl == cbn - 1) for q in range(2) for jl in range(cbn)],
                                 reads=[atk] + dwn.keys, writes=[("ps", b)])
                            S.op("dve", lambda: V.tensor_tensor(
                                out=xT[:, 2 * djp:2 * djp + 2, col0:col0 + n],
                                in0=xT[:, 2 * djp:2 * djp + 2, col0:col0 + n],
                                in1=q2(ps[:, b, 0:2 * n]), op=ALU.add),
                                reads=[("ps", b)] + ck("x", col0, n), writes=ck("x", col0, n), sub=djp)
                            if djp == 3:
                                atp.put(ath)
                        return part
                    for djp in range(4):
                        parts.append(mk(djp))
                    return parts

                def body(U):
                    ats = {}
                    r = []
                    for _ in up(U, tiles[0][0], tiles[0][1], (tbase + 0) % 2, r):
                        pass
                    ats[0] = r[0]
                    for i in range(NT):
                        parts = down_parts(U, tiles[i][0], tiles[i][1], ats[i])
                        k = 0
                        if i + 1 < NT:
                            r = []
                            for si, _ in enumerate(up(U, tiles[i + 1][0], tiles[i + 1][1], (tbase + i + 1) % 2, r)):
                                if last_block and i == 1 and si == 2:
                                    emit_norm(cl + C_PLEN, tiles[0][0], tiles[0][1], vcol_of(tiles[0][0]))
                                    st["ple_norm_done"].add(tiles[0])
                                if si % 2 == 1 and k < 4:
                                    parts[k]()
                                    k += 1
                            ats[i + 1] = r[0]
                            if i + 1 == NT - 1:
                                ctl["release"]("upA")
                                ctl["release"]("upB")
                        while k < 4:
                            parts[k]()
                            k += 1

                add_phase([("upA", kp(w_up, l, j0 * 128, (j0 + cbn) * 128), [8, cbn * 128]),
                           ("upB", kp(w_up, l, DFF + j0 * 128, DFF + (j0 + cbn) * 128), [8, cbn * 128]),
                           ("dwn", w_down[l, j0 * 128:(j0 + cbn) * 128, :].rearrange("(j p) c -> p j c", p=128),
                            [cbn, 1024])], body)

            for bi, (j0, cbn) in enumerate(FFN_BLOCKS):
                make_ffn(bi, j0, cbn)

            def ple_body(U):
                wpg, wpl = U["wpg"], U["wpl"]
                for ti, (col0, n) in enumerate(tiles):
                    for tj in (ti, ti + 1):
                        if tj < NT and tiles[tj] not in st["ple_norm_done"]:
                            emit_norm(cl + C_PLEN, tiles[tj][0], tiles[tj][1], vcol_of(tiles[tj][0]))
                            st["ple_norm_done"].add(tiles[tj])
                    for djp in range(4):
                        bg = bank()
                        S.mm([(ps[:, bg, q * n:(q + 1) * n], wpg.ap[:, k, (djp * 2 + q) * 128:(djp * 2 + q + 1) * 128],
                               hT[:, k, col0:col0 + n], k == 0, k == 7) for q in range(2) for k in range(8)],
                             reads=ck("h", col0, n) + wpg.keys, writes=[("ps", bg)])
                        sc, sck, sch = scr.get()
                        S.op("act", lambda: A.activation(out=sc[:, :, 0:n], in_=q2(ps[:, bg, 0:2 * n]),
                                                         func=AF.Sigmoid),
                             reads=[("ps", bg)], writes=[sck])
                        be = bank()
                        S.mm([(ps[:, be, q * n:(q + 1) * n], wpl.ap[:, kk, (djp * 2 + q) * 128:(djp * 2 + q + 1) * 128],
                               pTs[:, kk, col0:col0 + n], kk == 0, kk == 1) for q in range(2) for kk in range(2)],
                             reads=ck("pT", col0, n) + wpl.keys, writes=[("ps", be)])
                        S.op("dve", lambda: V.tensor_tensor(
                            out=sc[:, :, 0:n], in0=sc[:, :, 0:n], in1=q2(ps[:, be, 0:2 * n]), op=ALU.mult),
                            reads=[sck, ("ps", be)], writes=[sck])
                        S.op("pool", lambda: G.tensor_tensor(
                            out=xT[:, 2 * djp:2 * djp + 2, col0:col0 + n],
                            in0=xT[:, 2 * djp:2 * djp + 2, col0:col0 + n], in1=sc[:, :, 0:n], op=ALU.add),
                            reads=[sck] + ck("x", col0, n), writes=ck("x", col0, n), sub=djp)
                        scr.put(sch)
                    if l == 1:
                        if base + col0 >= 0:
                            emit_norm(C_FIN, col0, n, None, out_f32=ubuf)
                            oc = base + col0
                            out_tickets.append(S.dma("sp", out[:, :, oc:oc + n], ubuf[:, :, 0:n], reads=["u"]))
                        if s + 1 < 3:
                            c0 = (col0 // 256) * 256
                            nb = bases[s + 1]
                            S.dma("sp", xT[:, :, c0:c0 + 256], xin[:, :, nb + 256 + c0: nb + 256 + c0 + 256],
                                  writes=ck("x", c0, 256))
                    if ti == 1:
                        if l == 0:
                            ns, nl = s, 1
                        elif s + 1 < 3:
                            ns, nl = s + 1, 0
                        else:
                            ns = None
                        if ns is not None:
                            nt0 = (128, 128) if (ns == 0 and nl == 1) else (0, 256)
                            nv = nt0[0] if (ns == 0 and nt0[0] < 256) else None
                            emit_norm(nl * LW + C_MIXN, nt0[0], nt0[1], nv)
                            pre_pa_norm[(ns, nl)] = True

            add_phase([("wpg", kp(w_pg, l, 0, 1024), [8, 1024]),
                       ("wpl", kp(w_ple, l, 0, 1024), [2, 1024])], ple_body)

        def x_load_phase(s):
            base = bases[s]

            def body(U):
                return
            add_phase([], body)

        for s in range(3):
            x_load_phase(s)
            for l in range(2):
                make_sl(s, l)

        ring = {"ptr": 0}
        live = {}
        loaded = {}
        regions = {}

        def try_load(pi, j):
            name, src, dims = phases[pi][0][j]
            size = 1
            for d_ in dims:
                size *= d_
            assert size % 1024 == 0
            npg = size // 1024
            if os.environ.get("KV_FIFO"):
                p0 = ring["ptr"]
                if p0 + npg > ring_pages:
                    p0 = 0
                for regs in live.values():
                    for (a, b_) in regs:
                        if not (p0 + npg <= a or p0 >= b_):
                            return False
                ring["ptr"] = p0 + npg
            else:
                occ = [r for regs in live.values() for r in regs]
                if npg >= 6:
                    cands = list(range(0, ring_pages - npg + 1, npg))
                else:
                    cands = list(range(ring_pages - npg, -1, -1))
                p0 = None
                for c in cands:
                    if all(c + npg <= a or c >= b_ for (a, b_) in occ):
                        p0 = c
                        break
                if p0 is None:
                    return False
            live.setdefault(pi, []).append((p0, p0 + npg))
            regions[(pi, j)] = (p0, p0 + npg)
            if os.environ.get("KV_DBG"):
                print("LOAD phase", pi, name, "pages", p0, p0 + npg, "at_phase", cur_phase[0])
            flat = wr[:, p0 * 1024:p0 * 1024 + size]
            if len(dims) == 2:
                ap = flat.rearrange("p (a b) -> p a b", a=dims[0])
            else:
                ap = flat.rearrange("p (a b c) -> p a b c", a=dims[0], b=dims[1])
            keys = [("wr", pg) for pg in range(p0, p0 + npg)]
            S.dma("pool", ap, src, reads=(), writes=keys)
            loaded[(pi, j)] = Unit(ap, keys)
            return True

        cur_phase = [0]

        def prefetch_from(pi):
            for pj in range(pi + 1, min(pi + 5, len(phases))):
                for j in range(len(phases[pj][0])):
                    if (pj, j) not in loaded:
                        if not try_load(pj, j):
                            return

        def release_unit(name):
            pi = cur_phase[0]
            units = phases[pi][0]
            j = [u[0] for u in units].index(name)
            live[pi].remove(regions[(pi, j)])
            prefetch_from(pi)

        ctl["release"] = release_unit
        for pi in range(len(phases)):
            cur_phase[0] = pi
            units, body = phases[pi]
            for j in range(len(units)):
                if (pi, j) not in loaded:
                    ok = try_load(pi, j)
                    assert ok, "weight ring too small for phase %d" % pi
            prefetch_from(pi)
            body({units[j][0]: loaded[(pi, j)] for j in range(len(units))})
            live.pop(pi, None)

        for t in out_tickets:
            S._wait("sp", [t])
    return nc


_CACHE = {}


def _pool_mats(first):
    Bcur = np.zeros((4, 128, 128), np.float32)
    Bprev = np.zeros((4, 128, 128), np.float32)
    for g, w in enumerate((2, 4, 8, 16)):
        for t in range(128):
            cnt = min(t + 1, w) if first else w
            for sg in range(t - w + 1, t + 1):
                if sg >= 0:
                    Bcur[g, sg, t] += 1.0 / cnt
                elif not first:
                    Bprev[g, sg + 128, t] += 1.0 / cnt
            Bcur[g, t, t] -= 1.0
    return Bcur, Bprev


def kernel(x, p, mix_norm, w_in, w_pool, pool_scale, sgu_norm, w_spatial, b_spatial,
           w_branch_a, w_branch_b, w_out, ffn_norm, w_up, conv_w, conv_b, w_down,
           ple_norm, w_ple_gate, w_ple, final_norm):
    f = np.float32
    x = np.asarray(x, f)
    p = np.asarray(p, f)
    if "nc" not in _CACHE:
        _CACHE["nc"] = build_program()
    nc = _CACHE["nc"]

    def vec8(v):
        return np.asarray(v, f).reshape(8, 128).T

    Bcur, Bprev = _pool_mats(False)
    Bfirst, _ = _pool_mats(True)
    mask = (np.arange(128)[None, :] >= np.arange(128)[:, None]).astype(f)
    gsgu = np.ascontiguousarray(np.broadcast_to(np.asarray(sgu_norm, f)[None, :, :], (128, 2, 1024)))
    bTt = np.ascontiguousarray(np.broadcast_to(np.asarray(b_spatial, f).reshape(1, 2, 1024), (128, 2, 1024)))
    wsT = np.ascontiguousarray(np.transpose(np.asarray(w_spatial, f), (3, 0, 1, 2)).reshape(128, 2, 1024))
    shared = {
        "w_in": np.asarray(w_in, f), "w_pool": np.asarray(w_pool, f),
        "w_branch_a": np.asarray(w_branch_a, f), "w_branch_b": np.asarray(w_branch_b, f),
        "w_out": np.asarray(w_out, f), "w_up": np.asarray(w_up, f), "w_down": np.asarray(w_down, f),
        "w_ple_gate": np.asarray(w_ple_gate, f), "w_ple": np.asarray(w_ple, f),
        "gsgu": gsgu, "bT": bTt, "wsT": wsT, "mask": mask,
    }
    cst0 = np.zeros((128, NCST), f)
    for l in range(2):
        o = l * LW
        cst0[:, o + C_MIXN:o + C_MIXN + 8] = vec8(mix_norm[l])
        cst0[:, o + C_FFNN:o + C_FFNN + 8] = vec8(ffn_norm[l])
        cst0[:, o + C_PLEN:o + C_PLEN + 8] = vec8(ple_norm[l])
        cst0[:, o + C_PSC:o + C_PSC + 8] = vec8(pool_scale[l])
        for k in range(3):
            cst0[:, o + C_CW + k * 44:o + C_CW + (k + 1) * 44] = np.asarray(conv_w[l][k], f).reshape(44, 128).T
        cst0[:, o + C_CB:o + C_CB + 44] = np.asarray(conv_b[l], f).reshape(44, 128).T
    cst0[:, C_FIN:C_FIN + 8] = vec8(final_norm)
    cst0[:, C_EPS:C_EPS + 8] = EPS

    in_maps = []
    for core in range(8):
        b = core // 4
        t0 = (core % 4) * TOK
        xs = np.zeros((TIN, D), f)
        ps_ = np.zeros((2, TIN, 256), f)
        lo = t0 - HALO
        if lo >= 0:
            xs[:] = x[b, lo:t0 + TOK]
            ps_[:] = p[:, b, lo:t0 + TOK]
        else:
            xs[HALO:] = x[b, 0:TOK]
            ps_[:, HALO:] = p[:, b, 0:TOK]
        xTc = np.ascontiguousarray(xs.T.reshape(8, 128, TIN).transpose(1, 0, 2))
        pTc = np.ascontiguousarray(ps_.transpose(0, 2, 1).reshape(2, 2, 128, TIN).transpose(0, 2, 1, 3))
        cstc = cst0.copy()
        cstc[:, C_VALID:C_VALID + 256] = 1.0 if lo >= 0 else 0.0
        cbc = np.zeros((128, NCSTB), f)
        cbc[:, CB_ONES:CB_ONES + 128] = 1.0 / 1024.0
        cbc[:, CB_ONE1:CB_ONE1 + 128] = 1.0
        for g in range(4):
            cbc[:, CB_BCUR + g * 128:CB_BCUR + (g + 1) * 128] = Bcur[g]
            cbc[:, CB_BPREV + g * 128:CB_BPREV + (g + 1) * 128] = Bprev[g]
            cbc[:, CB_BFIRST + g * 128:CB_BFIRST + (g + 1) * 128] = (Bfirst[g] if lo < 0 else Bcur[g])
        m = dict(shared)
        m.update({"xT": xTc, "pT": pTc, "cst": cstc, "cstb": cbc})
        in_maps.append(m)

    res = run_bass_kernel_spmd(nc, in_maps, core_ids=list(range(8)))
    outp = np.zeros((2, SEQ, D), f)
    for core in range(8):
        b = core // 4
        t0 = (core % 4) * TOK
        oT = np.asarray(res.results[core]["outT"], f)
        outp[b, t0:t0 + TOK] = oT.transpose(2, 1, 0).reshape(TOK, D)
    return outp
```

```python
import os
import numpy as np
from contextlib import ExitStack
import concourse.bass as bass
import concourse.mybir as mybir
from concourse.bass_utils import run_bass_kernel_spmd

F32 = mybir.dt.float32
BF16 = mybir.dt.bfloat16
AF = mybir.ActivationFunctionType
ALU = mybir.AluOpType

D = 1024
DFF = 2816
NFF = 22
SEQ = 8192
TOK = 2048
HALO = 256
TIN = TOK + HALO
TS = 768
EPS = 1e-6
LW = 208
C_MIXN, C_FFNN, C_PLEN, C_PSC, C_CW, C_CB = 0, 8, 16, 24, 32, 164
C_FIN = 2 * LW
C_VALID = C_FIN + 8
C_EPS = C_VALID + 256
NCST = C_EPS + 8
CB_ONES, CB_BCUR, CB_BPREV, CB_BFIRST = 0, 128, 640, 1152
CB_ONE1 = 1664
NCSTB = 1792
ND = 8
SKIP = os.environ.get('KV_SKIP', '')
FFN_BLOCKS = [(0, 6), (6, 6), (12, 6), (18, 4)]


class Sched:
    def __init__(self, nc, es):
        self.nc = nc
        self.engs = {"pe": nc.tensor, "act": nc.scalar, "dve": nc.vector,
                     "pool": nc.gpsimd, "sp": nc.sync}
        self.sem = {}
        self.cnt = {}
        for e in self.engs:
            self.sem[e] = es.enter_context(nc.semaphore("s_" + e))
            self.cnt[e] = 0
        self.dsem = {}
        self.dcnt = {}
        self.drr = {}
        for q in ("pool", "sp"):
            self.dsem[q] = [es.enter_context(nc.semaphore("d_%s%d" % (q, i))) for i in range(ND)]
            self.dcnt[q] = [0] * ND
            self.drr[q] = 0
        self.seen = {e: {} for e in self.engs}
        self.lastw = {}
        self.readers = {}
        self.lastw_sub = {}
        self.readers_sub = {}
        self.nops = 0

    def _same(self, e, t, sub, tsub):
        if t[3] != self.cnt[e]:
            return False
        if sub is not None and tsub is not None and sub != tsub:
            return False
        return True

    def _deps(self, e, reads, writes, sub=None):
        deps = []
        for k in reads:
            w = self.lastw.get(k)
            if w is not None:
                if w[0] != e or w[0] == "dma":
                    deps.append(w)
                elif self._same(e, w, sub, self.lastw_sub.get(k)):
                    deps.append(w)
            if isinstance(k, tuple) and k[0] == "ps":
                for r in self.readers.get(k, {}).values():
                    if r[0] != e:
                        deps.append(r)
        for k in writes:
            w = self.lastw.get(k)
            if w is not None:
                if w[0] != e or w[0] == "dma":
                    deps.append(w)
                elif self._same(e, w, sub, self.lastw_sub.get(k)):
                    deps.append(w)
            for sk, r in self.readers.get(k, {}).items():
                if r[0] != e or r[0] == "dma":
                    deps.append(r)
                elif self._same(e, r, sub, self.readers_sub.get(k, {}).get(sk)):
                    deps.append(r)
        return deps

    def _wait(self, e, deps):
        best = {}
        for d in deps:
            if d[2] not in best or best[d[2]][3] < d[3]:
                best[d[2]] = d
        for d in best.values():
            if self.seen[e].get(d[2], 0) >= d[3]:
                continue
            self.engs[e].wait_ge(d[1], d[3])
            self.seen[e][d[2]] = d[3]

    def _reg(self, t, reads, writes, sub=None):
        for k in reads:
            self.readers.setdefault(k, {})[t[2]] = t
            self.readers_sub.setdefault(k, {})[t[2]] = sub
        for k in writes:
            self.lastw[k] = t
            self.lastw_sub[k] = sub
            self.readers[k] = {}
            self.readers_sub[k] = {}

    def op(self, e, fn, reads=(), writes=(), sub=None):
        self._wait(e, self._deps(e, reads, writes, sub))
        ins = fn()
        self.cnt[e] += 1
        ins.then_inc(self.sem[e], 1)
        t = (e, self.sem[e], "s_" + e, self.cnt[e])
        self._reg(t, reads, writes, sub)
        self.nops += 1
        return t

    def mm(self, grp, reads=(), writes=()):
        e = "pe"
        self._wait(e, self._deps(e, reads, writes))
        ins = None
        for (o, l, r, st, sp) in grp:
            ins = self.nc.tensor.matmul(o, l, r, start=st, stop=sp)
            self.nops += 1
        self.cnt[e] += 1
        ins.then_inc(self.sem[e], 1)
        t = (e, self.sem[e], "s_" + e, self.cnt[e])
        self._reg(t, reads, writes)
        return t

    def dma(self, q, out, in_, reads=(), writes=()):
        i = self.drr[q]
        self.drr[q] = (i + 1) % ND
        sem = self.dsem[q][i]
        key = "d_%s%d" % (q, i)
        deps = self._deps("dma", reads, writes)
        if self.dcnt[q][i] > 0:
            deps.append(("dma", sem, key, self.dcnt[q][i]))
        self._wait(q, deps)
        ins = self.engs[q].dma_start(out=out, in_=in_)
        self.dcnt[q][i] += 16
        ins.then_inc(sem, 16)
        t = ("dma", sem, key, self.dcnt[q][i])
        self._reg(t, reads, writes)
        self.nops += 1
        return t


class BufPool:
    def __init__(self, bufs, name):
        self.bufs = bufs
        self.name = name
        self.free = list(range(len(bufs)))

    def get(self):
        assert self.free, "pool %s exhausted" % self.name
        j = self.free.pop(0)
        return self.bufs[j], (self.name, j), j

    def put(self, j):
        assert j not in self.free
        self.free.append(j)


class Unit:
    def __init__(self, ap, keys):
        self.ap = ap
        self.keys = keys


def build_program():
    nc = bass.Bass("TRN2", target_bir_lowering=False)

    def dram(name, shape, kind="ExternalInput"):
        return nc.dram_tensor(name, shape, F32, kind=kind).ap()

    xin = dram("xT", [128, 8, TIN])
    pin = dram("pT", [2, 128, 2, TIN])
    w_in = dram("w_in", [2, 1024, 5120])
    w_pool = dram("w_pool", [2, 4, 256, 256])
    w_a = dram("w_branch_a", [2, 1024, 1024])
    w_b = dram("w_branch_b", [2, 1024, 1024])
    w_out = dram("w_out", [2, 1024, 1024])
    w_up = dram("w_up", [2, 1024, 2 * DFF])
    w_down = dram("w_down", [2, DFF, 1024])
    w_pg = dram("w_ple_gate", [2, 1024, 1024])
    w_ple = dram("w_ple", [2, 256, 1024])
    cst_in = dram("cst", [128, NCST])
    cstb_in = dram("cstb", [128, NCSTB])
    gsgu_in = dram("gsgu", [128, 2, 1024])
    bT_in = dram("bT", [128, 2, 1024])
    wsT_in = dram("wsT", [128, 2, 1024])
    mask_in = dram("mask", [128, 128])
    out = dram("outT", [128, 8, TOK], kind="ExternalOutput")

    with ExitStack() as es:
        S = Sched(nc, es)
        V = nc.vector
        G = nc.gpsimd
        A = nc.scalar

        def sb(name, shape, dt):
            return es.enter_context(nc.sbuf_tensor(name, shape, dt))

        xT = sb("xT_sb", [128, 8, TS], F32)
        hT = sb("hT_sb", [128, 8, TS], BF16)
        mixa = sb("mixa_sb", [128, 8, TS], BF16)
        cst = sb("cst_sb", [128, NCST], F32)
        cb = sb("cb_sb", [128, NCSTB], BF16)
        mask = sb("mask_sb", [128, 128], F32)
        gsgu = sb("gsgu_sb", [128, 2, 1024], F32)
        brow = sb("brow_sb", [128, 2, 1024], BF16)
        wsb = sb("wsb_sb", [128, 2, 8, 128], BF16)
        zpc = sb("zpc_sb", [128, 2, 1024], BF16)
        upc = sb("upc_sb", [128, 2, 2, NFF, 4], F32)
        bigp = BufPool([sb("pa%d" % i, [128, 8, 256], BF16) for i in range(3)], "pa")
        rsp = BufPool([sb("rs%d" % i, [128, 256], F32) for i in range(2)], "rs")
        zp_tm = sb("zp_tm", [128, 3, 1024], BF16)
        vgp = BufPool([sb("vg%d" % i, [128, 1024], F32) for i in range(2)], "vg")
        scr = BufPool([sb("sc%d" % i, [128, 2, 264], F32) for i in range(6)], "sc")
        ubuf = sb("u_sb", [128, 8, 256], F32)
        atp = BufPool([sb("at%d" % i, [128, 6, 256], BF16) for i in range(2)], "at")
        pTs = sb("pT_sb", [128, 2, TS], BF16)
        ssp = BufPool([sb("ss%d" % i, [128, 2], F32) for i in range(4)], "ss")
        ps = es.enter_context(nc.psum_tensor("ps", [128, 8, 512], F32))

        remaining = int(nc.sbuf_bytes_remaining)
        ring_pages = (remaining - 512) // 2048
        assert ring_pages >= 34, ring_pages
        wr = sb("wring", [128, ring_pages * 1024], BF16)

        bank_i = [0]

        def bank():
            b = bank_i[0] % 8
            bank_i[0] += 1
            return b

        def kp(w, l, c0, c1):
            return w[l, :, c0:c1].rearrange("(k p) c -> p k c", p=128)

        def ck(prefix, col0, n):
            return [(prefix, c) for c in range(col0 // 128, (col0 + n) // 128)]

        def q2(ap):
            return ap.rearrange("p (q t) -> p q t", q=2)

        for t3 in range(3):
            S.dma("sp", xT[:, :, t3 * 256:(t3 + 1) * 256], xin[:, :, t3 * 256:(t3 + 1) * 256],
                  writes=ck("x", t3 * 256, 256))
        S.dma("sp", cst[:], cst_in, writes=["cst"])
        S.dma("sp", mask[:], mask_in, writes=["mask"])
        S.dma("sp", gsgu[:], gsgu_in, writes=["gsgu"])
        S.dma("pool", cb[:], cstb_in, writes=["cb"])
        uflat = ubuf[:].rearrange("p a b -> p (a b)")
        S.dma("sp", uflat, wsT_in.rearrange("p l c -> p (l c)"), writes=["u"])
        for l in range(2):
            for h in range(8):
                S.op("dve", lambda l=l, h=h: V.tensor_tensor(
                    out=wsb[:, l, h, :], in0=uflat[:, l * 1024 + h * 128:l * 1024 + (h + 1) * 128],
                    in1=mask[:], op=ALU.mult), reads=["u", "mask"], writes=["wsb"])
        browf = brow[:].rearrange("p l c -> p (l c)")
        S.dma("sp", uflat, bT_in.rearrange("p l c -> p (l c)"), reads=["wsb"], writes=["u"])
        S.op("dve", lambda: V.memset(browf[0:64, :], 0.0), writes=["brow"])
        S.op("dve", lambda: V.tensor_copy(out=browf[0:1, :], in_=uflat[0:1, :]), reads=["u"], writes=["brow"])
        btmp = zp_tm[:, 0:2, :].rearrange("p a b -> p (a b)")
        S.op("dve", lambda: V.tensor_copy(out=btmp[32:33, :], in_=uflat[32:33, :]), reads=["u"],
             writes=[("zp", 0), ("zp", 1)])
        S.op("dve", lambda: V.tensor_tensor(out=uflat[32:33, :], in0=uflat[32:33, :], in1=btmp[32:33, :],
                                            op=ALU.subtract), reads=["u", ("zp", 0), ("zp", 1)], writes=["u"])
        S.op("dve", lambda: V.tensor_copy(out=browf[32:33, :], in_=uflat[32:33, :]), reads=["u"], writes=["brow"])
        S.op("dve", lambda: V.memset(zpc[:], 0.0), writes=[("zpc", 0), ("zpc", 1)])
        S.op("dve", lambda: V.memset(upc[:], 0.0),
             writes=[("upc", l, pr, j) for l in range(2) for pr in range(2) for j in range(NFF)])

        ones = cb[:, CB_ONES:CB_ONES + 128]
        one33 = cb[0:33, CB_ONE1:CB_ONE1 + 128]

        def Bm(off, g):
            return cb[:, off + g * 128: off + (g + 1) * 128]

        def emit_norm(gcol, col0, n, vcol=None, out_f32=None):
            sq, sqk, sqh = bigp.get()
            S.op("act", lambda: A.activation(out=sq[:, :, 0:n], in_=xT[:, :, col0:col0 + n], func=AF.Square),
                 reads=ck("x", col0, n), writes=[sqk])
            b = bank()
            S.mm([(ps[:, b, 0:n], ones, sq[:, k, 0:n], k == 0, k == 7) for k in range(8)],
                 reads=[sqk, "cb"], writes=[("ps", b)])
            bigp.put(sqh)
            rs, rsk, rsh = rsp.get()
            S.op("act", lambda: A.activation(out=rs[:, 0:n], in_=ps[:, b, 0:n], func=AF.Sqrt,
                                             bias=cst[:, C_EPS:C_EPS + 1], scale=1.0),
                 reads=[("ps", b), "cst"], writes=[rsk])
            S.op("dve", lambda: V.reciprocal(out=rs[:, 0:n], in_=rs[:, 0:n]), reads=[rsk], writes=[rsk])
            if vcol is not None:
                S.op("dve", lambda: V.tensor_tensor(out=rs[:, 0:n], in0=rs[:, 0:n],
                                                    in1=cst[:, C_VALID + vcol:C_VALID + vcol + n], op=ALU.mult),
                     reads=[rsk, "cst"], writes=[rsk])
            for k in range(8):
                if out_f32 is None:
                    o = hT[:, k, col0:col0 + n]
                    wk = ck("h", col0, n)
                else:
                    o = out_f32[:, k, 0:n]
                    wk = ["u"]
                if k < 5:
                    S.op("dve", lambda k=k, o=o: V.scalar_tensor_tensor(
                        out=o, in0=xT[:, k, col0:col0 + n], scalar=cst[:, gcol + k:gcol + k + 1],
                        in1=rs[:, 0:n], op0=ALU.mult, op1=ALU.mult),
                        reads=ck("x", col0, n) + [rsk, "cst"], writes=wk, sub=k)
                else:
                    tb, tbk, tbh = scr.get()
                    S.op("act", lambda k=k, tb=tb: A.activation(
                        out=tb[:, 0, 0:n], in_=xT[:, k, col0:col0 + n], func=AF.Identity,
                        scale=cst[:, gcol + k:gcol + k + 1]), reads=ck("x", col0, n) + ["cst"], writes=[tbk])
                    S.op("pool", lambda o=o, tb=tb: G.tensor_tensor(
                        out=o, in0=tb[:, 0, 0:n], in1=rs[:, 0:n], op=ALU.mult),
                        reads=[tbk, rsk], writes=wk, sub=k)
                    scr.put(tbh)
            rsp.put(rsh)

        zcount = [0]
        tcount = [0, 0]
        out_tickets = []
        bases = [-256, 512, 1280]
        phases = []
        ctl = {}

        def add_phase(units, body):
            phases.append((units, body))

        def make_sl(s, l):
            base = bases[s]
            cl = l * LW
            if s == 0 and l == 1:
                tiles = [(128, 128), (256, 256), (512, 256)]
            else:
                tiles = [(0, 256), (256, 256), (512, 256)]
            if s == 0 and l == 0:
                next_tiles = [(128, 128), (256, 256), (512, 256)]
            else:
                next_tiles = tiles
            NT = len(tiles)
            tbase = tcount[l]
            tcount[l] += NT

            def vcol_of(col0):
                return col0 if (s == 0 and col0 < 256) else None

            first_chunk = tiles[0][0] // 128
            slots = {}
            st = {"ple_norm_done": set(), "pa_norm_done": set()}

            def pa_tile(U, col0, n, is_last):
                Wpool, wp, wa, Wga = U["Wpool"], U["wp"], U["wa"], U["Wga"]
                nch = n // 128
                for ci in range(nch):
                    c = col0 // 128 + ci
                    slot = zcount[0] % 3
                    zcount[0] += 1
                    slots[c] = slot
                    for cbk in range(2):
                        b = bank()
                        S.mm([(ps[:, b, :], hT[:, k, c * 128:(c + 1) * 128],
                               Wpool.ap[:, k, cbk * 512:(cbk + 1) * 512], k == 0, k == 7) for k in range(8)],
                             reads=[("h", c)] + Wpool.keys, writes=[("ps", b)])
                        S.op("act", lambda b=b, slot=slot, cbk=cbk: A.copy(
                            out=zp_tm[:, slot, cbk * 512:(cbk + 1) * 512], in_=ps[:, b, :]),
                            reads=[("ps", b)], writes=[("zp", slot)], sub=cbk)
                if is_last:
                    ctl["release"]("Wpool")
                gates = {}

                def gate(djp):
                    bg = bank()
                    S.mm([(ps[:, bg, q * n:(q + 1) * n], Wga.ap[:, k, (djp * 2 + q) * 128:(djp * 2 + q + 1) * 128],
                           hT[:, k, col0:col0 + n], k == 0, k == 7) for q in range(2) for k in range(8)],
                         reads=ck("h", col0, n) + Wga.keys, writes=[("ps", bg)])
                    sc, sck, sch = scr.get()
                    S.op("act", lambda: A.activation(out=sc[:, :, 0:n], in_=q2(ps[:, bg, 0:2 * n]), func=AF.Sigmoid),
                         reads=[("ps", bg)], writes=[sck])
                    gates[djp] = (sc, sck, sch)

                gate(0)
                gate(1)
                pl, plk, plh = bigp.get()
                for ci in range(nch):
                    c = col0 // 128 + ci
                    slot = slots[c]
                    if c == first_chunk:
                        prev_ap, prev_key = zpc[:, l, :], ("zpc", l)
                    else:
                        prev_ap, prev_key = zp_tm[:, slots[c - 1], :], ("zp", slots[c - 1])
                    boff = CB_BFIRST if (s == 0 and c == 2) else CB_BCUR
                    for jh in range(2):
                        b = bank()
                        grp = []
                        for jj in range(4):
                            j = jh * 4 + jj
                            g = j // 2
                            o = ps[:, b, jj * 128:(jj + 1) * 128]
                            grp.append((o, zp_tm[:, slot, j * 128:(j + 1) * 128], Bm(boff, g), True, False))
                            grp.append((o, prev_ap[:, j * 128:(j + 1) * 128], Bm(CB_BPREV, g), False, True))
                        S.mm(grp, reads=[("zp", slot), prev_key, "cb"], writes=[("ps", b)])
                        S.op("act", lambda b=b, jh=jh, ci=ci: A.copy(
                            out=pl[:, jh * 4:(jh + 1) * 4, ci * 128:(ci + 1) * 128],
                            in_=ps[:, b, :].rearrange("p (j t) -> p j t", j=4)),
                            reads=[("ps", b)], writes=[plk], sub=(ci, jh))
                gate(2)
                yp, ypk, yph = bigp.get()
                for djp in range(4):
                    b = bank()
                    grp = []
                    for q in range(2):
                        dj = djp * 2 + q
                        g = dj // 2
                        for cc in range(2):
                            grp.append((ps[:, b, q * n:(q + 1) * n],
                                        wp.ap[:, g, cc, (dj % 2) * 128:(dj % 2 + 1) * 128],
                                        pl[:, 2 * g + cc, 0:n], cc == 0, cc == 1))
                    S.mm(grp, reads=[plk] + wp.keys, writes=[("ps", b)])
                    for q in range(2):
                        dj = djp * 2 + q
                        S.op("act", lambda b=b, q=q, dj=dj: A.activation(
                            out=yp[:, dj, 0:n], in_=ps[:, b, q * n:(q + 1) * n], func=AF.Identity,
                            scale=cst[:, cl + C_PSC + dj:cl + C_PSC + dj + 1]),
                            reads=[("ps", b), "cst"], writes=[ypk], sub=dj)
                bigp.put(plh)
                gate(3)
                for djp in range(4):
                    sc, sck, sch = gates[djp]
                    by = bank()
                    S.mm([(ps[:, by, q * n:(q + 1) * n], wa.ap[:, k, (djp * 2 + q) * 128:(djp * 2 + q + 1) * 128],
                           yp[:, k, 0:n], k == 0, k == 7) for q in range(2) for k in range(8)],
                         reads=[ypk] + wa.keys, writes=[("ps", by)])
                    S.op("dve", lambda by=by, sc=sc, djp=djp: V.tensor_tensor(
                        out=mixa[:, 2 * djp:2 * djp + 2, col0:col0 + n], in0=sc[:, :, 0:n],
                        in1=q2(ps[:, by, 0:2 * n]), op=ALU.mult),
                        reads=[sck, ("ps", by)], writes=ck("ma", col0, n), sub=djp)
                    scr.put(sch)
                bigp.put(yph)

            def pa_body(U):
                if tiles[0] not in st["pa_norm_done"]:
                    emit_norm(cl + C_MIXN, tiles[0][0], tiles[0][1], vcol_of(tiles[0][0]))
                for i, (col0, n) in enumerate(tiles):
                    if i + 1 < NT and tiles[i + 1] not in st["pa_norm_done"]:
                        emit_norm(cl + C_MIXN, tiles[i + 1][0], tiles[i + 1][1], vcol_of(tiles[i + 1][0]))
                    pa_tile(U, col0, n, i == NT - 1)
                last_c = (tiles[-1][0] + tiles[-1][1]) // 128 - 1
                S.op("act", lambda: A.copy(out=zpc[:, l, :], in_=zp_tm[:, slots[last_c], :]),
                     reads=[("zp", slots[last_c])], writes=[("zpc", l)])

            add_phase([("Wpool", kp(w_in, l, 0, 1024), [8, 1024]),
                       ("Wga", kp(w_in, l, 3072, 4096), [8, 1024]),
                       ("wp", w_pool[l].rearrange("g (c p) d -> p g c d", p=128), [4, 2, 256]),
                       ("wa", kp(w_a, l, 0, 1024), [8, 1024])], pa_body)

            def pb_tile(U, col0, n, is_last):
                Wu, Wv, wb, Wgb = U["Wu"], U["Wv"], U["wb"], U["Wgb"]
                nch = n // 128
                vslots = []
                vgs = []
                for ci in range(nch):
                    c = col0 // 128 + ci
                    slot = zcount[0] % 3
                    zcount[0] += 1
                    vslots.append(slot)
                    vg, vgk, vgh = vgp.get()
                    ss, ssk, ssh = ssp.get()
                    vgs.append((vg, vgk, vgh, ss, ssk, ssh, slot))
                    for cbk in range(2):
                        b = bank()
                        S.mm([(ps[:, b, :], hT[:, k, c * 128:(c + 1) * 128],
                               Wv.ap[:, k, cbk * 512:(cbk + 1) * 512], k == 0, k == 7) for k in range(8)],
                             reads=[("h", c)] + Wv.keys, writes=[("ps", b)])
                        S.op("act", lambda: A.activation(
                            out=vg[:, cbk * 512:(cbk + 1) * 512], in_=ps[:, b, :], func=AF.Gelu),
                            reads=[("ps", b)], writes=[vgk], sub=cbk)
                for (vg, vgk, vgh, ss, ssk, ssh, slot) in vgs:
                    S.op("dve", lambda: V.memset(ss[:], 0.0), writes=[ssk])
                    S.op("dve", lambda: V.scalar_tensor_tensor(
                        out=zp_tm[:, slot, :], in0=vg[:], scalar=1.0, in1=vg[:], op0=ALU.mult, op1=ALU.mult,
                        accum_out=ss[:, 0:1]), reads=[vgk, ssk], writes=[("zp", slot), ssk])
                for (vg, vgk, vgh, ss, ssk, ssh, slot) in vgs:
                    S.op("act", lambda: A.activation(
                        out=ss[:, 1:2], in_=ss[:, 0:1], func=AF.Sqrt, bias=cst[:, C_EPS:C_EPS + 1],
                        scale=1.0 / 1024.0), reads=[ssk, "cst"], writes=[ssk])
                for (vg, vgk, vgh, ss, ssk, ssh, slot) in vgs:
                    S.op("dve", lambda: V.reciprocal(out=ss[:, 1:2], in_=ss[:, 1:2]), reads=[ssk], writes=[ssk])
                for (vg, vgk, vgh, ss, ssk, ssh, slot) in vgs:
                    S.op("act", lambda: A.activation(out=vg[:], in_=vg[:], func=AF.Identity, scale=ss[:, 1:2]),
                         reads=[vgk, ssk], writes=[vgk])
                for (vg, vgk, vgh, ss, ssk, ssh, slot) in vgs:
                    S.op("pool", lambda: G.tensor_tensor(
                        out=zp_tm[:, slot, :], in0=vg[:], in1=gsgu[:, l, :], op=ALU.mult),
                        reads=[vgk, "gsgu"], writes=[("zp", slot)])
                    vgp.put(vgh)
                    ssp.put(ssh)
                if is_last:
                    ctl["release"]("Wv")
                for fp in range(4):
                    b = bank()
                    S.mm([(ps[:, b, q * n:(q + 1) * n], Wu.ap[:, k, (fp * 2 + q) * 128:(fp * 2 + q + 1) * 128],
                           hT[:, k, col0:col0 + n], k == 0, k == 7) for q in range(2) for k in range(8)],
                         reads=ck("h", col0, n) + Wu.keys, writes=[("ps", b)])
                    S.op("act", lambda b=b, fp=fp: A.activation(
                        out=ubuf[:, 2 * fp:2 * fp + 2, 0:n], in_=q2(ps[:, b, 0:2 * n]), func=AF.Gelu),
                        reads=[("ps", b)], writes=["u"], sub=fp)
                if is_last:
                    ctl["release"]("Wu")
                gt, gtk, gth = bigp.get()
                for ci in range(nch):
                    slot = vslots[ci]
                    for hh in range(2):
                        b = bank()
                        grp = []
                        for jj in range(4):
                            h = hh * 4 + jj
                            o = ps[:, b, jj * 128:(jj + 1) * 128]
                            grp.append((o, zp_tm[:, slot, h * 128:(h + 1) * 128], wsb[:, l, h, :], True, False))
                            grp.append((o, one33, brow[0:33, l, h * 128:(h + 1) * 128], False, True))
                        S.mm(grp, reads=[("zp", slot), "wsb", "brow", "cb"], writes=[("ps", b)])
                        S.op("dve", lambda b=b, hh=hh, ci=ci: V.tensor_tensor(
                            out=gt[:, hh * 4:(hh + 1) * 4, ci * 128:(ci + 1) * 128],
                            in0=ps[:, b, :].rearrange("p (j t) -> p j t", j=4),
                            in1=ubuf[:, hh * 4:(hh + 1) * 4, ci * 128:(ci + 1) * 128], op=ALU.mult),
                            reads=[("ps", b), "u"], writes=[gtk], sub=(ci, hh))
                gsc = {}
                for djp in range(4):
                    bg = bank()
                    S.mm([(ps[:, bg, q * n:(q + 1) * n], Wgb.ap[:, k, (djp * 2 + q) * 128:(djp * 2 + q + 1) * 128],
                           hT[:, k, col0:col0 + n], k == 0, k == 7) for q in range(2) for k in range(8)],
                         reads=ck("h", col0, n) + Wgb.keys, writes=[("ps", bg)])
                    sc, sck, sch = scr.get()
                    S.op("act", lambda bg=bg, sc=sc: A.activation(out=sc[:, :, 0:n], in_=q2(ps[:, bg, 0:2 * n]),
                                                                  func=AF.Sigmoid),
                         reads=[("ps", bg)], writes=[sck])
                    gsc[djp] = (sc, sck, sch)
                for djp in range(4):
                    sc, sck, sch = gsc[djp]
                    by = bank()
                    S.mm([(ps[:, by, q * n:(q + 1) * n], wb.ap[:, k, (djp * 2 + q) * 128:(djp * 2 + q + 1) * 128],
                           gt[:, k, 0:n], k == 0, k == 7) for q in range(2) for k in range(8)],
                         reads=[gtk] + wb.keys, writes=[("ps", by)])
                    S.op("dve", lambda by=by, sc=sc: V.tensor_tensor(
                        out=sc[:, :, 0:n], in0=sc[:, :, 0:n], in1=q2(ps[:, by, 0:2 * n]), op=ALU.mult),
                        reads=[sck, ("ps", by)], writes=[sck])
                    S.op("pool", lambda sc=sc, djp=djp: G.tensor_tensor(
                        out=mixa[:, 2 * djp:2 * djp + 2, col0:col0 + n], in0=sc[:, :, 0:n],
                        in1=mixa[:, 2 * djp:2 * djp + 2, col0:col0 + n], op=ALU.add),
                        reads=[sck] + ck("ma", col0, n), writes=ck("ma", col0, n), sub=djp)
                    scr.put(sch)
                bigp.put(gth)

            def pb_body(U):
                for i, (col0, n) in enumerate(tiles):
                    pb_tile(U, col0, n, i == NT - 1)

            add_phase([("Wv", kp(w_in, l, 2048, 3072), [8, 1024]),
                       ("Wu", kp(w_in, l, 1024, 2048), [8, 1024]),
                       ("Wgb", kp(w_in, l, 4096, 5120), [8, 1024]),
                       ("wb", kp(w_b, l, 0, 1024), [8, 1024])], pb_body)

            def pc_body(U):
                wo = U["wo"]
                for (col0, n) in tiles:
                    S.dma("pool", pTs[:, :, col0:col0 + n],
                          pin[l, :, :, base + 256 + col0: base + 256 + col0 + n],
                          writes=ck("pT", col0, n))

                def pc(col0, n):
                    for djp in range(4):
                        b = bank()
                        S.mm([(ps[:, b, q * n:(q + 1) * n], wo.ap[:, k, (djp * 2 + q) * 128:(djp * 2 + q + 1) * 128],
                               mixa[:, k, col0:col0 + n], k == 0, k == 7) for q in range(2) for k in range(8)],
                             reads=ck("ma", col0, n) + wo.keys, writes=[("ps", b)])
                        S.op("dve", lambda b=b, djp=djp: V.tensor_tensor(
                            out=xT[:, 2 * djp:2 * djp + 2, col0:col0 + n],
                            in0=xT[:, 2 * djp:2 * djp + 2, col0:col0 + n],
                            in1=q2(ps[:, b, 0:2 * n]), op=ALU.add),
                            reads=[("ps", b)] + ck("x", col0, n), writes=ck("x", col0, n), sub=djp)

                def nrm(col0, n):
                    emit_norm(cl + C_FFNN, col0, n, vcol_of(col0))

                pc(*tiles[0])
                for i in range(NT):
                    if i + 1 < NT:
                        pc(*tiles[i + 1])
                    nrm(*tiles[i])

            add_phase([("wo", kp(w_out, l, 0, 1024), [8, 1024])], pc_body)

            def make_ffn(bi, j0, cbn):
                last_block = (bi == len(FFN_BLOCKS) - 1)

                def up(U, col0, n, tpar):
                    upA, upB = U["upA"], U["upB"]
                    at, atk, ath = atp.get()
                    accs = {}

                    def stA(jl):
                        j = j0 + jl
                        b = bank()
                        grp = [(ps[:, b, 0:n], upA.ap[:, k, jl * 128:(jl + 1) * 128], hT[:, k, col0:col0 + n],
                                k == 0, k == 7) for k in range(8)]
                        grp += [(ps[:, b, n:2 * n], upB.ap[:, k, jl * 128:(jl + 1) * 128], hT[:, k, col0:col0 + n],
                                 k == 0, k == 7) for k in range(8)]
                        S.mm(grp, reads=ck("h", col0, n) + upA.keys + upB.keys, writes=[("ps", b)])
                        acc, acck, acch = scr.get()
                        accs[jl] = (acc, acck, acch, b)
                        if "memset" not in SKIP:
                            S.op("dve", lambda: V.memset(acc[:, :, n:n + 2], 0.0), writes=[acck, (acck, 0), (acck, 1)])
                        for q, f in ((0, j), (1, NFF + j)):
                            w2 = cst[:, cl + C_CW + 88 + f:cl + C_CW + 88 + f + 1]
                            bb = cst[:, cl + C_CB + f:cl + C_CB + f + 1]
                            S.op("act", lambda: A.activation(
                                out=acc[:, q, 0:n], in_=ps[:, b, q * n:q * n + n], func=AF.Identity, scale=w2, bias=bb),
                                reads=[("ps", b), "cst"], writes=[(acck, q)])

                    def stB(jl):
                        j = j0 + jl
                        acc, acck, acch, b = accs[jl]
                        for q, f in ((0, j), (1, NFF + j)):
                            w0 = cst[:, cl + C_CW + f:cl + C_CW + f + 1]
                            w1 = cst[:, cl + C_CW + 44 + f:cl + C_CW + 44 + f + 1]
                            S.op("dve", lambda: V.scalar_tensor_tensor(
                                out=acc[:, q, 1:n + 1], in0=ps[:, b, q * n:q * n + n], scalar=w1,
                                in1=acc[:, q, 1:n + 1], op0=ALU.mult, op1=ALU.add),
                                reads=[("ps", b), (acck, q), "cst"], writes=[(acck, q)])
                            S.op("dve", lambda: V.scalar_tensor_tensor(
                                out=acc[:, q, 2:n + 2], in0=ps[:, b, q * n:q * n + n], scalar=w0,
                                in1=acc[:, q, 2:n + 2], op0=ALU.mult, op1=ALU.add),
                                reads=[("ps", b), (acck, q), "cst"], writes=[(acck, q)])
                        if "add" not in SKIP:
                          S.op("pool", lambda: G.tensor_tensor(
                            out=acc[:, :, 0:2], in0=acc[:, :, 0:2],
                            in1=upc[:, l, tpar, j, :].rearrange("p (q t) -> p q t", q=2), op=ALU.add),
                            reads=[("upc", l, tpar, j), (acck, 0), (acck, 1)], writes=[(acck, 0), (acck, 1)],
                            sub="head")
                        if "copy" in SKIP:
                            pass
                        elif os.environ.get("KV_ACTCOPY"):
                            S.op("act", lambda: A.copy(
                                out=upc[:, l, 1 - tpar, j, :].rearrange("p (q t) -> p q t", q=2), in_=acc[:, :, n:n + 2]),
                                reads=[(acck, 0), (acck, 1)], writes=[("upc", l, 1 - tpar, j)], sub="tail")
                        else:
                            S.op("pool", lambda: G.tensor_copy(
                                out=upc[:, l, 1 - tpar, j, :].rearrange("p (q t) -> p q t", q=2), in_=acc[:, :, n:n + 2]),
                                reads=[(acck, 0), (acck, 1)], writes=[("upc", l, 1 - tpar, j)], sub="tail")

                    def stC(jl):
                        acc, acck, acch, b = accs[jl]
                        S.op("act", lambda: A.activation(out=acc[:, 0, 0:n], in_=acc[:, 0, 0:n], func=AF.Gelu),
                             reads=[(acck, 0)], writes=[(acck, 0)])
                        S.op("pool", lambda: G.tensor_tensor(
                            out=at[:, jl, 0:n], in0=acc[:, 0, 0:n], in1=acc[:, 1, 0:n], op=ALU.mult),
                            reads=[acck, (acck, 0), (acck, 1)], writes=[atk])
                        scr.put(acch)

                    for step in range(cbn + 2):
                        if step < cbn:
                            stA(step)
                        if 0 <= step - 1 < cbn:
                            stB(step - 1)
                        if 0 <= step - 2 < cbn:
                            stC(step - 2)
                    return (at, atk, ath)

                def down(U, col0, n, atr):
                    dwn = U["dwn"]
                    at, atk, ath = atr
                    for djp in range(4):
                        b = bank()
                        S.mm([(ps[:, b, q * n:(q + 1) * n], dwn.ap[:, jl, (djp * 2 + q) * 128:(djp * 2 + q + 1) * 128],
                               at[:, jl, 0:n], jl == 0, jl == cbn - 1) for q in range(2) for jl in range(cbn)],
                             reads=[atk] + dwn.keys, writes=[("ps", b)])
                        S.op("dve", lambda b=b, djp=djp: V.tensor_tensor(
                            out=xT[:, 2 * djp:2 * djp + 2, col0:col0 + n],
                            in0=xT[:, 2 * djp:2 * djp + 2, col0:col0 + n],
                            in1=q2(ps[:, b, 0:2 * n]), op=ALU.add),
                            reads=[("ps", b)] + ck("x", col0, n), writes=ck("x", col0, n), sub=djp)
                    atp.put(ath)

                def body(U):
                    ats = {}
                    ats[0] = up(U, tiles[0][0], tiles[0][1], (tbase + 0) % 2)
                    for i in range(NT):
                        if i + 1 < NT:
                            ats[i + 1] = up(U, tiles[i + 1][0], tiles[i + 1][1], (tbase + i + 1) % 2)
                            if i + 1 == NT - 1:
                                ctl["release"]("upA")
                                ctl["release"]("upB")
                        if last_block and i == 1:
                            emit_norm(cl + C_PLEN, tiles[0][0], tiles[0][1], vcol_of(tiles[0][0]))
                            st["ple_norm_done"].add(tiles[0])
                        down(U, tiles[i][0], tiles[i][1], ats[i])

                add_phase([("upA", kp(w_up, l, j0 * 128, (j0 + cbn) * 128), [8, cbn * 128]),
                           ("upB", kp(w_up, l, DFF + j0 * 128, DFF + (j0 + cbn) * 128), [8, cbn * 128]),
                           ("dwn", w_down[l, j0 * 128:(j0 + cbn) * 128, :].rearrange("(j p) c -> p j c", p=128),
                            [cbn, 1024])], body)

            for bi, (j0, cbn) in enumerate(FFN_BLOCKS):
                make_ffn(bi, j0, cbn)

            def ple_body(U):
                wpg, wpl = U["wpg"], U["wpl"]
                for ti, (col0, n) in enumerate(tiles):
                    for tj in (ti, ti + 1):
                        if tj < NT and tiles[tj] not in st["ple_norm_done"]:
                            emit_norm(cl + C_PLEN, tiles[tj][0], tiles[tj][1], vcol_of(tiles[tj][0]))
                            st["ple_norm_done"].add(tiles[tj])
                    for djp in range(4):
                        bg = bank()
                        S.mm([(ps[:, bg, q * n:(q + 1) * n], wpg.ap[:, k, (djp * 2 + q) * 128:(djp * 2 + q + 1) * 128],
                               hT[:, k, col0:col0 + n], k == 0, k == 7) for q in range(2) for k in range(8)],
                             reads=ck("h", col0, n) + wpg.keys, writes=[("ps", bg)])
                        sc, sck, sch = scr.get()
                        S.op("act", lambda: A.activation(out=sc[:, :, 0:n], in_=q2(ps[:, bg, 0:2 * n]),
                                                         func=AF.Sigmoid),
                             reads=[("ps", bg)], writes=[sck])
                        be = bank()
                        S.mm([(ps[:, be, q * n:(q + 1) * n], wpl.ap[:, kk, (djp * 2 + q) * 128:(djp * 2 + q + 1) * 128],
                               pTs[:, kk, col0:col0 + n], kk == 0, kk == 1) for q in range(2) for kk in range(2)],
                             reads=ck("pT", col0, n) + wpl.keys, writes=[("ps", be)])
                        S.op("dve", lambda: V.tensor_tensor(
                            out=sc[:, :, 0:n], in0=sc[:, :, 0:n], in1=q2(ps[:, be, 0:2 * n]), op=ALU.mult),
                            reads=[sck, ("ps", be)], writes=[sck])
                        S.op("pool", lambda: G.tensor_tensor(
                            out=xT[:, 2 * djp:2 * djp + 2, col0:col0 + n],
                            in0=xT[:, 2 * djp:2 * djp + 2, col0:col0 + n], in1=sc[:, :, 0:n], op=ALU.add),
                            reads=[sck] + ck("x", col0, n), writes=ck("x", col0, n), sub=djp)
                        scr.put(sch)
                    if l == 1:
                        if base + col0 >= 0:
                            emit_norm(C_FIN, col0, n, None, out_f32=ubuf)
                            oc = base + col0
                            out_tickets.append(S.dma("sp", out[:, :, oc:oc + n], ubuf[:, :, 0:n], reads=["u"]))
                        if s + 1 < 3:
                            c0 = (col0 // 256) * 256
                            nb = bases[s + 1]
                            S.dma("sp", xT[:, :, c0:c0 + 256], xin[:, :, nb + 256 + c0: nb + 256 + c0 + 256],
                                  writes=ck("x", c0, 256))

            add_phase([("wpg", kp(w_pg, l, 0, 1024), [8, 1024]),
                       ("wpl", kp(w_ple, l, 0, 1024), [2, 1024])], ple_body)

        def x_load_phase(s):
            base = bases[s]

            def body(U):
                return
            add_phase([], body)

        for s in range(3):
            x_load_phase(s)
            for l in range(2):
                make_sl(s, l)

        ring = {"ptr": 0}
        live = {}
        loaded = {}
        regions = {}

        def try_load(pi, j):
            name, src, dims = phases[pi][0][j]
            size = 1
            for d_ in dims:
                size *= d_
            assert size % 1024 == 0
            npg = size // 1024
            if os.environ.get("KV_FIFO"):
                p0 = ring["ptr"]
                if p0 + npg > ring_pages:
                    p0 = 0
                for regs in live.values():
                    for (a, b_) in regs:
                        if not (p0 + npg <= a or p0 >= b_):
                            return False
                ring["ptr"] = p0 + npg
            else:
                occ = [r for regs in live.values() for r in regs]
                if npg >= 6:
                    cands = list(range(0, ring_pages - npg + 1, npg))
                else:
                    cands = list(range(ring_pages - npg, -1, -1))
                p0 = None
                for c in cands:
                    if all(c + npg <= a or c >= b_ for (a, b_) in occ):
                        p0 = c
                        break
                if p0 is None:
                    return False
            live.setdefault(pi, []).append((p0, p0 + npg))
            regions[(pi, j)] = (p0, p0 + npg)
            if os.environ.get("KV_DBG"):
                print("LOAD phase", pi, name, "pages", p0, p0 + npg, "at_phase", cur_phase[0])
            flat = wr[:, p0 * 1024:p0 * 1024 + size]
            if len(dims) == 2:
                ap = flat.rearrange("p (a b) -> p a b", a=dims[0])
            else:
                ap = flat.rearrange("p (a b c) -> p a b c", a=dims[0], b=dims[1])
            keys = [("wr", pg) for pg in range(p0, p0 + npg)]
            S.dma("pool", ap, src, reads=(), writes=keys)
            loaded[(pi, j)] = Unit(ap, keys)
            return True

        cur_phase = [0]

        def prefetch_from(pi):
            for pj in range(pi + 1, min(pi + 5, len(phases))):
                for j in range(len(phases[pj][0])):
                    if (pj, j) not in loaded:
                        if not try_load(pj, j):
                            return

        def release_unit(name):
            pi = cur_phase[0]
            units = phases[pi][0]
            j = [u[0] for u in units].index(name)
            live[pi].remove(regions[(pi, j)])
            prefetch_from(pi)

        ctl["release"] = release_unit
        for pi in range(len(phases)):
            cur_phase[0] = pi
            units, body = phases[pi]
            for j in range(len(units)):
                if (pi, j) not in loaded:
                    ok = try_load(pi, j)
                    assert ok, "weight ring too small for phase %d" % pi
            prefetch_from(pi)
            body({units[j][0]: loaded[(pi, j)] for j in range(len(units))})
            live.pop(pi, None)

        for t in out_tickets:
            S._wait("sp", [t])
    return nc


_CACHE = {}


def _pool_mats(first):
    Bcur = np.zeros((4, 128, 128), np.float32)
    Bprev = np.zeros((4, 128, 128), np.float32)
    for g, w in enumerate((2, 4, 8, 16)):
        for t in range(128):
            cnt = min(t + 1, w) if first else w
            for sg in range(t - w + 1, t + 1):
                if sg >= 0:
                    Bcur[g, sg, t] += 1.0 / cnt
                elif not first:
                    Bprev[g, sg + 128, t] += 1.0 / cnt
            Bcur[g, t, t] -= 1.0
    return Bcur, Bprev


def kernel(x, p, mix_norm, w_in, w_pool, pool_scale, sgu_norm, w_spatial, b_spatial,
           w_branch_a, w_branch_b, w_out, ffn_norm, w_up, conv_w, conv_b, w_down,
           ple_norm, w_ple_gate, w_ple, final_norm):
    f = np.float32
    x = np.asarray(x, f)
    p = np.asarray(p, f)
    if "nc" not in _CACHE:
        _CACHE["nc"] = build_program()
    nc = _CACHE["nc"]

    def vec8(v):
        return np.asarray(v, f).reshape(8, 128).T

    Bcur, Bprev = _pool_mats(False)
    Bfirst, _ = _pool_mats(True)
    mask = (np.arange(128)[None, :] >= np.arange(128)[:, None]).astype(f)
    gsgu = np.ascontiguousarray(np.broadcast_to(np.asarray(sgu_norm, f)[None, :, :], (128, 2, 1024)))
    bTt = np.ascontiguousarray(np.broadcast_to(np.asarray(b_spatial, f).reshape(1, 2, 1024), (128, 2, 1024)))
    wsT = np.ascontiguousarray(np.transpose(np.asarray(w_spatial, f), (3, 0, 1, 2)).reshape(128, 2, 1024))
    shared = {
        "w_in": np.asarray(w_in, f), "w_pool": np.asarray(w_pool, f),
        "w_branch_a": np.asarray(w_branch_a, f), "w_branch_b": np.asarray(w_branch_b, f),
        "w_out": np.asarray(w_out, f), "w_up": np.asarray(w_up, f), "w_down": np.asarray(w_down, f),
        "w_ple_gate": np.asarray(w_ple_gate, f), "w_ple": np.asarray(w_ple, f),
        "gsgu": gsgu, "bT": bTt, "wsT": wsT, "mask": mask,
    }
    cst0 = np.zeros((128, NCST), f)
    for l in range(2):
        o = l * LW
        cst0[:, o + C_MIXN:o + C_MIXN + 8] = vec8(mix_norm[l])
        cst0[:, o + C_FFNN:o + C_FFNN + 8] = vec8(ffn_norm[l])
        cst0[:, o + C_PLEN:o + C_PLEN + 8] = vec8(ple_norm[l])
        cst0[:, o + C_PSC:o + C_PSC + 8] = vec8(pool_scale[l])
        for k in range(3):
            cst0[:, o + C_CW + k * 44:o + C_CW + (k + 1) * 44] = np.asarray(conv_w[l][k], f).reshape(44, 128).T
        cst0[:, o + C_CB:o + C_CB + 44] = np.asarray(conv_b[l], f).reshape(44, 128).T
    cst0[:, C_FIN:C_FIN + 8] = vec8(final_norm)
    cst0[:, C_EPS:C_EPS + 8] = EPS

    in_maps = []
    for core in range(8):
        b = core // 4
        t0 = (core % 4) * TOK
        xs = np.zeros((TIN, D), f)
        ps_ = np.zeros((2, TIN, 256), f)
        lo = t0 - HALO
        if lo >= 0:
            xs[:] = x[b, lo:t0 + TOK]
            ps_[:] = p[:, b, lo:t0 + TOK]
        else:
            xs[HALO:] = x[b, 0:TOK]
            ps_[:, HALO:] = p[:, b, 0:TOK]
        xTc = np.ascontiguousarray(xs.T.reshape(8, 128, TIN).transpose(1, 0, 2))
        pTc = np.ascontiguousarray(ps_.transpose(0, 2, 1).reshape(2, 2, 128, TIN).transpose(0, 2, 1, 3))
        cstc = cst0.copy()
        cstc[:, C_VALID:C_VALID + 256] = 1.0 if lo >= 0 else 0.0
        cbc = np.zeros((128, NCSTB), f)
        cbc[:, CB_ONES:CB_ONES + 128] = 1.0 / 1024.0
        cbc[:, CB_ONE1:CB_ONE1 + 128] = 1.0
        for g in range(4):
            cbc[:, CB_BCUR + g * 128:CB_BCUR + (g + 1) * 128] = Bcur[g]
            cbc[:, CB_BPREV + g * 128:CB_BPREV + (g + 1) * 128] = Bprev[g]
            cbc[:, CB_BFIRST + g * 128:CB_BFIRST + (g + 1) * 128] = (Bfirst[g] if lo < 0 else Bcur[g])
        m = dict(shared)
        m.update({"xT": xTc, "pT": pTc, "cst": cstc, "cstb": cbc})
        in_maps.append(m)

    res = run_bass_kernel_spmd(nc, in_maps, core_ids=list(range(8)))
    outp = np.zeros((2, SEQ, D), f)
    for core in range(8):
        b = core // 4
        t0 = (core % 4) * TOK
        oT = np.asarray(res.results[core]["outT"], f)
        outp[b, t0:t0 + TOK] = oT.transpose(2, 1, 0).reshape(TOK, D)
    return outp
```

```python
import os
import numpy as np
from contextlib import ExitStack
import concourse.bass as bass
import concourse.mybir as mybir
from concourse.bass_utils import run_bass_kernel_spmd

F32 = mybir.dt.float32
BF16 = mybir.dt.bfloat16
AF = mybir.ActivationFunctionType
ALU = mybir.AluOpType

D = 1024
DFF = 2816
NFF = 22
SEQ = 8192
TOK = 2048
HALO = 256
TIN = TOK + HALO
TS = 768
EPS = 1e-6
LW = 208
C_MIXN, C_FFNN, C_PLEN, C_PSC, C_CW, C_CB = 0, 8, 16, 24, 32, 164
C_FIN = 2 * LW
C_VALID = C_FIN + 8
C_EPS = C_VALID + 256
NCST = C_EPS + 8
CB_ONES, CB_BCUR, CB_BPREV, CB_BFIRST = 0, 128, 640, 1152
CB_ONE1 = 1664
NCSTB = 1792
ND = 8
SKIP = os.environ.get('KV_SKIP', '')
FFN_BLOCKS = [(0, 6), (6, 6), (12, 6), (18, 4)]


class Sched:
    def __init__(self, nc, es):
        self.nc = nc
        self.engs = {"pe": nc.tensor, "act": nc.scalar, "dve": nc.vector,
                     "pool": nc.gpsimd, "sp": nc.sync}
        self.sem = {}
        self.cnt = {}
        for e in self.engs:
            self.sem[e] = es.enter_context(nc.semaphore("s_" + e))
            self.cnt[e] = 0
        self.dsem = {}
        self.dcnt = {}
        self.drr = {}
        for q in ("pool", "sp"):
            self.dsem[q] = [es.enter_context(nc.semaphore("d_%s%d" % (q, i))) for i in range(ND)]
            self.dcnt[q] = [0] * ND
            self.drr[q] = 0
        self.seen = {e: {} for e in self.engs}
        self.lastw = {}
        self.readers = {}
        self.lastw_sub = {}
        self.readers_sub = {}
        self.nops = 0

    def _same(self, e, t, sub, tsub):
        if t[3] < self.cnt[e] - 3:
            return False
        if sub is not None and tsub is not None and sub != tsub:
            return False
        return True

    def _deps(self, e, reads, writes, sub=None):
        deps = []
        for k in reads:
            w = self.lastw.get(k)
            if w is not None:
                if w[0] != e or w[0] == "dma":
                    deps.append(w)
                elif self._same(e, w, sub, self.lastw_sub.get(k)):
                    deps.append(w)
            if isinstance(k, tuple) and k[0] == "ps":
                for r in self.readers.get(k, {}).values():
                    if r[0] != e:
                        deps.append(r)
        for k in writes:
            w = self.lastw.get(k)
            if w is not None:
                if w[0] != e or w[0] == "dma":
                    deps.append(w)
                elif self._same(e, w, sub, self.lastw_sub.get(k)):
                    deps.append(w)
            for sk, r in self.readers.get(k, {}).items():
                if r[0] != e or r[0] == "dma":
                    deps.append(r)
                elif self._same(e, r, sub, self.readers_sub.get(k, {}).get(sk)):
                    deps.append(r)
        return deps

    def _wait(self, e, deps):
        best = {}
        for d in deps:
            if d[2] not in best or best[d[2]][3] < d[3]:
                best[d[2]] = d
        for d in best.values():
            if self.seen[e].get(d[2], 0) >= d[3]:
                continue
            self.engs[e].wait_ge(d[1], d[3])
            self.seen[e][d[2]] = d[3]

    def _reg(self, t, reads, writes, sub=None):
        for k in reads:
            self.readers.setdefault(k, {})[t[2]] = t
            self.readers_sub.setdefault(k, {})[t[2]] = sub
        for k in writes:
            self.lastw[k] = t
            self.lastw_sub[k] = sub
            self.readers[k] = {}
            self.readers_sub[k] = {}

    def op(self, e, fn, reads=(), writes=(), sub=None):
        self._wait(e, self._deps(e, reads, writes, sub))
        ins = fn()
        self.cnt[e] += 1
        ins.then_inc(self.sem[e], 1)
        t = (e, self.sem[e], "s_" + e, self.cnt[e])
        self._reg(t, reads, writes, sub)
        self.nops += 1
        return t

    def mm(self, grp, reads=(), writes=()):
        e = "pe"
        self._wait(e, self._deps(e, reads, writes))
        ins = None
        for (o, l, r, st, sp) in grp:
            ins = self.nc.tensor.matmul(o, l, r, start=st, stop=sp)
            self.nops += 1
        self.cnt[e] += 1
        ins.then_inc(self.sem[e], 1)
        t = (e, self.sem[e], "s_" + e, self.cnt[e])
        self._reg(t, reads, writes)
        return t

    def dma(self, q, out, in_, reads=(), writes=()):
        i = self.drr[q]
        self.drr[q] = (i + 1) % ND
        sem = self.dsem[q][i]
        key = "d_%s%d" % (q, i)
        deps = self._deps("dma", reads, writes)
        if self.dcnt[q][i] > 0:
            deps.append(("dma", sem, key, self.dcnt[q][i]))
        self._wait(q, deps)
        ins = self.engs[q].dma_start(out=out, in_=in_)
        self.dcnt[q][i] += 16
        ins.then_inc(sem, 16)
        t = ("dma", sem, key, self.dcnt[q][i])
        self._reg(t, reads, writes)
        self.nops += 1
        return t


class BufPool:
    def __init__(self, bufs, name):
        self.bufs = bufs
        self.name = name
        self.free = list(range(len(bufs)))

    def get(self):
        assert self.free, "pool %s exhausted" % self.name
        j = self.free.pop(0)
        return self.bufs[j], (self.name, j), j

    def put(self, j):
        assert j not in self.free
        self.free.append(j)


class Unit:
    def __init__(self, ap, keys):
        self.ap = ap
        self.keys = keys


def build_program():
    nc = bass.Bass("TRN2", target_bir_lowering=False)

    def dram(name, shape, kind="ExternalInput"):
        return nc.dram_tensor(name, shape, F32, kind=kind).ap()

    xin = dram("xT", [128, 8, TIN])
    pin = dram("pT", [2, 128, 2, TIN])
    w_in = dram("w_in", [2, 1024, 5120])
    w_pool = dram("w_pool", [2, 4, 256, 256])
    w_a = dram("w_branch_a", [2, 1024, 1024])
    w_b = dram("w_branch_b", [2, 1024, 1024])
    w_out = dram("w_out", [2, 1024, 1024])
    w_up = dram("w_up", [2, 1024, 2 * DFF])
    w_down = dram("w_down", [2, DFF, 1024])
    w_pg = dram("w_ple_gate", [2, 1024, 1024])
    w_ple = dram("w_ple", [2, 256, 1024])
    cst_in = dram("cst", [128, NCST])
    cstb_in = dram("cstb", [128, NCSTB])
    gsgu_in = dram("gsgu", [128, 2, 1024])
    bT_in = dram("bT", [128, 2, 1024])
    wsT_in = dram("wsT", [128, 2, 1024])
    mask_in = dram("mask", [128, 128])
    out = dram("outT", [128, 8, TOK], kind="ExternalOutput")

    with ExitStack() as es:
        S = Sched(nc, es)
        V = nc.vector
        G = nc.gpsimd
        A = nc.scalar

        def sb(name, shape, dt):
            return es.enter_context(nc.sbuf_tensor(name, shape, dt))

        xT = sb("xT_sb", [128, 8, TS], F32)
        hT = sb("hT_sb", [128, 8, TS], BF16)
        mixa = sb("mixa_sb", [128, 8, TS], BF16)
        cst = sb("cst_sb", [128, NCST], F32)
        cb = sb("cb_sb", [128, NCSTB], BF16)
        mask = sb("mask_sb", [128, 128], F32)
        gsgu = sb("gsgu_sb", [128, 2, 1024], F32)
        brow = sb("brow_sb", [128, 2, 1024], BF16)
        wsb = sb("wsb_sb", [128, 2, 8, 128], BF16)
        zpc = sb("zpc_sb", [128, 2, 1024], BF16)
        upc = sb("upc_sb", [128, 2, 2, NFF, 4], F32)
        bigp = BufPool([sb("pa%d" % i, [128, 8, 256], BF16) for i in range(3)], "pa")
        rsp = BufPool([sb("rs%d" % i, [128, 256], F32) for i in range(2)], "rs")
        zp_tm = sb("zp_tm", [128, 3, 1024], BF16)
        vgp = BufPool([sb("vg%d" % i, [128, 1024], F32) for i in range(2)], "vg")
        scr = BufPool([sb("sc%d" % i, [128, 2, 264], F32) for i in range(6)], "sc")
        ubuf = sb("u_sb", [128, 8, 256], F32)
        atp = BufPool([sb("at%d" % i, [128, 6, 256], BF16) for i in range(2)], "at")
        pTs = sb("pT_sb", [128, 2, TS], BF16)
        ssp = BufPool([sb("ss%d" % i, [128, 2], F32) for i in range(4)], "ss")
        ps = es.enter_context(nc.psum_tensor("ps", [128, 8, 512], F32))

        remaining = int(nc.sbuf_bytes_remaining)
        ring_pages = (remaining - 512) // 2048
        assert ring_pages >= 34, ring_pages
        wr = sb("wring", [128, ring_pages * 1024], BF16)

        bank_i = [0]

        def bank():
            b = bank_i[0] % 8
            bank_i[0] += 1
            return b

        def kp(w, l, c0, c1):
            return w[l, :, c0:c1].rearrange("(k p) c -> p k c", p=128)

        def ck(prefix, col0, n):
            return [(prefix, c) for c in range(col0 // 128, (col0 + n) // 128)]

        def q2(ap):
            return ap.rearrange("p (q t) -> p q t", q=2)

        S.dma("sp", cst[:], cst_in, writes=["cst"])
        S.dma("sp", mask[:], mask_in, writes=["mask"])
        S.dma("sp", gsgu[:], gsgu_in, writes=["gsgu"])
        S.dma("pool", cb[:], cstb_in, writes=["cb"])
        uflat = ubuf[:].rearrange("p a b -> p (a b)")
        S.dma("sp", uflat, wsT_in.rearrange("p l c -> p (l c)"), writes=["u"])
        for l in range(2):
            for h in range(8):
                S.op("dve", lambda l=l, h=h: V.tensor_tensor(
                    out=wsb[:, l, h, :], in0=uflat[:, l * 1024 + h * 128:l * 1024 + (h + 1) * 128],
                    in1=mask[:], op=ALU.mult), reads=["u", "mask"], writes=["wsb"])
        browf = brow[:].rearrange("p l c -> p (l c)")
        S.dma("sp", uflat, bT_in.rearrange("p l c -> p (l c)"), reads=["wsb"], writes=["u"])
        S.op("dve", lambda: V.memset(browf[0:64, :], 0.0), writes=["brow"])
        S.op("dve", lambda: V.tensor_copy(out=browf[0:1, :], in_=uflat[0:1, :]), reads=["u"], writes=["brow"])
        btmp = zp_tm[:, 0:2, :].rearrange("p a b -> p (a b)")
        S.op("dve", lambda: V.tensor_copy(out=btmp[32:33, :], in_=uflat[32:33, :]), reads=["u"],
             writes=[("zp", 0), ("zp", 1)])
        S.op("dve", lambda: V.tensor_tensor(out=uflat[32:33, :], in0=uflat[32:33, :], in1=btmp[32:33, :],
                                            op=ALU.subtract), reads=["u", ("zp", 0), ("zp", 1)], writes=["u"])
        S.op("dve", lambda: V.tensor_copy(out=browf[32:33, :], in_=uflat[32:33, :]), reads=["u"], writes=["brow"])
        S.op("dve", lambda: V.memset(zpc[:], 0.0), writes=[("zpc", 0), ("zpc", 1)])
        S.op("dve", lambda: V.memset(upc[:], 0.0),
             writes=[("upc", l, pr, j) for l in range(2) for pr in range(2) for j in range(NFF)])

        ones = cb[:, CB_ONES:CB_ONES + 128]
        one33 = cb[0:33, CB_ONE1:CB_ONE1 + 128]

        def Bm(off, g):
            return cb[:, off + g * 128: off + (g + 1) * 128]

        def emit_norm(gcol, col0, n, vcol=None, out_f32=None):
            sq, sqk, sqh = bigp.get()
            S.op("act", lambda: A.activation(out=sq[:, :, 0:n], in_=xT[:, :, col0:col0 + n], func=AF.Square),
                 reads=ck("x", col0, n), writes=[sqk])
            b = bank()
            S.mm([(ps[:, b, 0:n], ones, sq[:, k, 0:n], k == 0, k == 7) for k in range(8)],
                 reads=[sqk, "cb"], writes=[("ps", b)])
            bigp.put(sqh)
            rs, rsk, rsh = rsp.get()
            S.op("act", lambda: A.activation(out=rs[:, 0:n], in_=ps[:, b, 0:n], func=AF.Sqrt,
                                             bias=cst[:, C_EPS:C_EPS + 1], scale=1.0),
                 reads=[("ps", b), "cst"], writes=[rsk])
            S.op("dve", lambda: V.reciprocal(out=rs[:, 0:n], in_=rs[:, 0:n]), reads=[rsk], writes=[rsk])
            if vcol is not None:
                S.op("dve", lambda: V.tensor_tensor(out=rs[:, 0:n], in0=rs[:, 0:n],
                                                    in1=cst[:, C_VALID + vcol:C_VALID + vcol + n], op=ALU.mult),
                     reads=[rsk, "cst"], writes=[rsk])
            for k in range(8):
                if out_f32 is None:
                    o = hT[:, k, col0:col0 + n]
                    wk = ck("h", col0, n)
                else:
                    o = out_f32[:, k, 0:n]
                    wk = ["u"]
                if k < 5:
                    S.op("dve", lambda k=k, o=o: V.scalar_tensor_tensor(
                        out=o, in0=xT[:, k, col0:col0 + n], scalar=cst[:, gcol + k:gcol + k + 1],
                        in1=rs[:, 0:n], op0=ALU.mult, op1=ALU.mult),
                        reads=ck("x", col0, n) + [rsk, "cst"], writes=wk, sub=k)
                else:
                    tb, tbk, tbh = scr.get()
                    S.op("act", lambda k=k, tb=tb: A.activation(
                        out=tb[:, 0, 0:n], in_=xT[:, k, col0:col0 + n], func=AF.Identity,
                        scale=cst[:, gcol + k:gcol + k + 1]), reads=ck("x", col0, n) + ["cst"], writes=[tbk])
                    S.op("pool", lambda o=o, tb=tb: G.tensor_tensor(
                        out=o, in0=tb[:, 0, 0:n], in1=rs[:, 0:n], op=ALU.mult),
                        reads=[tbk, rsk], writes=wk, sub=k)
                    scr.put(tbh)
            rsp.put(rsh)

        zcount = [0]
        tcount = [0, 0]
        out_tickets = []
        bases = [-256, 512, 1280]
        phases = []
        ctl = {}

        def add_phase(units, body):
            phases.append((units, body))

        def make_sl(s, l):
            base = bases[s]
            cl = l * LW
            if s == 0 and l == 1:
                tiles = [(128, 128), (256, 256), (512, 256)]
            else:
                tiles = [(0, 256), (256, 256), (512, 256)]
            if s == 0 and l == 0:
                next_tiles = [(128, 128), (256, 256), (512, 256)]
            else:
                next_tiles = tiles
            NT = len(tiles)
            tbase = tcount[l]
            tcount[l] += NT

            def vcol_of(col0):
                return col0 if (s == 0 and col0 < 256) else None

            first_chunk = tiles[0][0] // 128
            slots = {}
            st = {"ple_norm_done": set(), "pa_norm_done": set()}

            def pa_tile(U, col0, n, is_last):
                Wpool, wp, wa, Wga = U["Wpool"], U["wp"], U["wa"], U["Wga"]
                nch = n // 128
                for ci in range(nch):
                    c = col0 // 128 + ci
                    slot = zcount[0] % 3
                    zcount[0] += 1
                    slots[c] = slot
                    for cbk in range(2):
                        b = bank()
                        S.mm([(ps[:, b, :], hT[:, k, c * 128:(c + 1) * 128],
                               Wpool.ap[:, k, cbk * 512:(cbk + 1) * 512], k == 0, k == 7) for k in range(8)],
                             reads=[("h", c)] + Wpool.keys, writes=[("ps", b)])
                        S.op("act", lambda b=b, slot=slot, cbk=cbk: A.copy(
                            out=zp_tm[:, slot, cbk * 512:(cbk + 1) * 512], in_=ps[:, b, :]),
                            reads=[("ps", b)], writes=[("zp", slot)], sub=cbk)
                if is_last:
                    ctl["release"]("Wpool")
                gates = {}

                def gate(djp):
                    bg = bank()
                    S.mm([(ps[:, bg, q * n:(q + 1) * n], Wga.ap[:, k, (djp * 2 + q) * 128:(djp * 2 + q + 1) * 128],
                           hT[:, k, col0:col0 + n], k == 0, k == 7) for q in range(2) for k in range(8)],
                         reads=ck("h", col0, n) + Wga.keys, writes=[("ps", bg)])
                    sc, sck, sch = scr.get()
                    S.op("act", lambda: A.activation(out=sc[:, :, 0:n], in_=q2(ps[:, bg, 0:2 * n]), func=AF.Sigmoid),
                         reads=[("ps", bg)], writes=[sck])
                    gates[djp] = (sc, sck, sch)

                gate(0)
                gate(1)
                pl, plk, plh = bigp.get()
                for ci in range(nch):
                    c = col0 // 128 + ci
                    slot = slots[c]
                    if c == first_chunk:
                        prev_ap, prev_key = zpc[:, l, :], ("zpc", l)
                    else:
                        prev_ap, prev_key = zp_tm[:, slots[c - 1], :], ("zp", slots[c - 1])
                    boff = CB_BFIRST if (s == 0 and c == 2) else CB_BCUR
                    for jh in range(2):
                        b = bank()
                        grp = []
                        for jj in range(4):
                            j = jh * 4 + jj
                            g = j // 2
                            o = ps[:, b, jj * 128:(jj + 1) * 128]
                            grp.append((o, zp_tm[:, slot, j * 128:(j + 1) * 128], Bm(boff, g), True, False))
                            grp.append((o, prev_ap[:, j * 128:(j + 1) * 128], Bm(CB_BPREV, g), False, True))
                        S.mm(grp, reads=[("zp", slot), prev_key, "cb"], writes=[("ps", b)])
                        S.op("act", lambda b=b, jh=jh, ci=ci: A.copy(
                            out=pl[:, jh * 4:(jh + 1) * 4, ci * 128:(ci + 1) * 128],
                            in_=ps[:, b, :].rearrange("p (j t) -> p j t", j=4)),
                            reads=[("ps", b)], writes=[plk], sub=(ci, jh))
                gate(2)
                yp, ypk, yph = bigp.get()
                for djp in range(4):
                    b = bank()
                    grp = []
                    for q in range(2):
                        dj = djp * 2 + q
                        g = dj // 2
                        for cc in range(2):
                            grp.append((ps[:, b, q * n:(q + 1) * n],
                                        wp.ap[:, g, cc, (dj % 2) * 128:(dj % 2 + 1) * 128],
                                        pl[:, 2 * g + cc, 0:n], cc == 0, cc == 1))
                    S.mm(grp, reads=[plk] + wp.keys, writes=[("ps", b)])
                    for q in range(2):
                        dj = djp * 2 + q
                        S.op("act", lambda b=b, q=q, dj=dj: A.activation(
                            out=yp[:, dj, 0:n], in_=ps[:, b, q * n:(q + 1) * n], func=AF.Identity,
                            scale=cst[:, cl + C_PSC + dj:cl + C_PSC + dj + 1]),
                            reads=[("ps", b), "cst"], writes=[ypk], sub=dj)
                bigp.put(plh)
                gate(3)
                for djp in range(4):
                    sc, sck, sch = gates[djp]
                    by = bank()
                    S.mm([(ps[:, by, q * n:(q + 1) * n], wa.ap[:, k, (djp * 2 + q) * 128:(djp * 2 + q + 1) * 128],
                           yp[:, k, 0:n], k == 0, k == 7) for q in range(2) for k in range(8)],
                         reads=[ypk] + wa.keys, writes=[("ps", by)])
                    S.op("dve", lambda by=by, sc=sc, djp=djp: V.tensor_tensor(
                        out=mixa[:, 2 * djp:2 * djp + 2, col0:col0 + n], in0=sc[:, :, 0:n],
                        in1=q2(ps[:, by, 0:2 * n]), op=ALU.mult),
                        reads=[sck, ("ps", by)], writes=ck("ma", col0, n), sub=djp)
                    scr.put(sch)
                bigp.put(yph)

            def pa_body(U):
                if tiles[0] not in st["pa_norm_done"]:
                    emit_norm(cl + C_MIXN, tiles[0][0], tiles[0][1], vcol_of(tiles[0][0]))
                for i, (col0, n) in enumerate(tiles):
                    if i + 1 < NT and tiles[i + 1] not in st["pa_norm_done"]:
                        emit_norm(cl + C_MIXN, tiles[i + 1][0], tiles[i + 1][1], vcol_of(tiles[i + 1][0]))
                    pa_tile(U, col0, n, i == NT - 1)
                last_c = (tiles[-1][0] + tiles[-1][1]) // 128 - 1
                S.op("act", lambda: A.copy(out=zpc[:, l, :], in_=zp_tm[:, slots[last_c], :]),
                     reads=[("zp", slots[last_c])], writes=[("zpc", l)])

            add_phase([("Wpool", kp(w_in, l, 0, 1024), [8, 1024]),
                       ("Wga", kp(w_in, l, 3072, 4096), [8, 1024]),
                       ("wp", w_pool[l].rearrange("g (c p) d -> p g c d", p=128), [4, 2, 256]),
                       ("wa", kp(w_a, l, 0, 1024), [8, 1024])], pa_body)

            def pb_tile(U, col0, n, is_last):
                Wu, Wv, wb, Wgb = U["Wu"], U["Wv"], U["wb"], U["Wgb"]
                nch = n // 128
                vslots = []
                vgs = []
                for ci in range(nch):
                    c = col0 // 128 + ci
                    slot = zcount[0] % 3
                    zcount[0] += 1
                    vslots.append(slot)
                    vg, vgk, vgh = vgp.get()
                    ss, ssk, ssh = ssp.get()
                    vgs.append((vg, vgk, vgh, ss, ssk, ssh, slot))
                    for cbk in range(2):
                        b = bank()
                        S.mm([(ps[:, b, :], hT[:, k, c * 128:(c + 1) * 128],
                               Wv.ap[:, k, cbk * 512:(cbk + 1) * 512], k == 0, k == 7) for k in range(8)],
                             reads=[("h", c)] + Wv.keys, writes=[("ps", b)])
                        S.op("act", lambda: A.activation(
                            out=vg[:, cbk * 512:(cbk + 1) * 512], in_=ps[:, b, :], func=AF.Gelu),
                            reads=[("ps", b)], writes=[vgk], sub=cbk)
                for (vg, vgk, vgh, ss, ssk, ssh, slot) in vgs:
                    S.op("dve", lambda: V.memset(ss[:], 0.0), writes=[ssk])
                    S.op("dve", lambda: V.scalar_tensor_tensor(
                        out=zp_tm[:, slot, :], in0=vg[:], scalar=1.0, in1=vg[:], op0=ALU.mult, op1=ALU.mult,
                        accum_out=ss[:, 0:1]), reads=[vgk, ssk], writes=[("zp", slot), ssk])
                for (vg, vgk, vgh, ss, ssk, ssh, slot) in vgs:
                    S.op("act", lambda: A.activation(
                        out=ss[:, 1:2], in_=ss[:, 0:1], func=AF.Sqrt, bias=cst[:, C_EPS:C_EPS + 1],
                        scale=1.0 / 1024.0), reads=[ssk, "cst"], writes=[ssk])
                for (vg, vgk, vgh, ss, ssk, ssh, slot) in vgs:
                    S.op("dve", lambda: V.reciprocal(out=ss[:, 1:2], in_=ss[:, 1:2]), reads=[ssk], writes=[ssk])
                for (vg, vgk, vgh, ss, ssk, ssh, slot) in vgs:
                    S.op("act", lambda: A.activation(out=vg[:], in_=vg[:], func=AF.Identity, scale=ss[:, 1:2]),
                         reads=[vgk, ssk], writes=[vgk])
                for (vg, vgk, vgh, ss, ssk, ssh, slot) in vgs:
                    S.op("pool", lambda: G.tensor_tensor(
                        out=zp_tm[:, slot, :], in0=vg[:], in1=gsgu[:, l, :], op=ALU.mult),
                        reads=[vgk, "gsgu"], writes=[("zp", slot)])
                    vgp.put(vgh)
                    ssp.put(ssh)
                if is_last:
                    ctl["release"]("Wv")
                for fp in range(4):
                    b = bank()
                    S.mm([(ps[:, b, q * n:(q + 1) * n], Wu.ap[:, k, (fp * 2 + q) * 128:(fp * 2 + q + 1) * 128],
                           hT[:, k, col0:col0 + n], k == 0, k == 7) for q in range(2) for k in range(8)],
                         reads=ck("h", col0, n) + Wu.keys, writes=[("ps", b)])
                    S.op("act", lambda b=b, fp=fp: A.activation(
                        out=ubuf[:, 2 * fp:2 * fp + 2, 0:n], in_=q2(ps[:, b, 0:2 * n]), func=AF.Gelu),
                        reads=[("ps", b)], writes=["u"], sub=fp)
                if is_last:
                    ctl["release"]("Wu")
                gt, gtk, gth = bigp.get()
                for ci in range(nch):
                    slot = vslots[ci]
                    for hh in range(2):
                        b = bank()
                        grp = []
                        for jj in range(4):
                            h = hh * 4 + jj
                            o = ps[:, b, jj * 128:(jj + 1) * 128]
                            grp.append((o, zp_tm[:, slot, h * 128:(h + 1) * 128], wsb[:, l, h, :], True, False))
                            grp.append((o, one33, brow[0:33, l, h * 128:(h + 1) * 128], False, True))
                        S.mm(grp, reads=[("zp", slot), "wsb", "brow", "cb"], writes=[("ps", b)])
                        S.op("dve", lambda b=b, hh=hh, ci=ci: V.tensor_tensor(
                            out=gt[:, hh * 4:(hh + 1) * 4, ci * 128:(ci + 1) * 128],
                            in0=ps[:, b, :].rearrange("p (j t) -> p j t", j=4),
                            in1=ubuf[:, hh * 4:(hh + 1) * 4, ci * 128:(ci + 1) * 128], op=ALU.mult),
                            reads=[("ps", b), "u"], writes=[gtk], sub=(ci, hh))
                gsc = {}
                for djp in range(4):
                    bg = bank()
                    S.mm([(ps[:, bg, q * n:(q + 1) * n], Wgb.ap[:, k, (djp * 2 + q) * 128:(djp * 2 + q + 1) * 128],
                           hT[:, k, col0:col0 + n], k == 0, k == 7) for q in range(2) for k in range(8)],
                         reads=ck("h", col0, n) + Wgb.keys, writes=[("ps", bg)])
                    sc, sck, sch = scr.get()
                    S.op("act", lambda bg=bg, sc=sc: A.activation(out=sc[:, :, 0:n], in_=q2(ps[:, bg, 0:2 * n]),
                                                                  func=AF.Sigmoid),
                         reads=[("ps", bg)], writes=[sck])
                    gsc[djp] = (sc, sck, sch)
                for djp in range(4):
                    sc, sck, sch = gsc[djp]
                    by = bank()
                    S.mm([(ps[:, by, q * n:(q + 1) * n], wb.ap[:, k, (djp * 2 + q) * 128:(djp * 2 + q + 1) * 128],
                           gt[:, k, 0:n], k == 0, k == 7) for q in range(2) for k in range(8)],
                         reads=[gtk] + wb.keys, writes=[("ps", by)])
                    S.op("dve", lambda by=by, sc=sc: V.tensor_tensor(
                        out=sc[:, :, 0:n], in0=sc[:, :, 0:n], in1=q2(ps[:, by, 0:2 * n]), op=ALU.mult),
                        reads=[sck, ("ps", by)], writes=[sck])
                    S.op("pool", lambda sc=sc, djp=djp: G.tensor_tensor(
                        out=mixa[:, 2 * djp:2 * djp + 2, col0:col0 + n], in0=sc[:, :, 0:n],
                        in1=mixa[:, 2 * djp:2 * djp + 2, col0:col0 + n], op=ALU.add),
                        reads=[sck] + ck("ma", col0, n), writes=ck("ma", col0, n), sub=djp)
                    scr.put(sch)
                bigp.put(gth)

            def pb_body(U):
                for i, (col0, n) in enumerate(tiles):
                    pb_tile(U, col0, n, i == NT - 1)

            add_phase([("Wv", kp(w_in, l, 2048, 3072), [8, 1024]),
                       ("Wu", kp(w_in, l, 1024, 2048), [8, 1024]),
                       ("Wgb", kp(w_in, l, 4096, 5120), [8, 1024]),
                       ("wb", kp(w_b, l, 0, 1024), [8, 1024])], pb_body)

            def pc_body(U):
                wo = U["wo"]
                for (col0, n) in tiles:
                    S.dma("pool", pTs[:, :, col0:col0 + n],
                          pin[l, :, :, base + 256 + col0: base + 256 + col0 + n],
                          writes=ck("pT", col0, n))

                def pc(col0, n):
                    for djp in range(4):
                        b = bank()
                        S.mm([(ps[:, b, q * n:(q + 1) * n], wo.ap[:, k, (djp * 2 + q) * 128:(djp * 2 + q + 1) * 128],
                               mixa[:, k, col0:col0 + n], k == 0, k == 7) for q in range(2) for k in range(8)],
                             reads=ck("ma", col0, n) + wo.keys, writes=[("ps", b)])
                        S.op("dve", lambda b=b, djp=djp: V.tensor_tensor(
                            out=xT[:, 2 * djp:2 * djp + 2, col0:col0 + n],
                            in0=xT[:, 2 * djp:2 * djp + 2, col0:col0 + n],
                            in1=q2(ps[:, b, 0:2 * n]), op=ALU.add),
                            reads=[("ps", b)] + ck("x", col0, n), writes=ck("x", col0, n), sub=djp)

                def nrm(col0, n):
                    emit_norm(cl + C_FFNN, col0, n, vcol_of(col0))

                pc(*tiles[0])
                for i in range(NT):
                    if i + 1 < NT:
                        pc(*tiles[i + 1])
                    nrm(*tiles[i])

            add_phase([("wo", kp(w_out, l, 0, 1024), [8, 1024])], pc_body)

            def make_ffn(bi, j0, cbn):
                last_block = (bi == len(FFN_BLOCKS) - 1)

                def up(U, col0, n, tpar):
                    upA, upB = U["upA"], U["upB"]
                    at, atk, ath = atp.get()
                    accs = {}

                    def stA(jl):
                        j = j0 + jl
                        b = bank()
                        grp = [(ps[:, b, 0:n], upA.ap[:, k, jl * 128:(jl + 1) * 128], hT[:, k, col0:col0 + n],
                                k == 0, k == 7) for k in range(8)]
                        grp += [(ps[:, b, n:2 * n], upB.ap[:, k, jl * 128:(jl + 1) * 128], hT[:, k, col0:col0 + n],
                                 k == 0, k == 7) for k in range(8)]
                        S.mm(grp, reads=ck("h", col0, n) + upA.keys + upB.keys, writes=[("ps", b)])
                        acc, acck, acch = scr.get()
                        accs[jl] = (acc, acck, acch, b)
                        S.op("dve", lambda: V.memset(acc[:, :, 0:1], 0.0), writes=[acck, (acck, 0), (acck, 1)],
                             sub="m0")
                        S.op("dve", lambda: V.memset(acc[:, :, n + 1:n + 2], 0.0), writes=[(acck, 0), (acck, 1)],
                             sub="m1")
                        for q, f in ((0, j), (1, NFF + j)):
                            w1 = cst[:, cl + C_CW + 44 + f:cl + C_CW + 44 + f + 1]
                            bb = cst[:, cl + C_CB + f:cl + C_CB + f + 1]
                            S.op("act", lambda: A.activation(
                                out=acc[:, q, 1:n + 1], in_=ps[:, b, q * n:q * n + n], func=AF.Identity,
                                scale=w1, bias=bb),
                                reads=[("ps", b), "cst"], writes=[(acck, q)])
                        S.op("pool", lambda: G.tensor_tensor(
                            out=acc[:, :, 0:2], in0=acc[:, :, 0:2],
                            in1=upc[:, l, tpar, j, :].rearrange("p (q t) -> p q t", q=2), op=ALU.add),
                            reads=[("upc", l, tpar, j), (acck, 0), (acck, 1)], writes=[(acck, 0), (acck, 1)],
                            sub="head")

                    def stB(jl):
                        j = j0 + jl
                        acc, acck, acch, b = accs[jl]
                        for q, f in ((0, j), (1, NFF + j)):
                            w2 = cst[:, cl + C_CW + 88 + f:cl + C_CW + 88 + f + 1]
                            S.op("dve", lambda: V.scalar_tensor_tensor(
                                out=acc[:, q, 0:n], in0=ps[:, b, q * n:q * n + n], scalar=w2,
                                in1=acc[:, q, 0:n], op0=ALU.mult, op1=ALU.add),
                                reads=[("ps", b), (acck, q), "cst"], writes=[(acck, q)])
                        for q, f in ((0, j), (1, NFF + j)):
                            w0 = cst[:, cl + C_CW + f:cl + C_CW + f + 1]
                            S.op("dve", lambda: V.scalar_tensor_tensor(
                                out=acc[:, q, 2:n + 2], in0=ps[:, b, q * n:q * n + n], scalar=w0,
                                in1=acc[:, q, 2:n + 2], op0=ALU.mult, op1=ALU.add),
                                reads=[("ps", b), (acck, q), "cst"], writes=[(acck, q)])
                        if "copy" in SKIP:
                            pass
                        elif os.environ.get("KV_ACTCOPY"):
                            S.op("act", lambda: A.copy(
                                out=upc[:, l, 1 - tpar, j, :].rearrange("p (q t) -> p q t", q=2), in_=acc[:, :, n:n + 2]),
                                reads=[(acck, 0), (acck, 1)], writes=[("upc", l, 1 - tpar, j)], sub="tail")
                        else:
                            S.op("pool", lambda: G.tensor_copy(
                                out=upc[:, l, 1 - tpar, j, :].rearrange("p (q t) -> p q t", q=2), in_=acc[:, :, n:n + 2]),
                                reads=[(acck, 0), (acck, 1)], writes=[("upc", l, 1 - tpar, j)], sub="tail")

                    def stC(jl):
                        acc, acck, acch, b = accs[jl]
                        S.op("act", lambda: A.activation(out=acc[:, 0, 0:n], in_=acc[:, 0, 0:n], func=AF.Gelu),
                             reads=[(acck, 0)], writes=[(acck, 0)])
                        S.op("pool", lambda: G.tensor_tensor(
                            out=at[:, jl, 0:n], in0=acc[:, 0, 0:n], in1=acc[:, 1, 0:n], op=ALU.mult),
                            reads=[acck, (acck, 0), (acck, 1)], writes=[atk])
                        scr.put(acch)

                    for step in range(cbn + 2):
                        if step < cbn:
                            stA(step)
                        if 0 <= step - 1 < cbn:
                            stB(step - 1)
                        if 0 <= step - 2 < cbn:
                            stC(step - 2)
                    return (at, atk, ath)

                def down(U, col0, n, atr):
                    dwn = U["dwn"]
                    at, atk, ath = atr
                    for djp in range(4):
                        b = bank()
                        S.mm([(ps[:, b, q * n:(q + 1) * n], dwn.ap[:, jl, (djp * 2 + q) * 128:(djp * 2 + q + 1) * 128],
                               at[:, jl, 0:n], jl == 0, jl == cbn - 1) for q in range(2) for jl in range(cbn)],
                             reads=[atk] + dwn.keys, writes=[("ps", b)])
                        S.op("dve", lambda b=b, djp=djp: V.tensor_tensor(
                            out=xT[:, 2 * djp:2 * djp + 2, col0:col0 + n],
                            in0=xT[:, 2 * djp:2 * djp + 2, col0:col0 + n],
                            in1=q2(ps[:, b, 0:2 * n]), op=ALU.add),
                            reads=[("ps", b)] + ck("x", col0, n), writes=ck("x", col0, n), sub=djp)
                    atp.put(ath)

                def body(U):
                    ats = {}
                    ats[0] = up(U, tiles[0][0], tiles[0][1], (tbase + 0) % 2)
                    for i in range(NT):
                        if i + 1 < NT:
                            ats[i + 1] = up(U, tiles[i + 1][0], tiles[i + 1][1], (tbase + i + 1) % 2)
                            if i + 1 == NT - 1:
                                ctl["release"]("upA")
                                ctl["release"]("upB")
                        if last_block and i == 1:
                            emit_norm(cl + C_PLEN, tiles[0][0], tiles[0][1], vcol_of(tiles[0][0]))
                            st["ple_norm_done"].add(tiles[0])
                        down(U, tiles[i][0], tiles[i][1], ats[i])

                add_phase([("upA", kp(w_up, l, j0 * 128, (j0 + cbn) * 128), [8, cbn * 128]),
                           ("upB", kp(w_up, l, DFF + j0 * 128, DFF + (j0 + cbn) * 128), [8, cbn * 128]),
                           ("dwn", w_down[l, j0 * 128:(j0 + cbn) * 128, :].rearrange("(j p) c -> p j c", p=128),
                            [cbn, 1024])], body)

            for bi, (j0, cbn) in enumerate(FFN_BLOCKS):
                make_ffn(bi, j0, cbn)

            def ple_body(U):
                wpg, wpl = U["wpg"], U["wpl"]
                pend = []

                def finalize(col0, n):
                    if base + col0 >= 0:
                        emit_norm(C_FIN, col0, n, None, out_f32=ubuf)
                        oc = base + col0
                        out_tickets.append(S.dma("sp", out[:, :, oc:oc + n], ubuf[:, :, 0:n], reads=["u"]))
                    if s + 1 < 3:
                        c0 = (col0 // 256) * 256
                        nb = bases[s + 1]
                        S.dma("sp", xT[:, :, c0:c0 + 256], xin[:, :, nb + 256 + c0: nb + 256 + c0 + 256],
                              writes=ck("x", c0, 256))

                for ti, (col0, n) in enumerate(tiles):
                    for tj in (ti, ti + 1):
                        if tj < NT and tiles[tj] not in st["ple_norm_done"]:
                            emit_norm(cl + C_PLEN, tiles[tj][0], tiles[tj][1], vcol_of(tiles[tj][0]))
                            st["ple_norm_done"].add(tiles[tj])
                    for djp in range(4):
                        bg = bank()
                        S.mm([(ps[:, bg, q * n:(q + 1) * n], wpg.ap[:, k, (djp * 2 + q) * 128:(djp * 2 + q + 1) * 128],
                               hT[:, k, col0:col0 + n], k == 0, k == 7) for q in range(2) for k in range(8)],
                             reads=ck("h", col0, n) + wpg.keys, writes=[("ps", bg)])
                        sc, sck, sch = scr.get()
                        S.op("act", lambda: A.activation(out=sc[:, :, 0:n], in_=q2(ps[:, bg, 0:2 * n]),
                                                         func=AF.Sigmoid),
                             reads=[("ps", bg)], writes=[sck])
                        be = bank()
                        S.mm([(ps[:, be, q * n:(q + 1) * n], wpl.ap[:, kk, (djp * 2 + q) * 128:(djp * 2 + q + 1) * 128],
                               pTs[:, kk, col0:col0 + n], kk == 0, kk == 1) for q in range(2) for kk in range(2)],
                             reads=ck("pT", col0, n) + wpl.keys, writes=[("ps", be)])
                        S.op("dve", lambda: V.tensor_tensor(
                            out=sc[:, :, 0:n], in0=sc[:, :, 0:n], in1=q2(ps[:, be, 0:2 * n]), op=ALU.mult),
                            reads=[sck, ("ps", be)], writes=[sck])
                        S.op("pool", lambda: G.tensor_tensor(
                            out=xT[:, 2 * djp:2 * djp + 2, col0:col0 + n],
                            in0=xT[:, 2 * djp:2 * djp + 2, col0:col0 + n], in1=sc[:, :, 0:n], op=ALU.add),
                            reads=[sck] + ck("x", col0, n), writes=ck("x", col0, n), sub=djp)
                        scr.put(sch)
                    if l == 1:
                        if pend:
                            finalize(*pend.pop())
                        pend.append((col0, n))
                while pend:
                    finalize(*pend.pop())

            add_phase([("wpg", kp(w_pg, l, 0, 1024), [8, 1024]),
                       ("wpl", kp(w_ple, l, 0, 1024), [2, 1024])], ple_body)

        def x_load_phase(s):
            base = bases[s]

            def body(U):
                if s > 0:
                    return
                for t3 in range(3):
                    S.dma("sp", xT[:, :, t3 * 256:(t3 + 1) * 256],
                          xin[:, :, base + 256 + t3 * 256: base + 256 + (t3 + 1) * 256],
                          writes=ck("x", t3 * 256, 256))
            add_phase([], body)

        for s in range(3):
            x_load_phase(s)
            for l in range(2):
                make_sl(s, l)

        ring = {"ptr": 0}
        live = {}
        loaded = {}
        regions = {}

        def try_load(pi, j):
            name, src, dims = phases[pi][0][j]
            size = 1
            for d_ in dims:
                size *= d_
            assert size % 1024 == 0
            npg = size // 1024
            if os.environ.get("KV_FIFO"):
                p0 = ring["ptr"]
                if p0 + npg > ring_pages:
                    p0 = 0
                for regs in live.values():
                    for (a, b_) in regs:
                        if not (p0 + npg <= a or p0 >= b_):
                            return False
                ring["ptr"] = p0 + npg
            else:
                occ = [r for regs in live.values() for r in regs]
                if npg >= 6:
                    cands = list(range(0, ring_pages - npg + 1, npg))
                else:
                    cands = list(range(ring_pages - npg, -1, -1))
                p0 = None
                for c in cands:
                    if all(c + npg <= a or c >= b_ for (a, b_) in occ):
                        p0 = c
                        break
                if p0 is None:
                    return False
            live.setdefault(pi, []).append((p0, p0 + npg))
            regions[(pi, j)] = (p0, p0 + npg)
            if os.environ.get("KV_DBG"):
                print("LOAD phase", pi, name, "pages", p0, p0 + npg, "at_phase", cur_phase[0])
            flat = wr[:, p0 * 1024:p0 * 1024 + size]
            if len(dims) == 2:
                ap = flat.rearrange("p (a b) -> p a b", a=dims[0])
            else:
                ap = flat.rearrange("p (a b c) -> p a b c", a=dims[0], b=dims[1])
            keys = [("wr", pg) for pg in range(p0, p0 + npg)]
            S.dma("pool", ap, src, reads=(), writes=keys)
            loaded[(pi, j)] = Unit(ap, keys)
            return True

        cur_phase = [0]

        def prefetch_from(pi):
            for pj in range(pi + 1, min(pi + 5, len(phases))):
                for j in range(len(phases[pj][0])):
                    if (pj, j) not in loaded:
                        if not try_load(pj, j):
                            return

        def release_unit(name):
            pi = cur_phase[0]
            units = phases[pi][0]
            j = [u[0] for u in units].index(name)
            live[pi].remove(regions[(pi, j)])
            prefetch_from(pi)

        ctl["release"] = release_unit
        for pi in range(len(phases)):
            cur_phase[0] = pi
            units, body = phases[pi]
            for j in range(len(units)):
                if (pi, j) not in loaded:
                    ok = try_load(pi, j)
                    assert ok, "weight ring too small for phase %d" % pi
            prefetch_from(pi)
            body({units[j][0]: loaded[(pi, j)] for j in range(len(units))})
            live.pop(pi, None)

        for t in out_tickets:
            S._wait("sp", [t])
    return nc


_CACHE = {}


def _pool_mats(first):
    Bcur = np.zeros((4, 128, 128), np.float32)
    Bprev = np.zeros((4, 128, 128), np.float32)
    for g, w in enumerate((2, 4, 8, 16)):
        for t in range(128):
            cnt = min(t + 1, w) if first else w
            for sg in range(t - w + 1, t + 1):
                if sg >= 0:
                    Bcur[g, sg, t] += 1.0 / cnt
                elif not first:
                    Bprev[g, sg + 128, t] += 1.0 / cnt
            Bcur[g, t, t] -= 1.0
    return Bcur, Bprev


def kernel(x, p, mix_norm, w_in, w_pool, pool_scale, sgu_norm, w_spatial, b_spatial,
           w_branch_a, w_branch_b, w_out, ffn_norm, w_up, conv_w, conv_b, w_down,
           ple_norm, w_ple_gate, w_ple, final_norm):
    f = np.float32
    x = np.asarray(x, f)
    p = np.asarray(p, f)
    if "nc" not in _CACHE:
        _CACHE["nc"] = build_program()
    nc = _CACHE["nc"]

    def vec8(v):
        return np.asarray(v, f).reshape(8, 128).T

    Bcur, Bprev = _pool_mats(False)
    Bfirst, _ = _pool_mats(True)
    mask = (np.arange(128)[None, :] >= np.arange(128)[:, None]).astype(f)
    gsgu = np.ascontiguousarray(np.broadcast_to(np.asarray(sgu_norm, f)[None, :, :], (128, 2, 1024)))
    bTt = np.ascontiguousarray(np.broadcast_to(np.asarray(b_spatial, f).reshape(1, 2, 1024), (128, 2, 1024)))
    wsT = np.ascontiguousarray(np.transpose(np.asarray(w_spatial, f), (3, 0, 1, 2)).reshape(128, 2, 1024))
    shared = {
        "w_in": np.asarray(w_in, f), "w_pool": np.asarray(w_pool, f),
        "w_branch_a": np.asarray(w_branch_a, f), "w_branch_b": np.asarray(w_branch_b, f),
        "w_out": np.asarray(w_out, f), "w_up": np.asarray(w_up, f), "w_down": np.asarray(w_down, f),
        "w_ple_gate": np.asarray(w_ple_gate, f), "w_ple": np.asarray(w_ple, f),
        "gsgu": gsgu, "bT": bTt, "wsT": wsT, "mask": mask,
    }
    cst0 = np.zeros((128, NCST), f)
    for l in range(2):
        o = l * LW
        cst0[:, o + C_MIXN:o + C_MIXN + 8] = vec8(mix_norm[l])
        cst0[:, o + C_FFNN:o + C_FFNN + 8] = vec8(ffn_norm[l])
        cst0[:, o + C_PLEN:o + C_PLEN + 8] = vec8(ple_norm[l])
        cst0[:, o + C_PSC:o + C_PSC + 8] = vec8(pool_scale[l])
        for k in range(3):
            cst0[:, o + C_CW + k * 44:o + C_CW + (k + 1) * 44] = np.asarray(conv_w[l][k], f).reshape(44, 128).T
        cst0[:, o + C_CB:o + C_CB + 44] = np.asarray(conv_b[l], f).reshape(44, 128).T
    cst0[:, C_FIN:C_FIN + 8] = vec8(final_norm)
    cst0[:, C_EPS:C_EPS + 8] = EPS

    in_maps = []
    for core in range(8):
        b = core // 4
        t0 = (core % 4) * TOK
        xs = np.zeros((TIN, D), f)
        ps_ = np.zeros((2, TIN, 256), f)
        lo = t0 - HALO
        if lo >= 0:
            xs[:] = x[b, lo:t0 + TOK]
            ps_[:] = p[:, b, lo:t0 + TOK]
        else:
            xs[HALO:] = x[b, 0:TOK]
            ps_[:, HALO:] = p[:, b, 0:TOK]
        xTc = np.ascontiguousarray(xs.T.reshape(8, 128, TIN).transpose(1, 0, 2))
        pTc = np.ascontiguousarray(ps_.transpose(0, 2, 1).reshape(2, 2, 128, TIN).transpose(0, 2, 1, 3))
        cstc = cst0.copy()
        cstc[:, C_VALID:C_VALID + 256] = 1.0 if lo >= 0 else 0.0
        cbc = np.zeros((128, NCSTB), f)
        cbc[:, CB_ONES:CB_ONES + 128] = 1.0 / 1024.0
        cbc[:, CB_ONE1:CB_ONE1 + 128] = 1.0
        for g in range(4):
            cbc[:, CB_BCUR + g * 128:CB_BCUR + (g + 1) * 128] = Bcur[g]
            cbc[:, CB_BPREV + g * 128:CB_BPREV + (g + 1) * 128] = Bprev[g]
            cbc[:, CB_BFIRST + g * 128:CB_BFIRST + (g + 1) * 128] = (Bfirst[g] if lo < 0 else Bcur[g])
        m = dict(shared)
        m.update({"xT": xTc, "pT": pTc, "cst": cstc, "cstb": cbc})
        in_maps.append(m)

    res = run_bass_kernel_spmd(nc, in_maps, core_ids=list(range(8)))
    outp = np.zeros((2, SEQ, D), f)
    for core in range(8):
        b = core // 4
        t0 = (core % 4) * TOK
        oT = np.asarray(res.results[core]["outT"], f)
        outp[b, t0:t0 + TOK] = oT.transpose(2, 1, 0).reshape(TOK, D)
    return outp
```

```python
import os
import numpy as np
from contextlib import ExitStack
import concourse.bass as bass
import concourse.mybir as mybir
from concourse.bass_utils import run_bass_kernel_spmd

F32 = mybir.dt.float32
BF16 = mybir.dt.bfloat16
AF = mybir.ActivationFunctionType
ALU = mybir.AluOpType

D = 1024
DFF = 2816
NFF = 22
SEQ = 8192
TOK = 2048
HALO = 256
TIN = TOK + HALO
TS = 768
EPS = 1e-6
LW = 208
C_MIXN, C_FFNN, C_PLEN, C_PSC, C_CW, C_CB = 0, 8, 16, 24, 32, 164
C_FIN = 2 * LW
C_VALID = C_FIN + 8
C_EPS = C_VALID + 256
NCST = C_EPS + 8
CB_ONES, CB_BCUR, CB_BPREV, CB_BFIRST = 0, 128, 640, 1152
CB_ONE1 = 1664
NCSTB = 1792
ND = 8
SKIP = os.environ.get('KV_SKIP', '')
FFN_BLOCKS = [(0, 6), (6, 6), (12, 6), (18, 4)]


class Sched:
    def __init__(self, nc, es):
        self.nc = nc
        self.engs = {"pe": nc.tensor, "act": nc.scalar, "dve": nc.vector,
                     "pool": nc.gpsimd, "sp": nc.sync}
        self.sem = {}
        self.cnt = {}
        for e in self.engs:
            self.sem[e] = es.enter_context(nc.semaphore("s_" + e))
            self.cnt[e] = 0
        self.dsem = {}
        self.dcnt = {}
        self.drr = {}
        for q in ("pool", "sp"):
            self.dsem[q] = [es.enter_context(nc.semaphore("d_%s%d" % (q, i))) for i in range(ND)]
            self.dcnt[q] = [0] * ND
            self.drr[q] = 0
        self.seen = {e: {} for e in self.engs}
        self.lastw = {}
        self.readers = {}
        self.lastw_sub = {}
        self.readers_sub = {}
        self.nops = 0

    def _same(self, e, t, sub, tsub):
        if t[3] < self.cnt[e] - 3:
            return False
        if sub is not None and tsub is not None and sub != tsub:
            return False
        return True

    def _deps(self, e, reads, writes, sub=None):
        deps = []
        for k in reads:
            w = self.lastw.get(k)
            if w is not None:
                if w[0] != e or w[0] == "dma":
                    deps.append(w)
                elif self._same(e, w, sub, self.lastw_sub.get(k)):
                    deps.append(w)
            if isinstance(k, tuple) and k[0] == "ps":
                for r in self.readers.get(k, {}).values():
                    if r[0] != e:
                        deps.append(r)
        for k in writes:
            w = self.lastw.get(k)
            if w is not None:
                if w[0] != e or w[0] == "dma":
                    deps.append(w)
                elif self._same(e, w, sub, self.lastw_sub.get(k)):
                    deps.append(w)
            for sk, r in self.readers.get(k, {}).items():
                if r[0] != e or r[0] == "dma":
                    deps.append(r)
                elif self._same(e, r, sub, self.readers_sub.get(k, {}).get(sk)):
                    deps.append(r)
        return deps

    def _wait(self, e, deps):
        best = {}
        for d in deps:
            if d[2] not in best or best[d[2]][3] < d[3]:
                best[d[2]] = d
        for d in best.values():
            if self.seen[e].get(d[2], 0) >= d[3]:
                continue
            self.engs[e].wait_ge(d[1], d[3])
            self.seen[e][d[2]] = d[3]

    def _reg(self, t, reads, writes, sub=None):
        for k in reads:
            self.readers.setdefault(k, {})[t[2]] = t
            self.readers_sub.setdefault(k, {})[t[2]] = sub
        for k in writes:
            self.lastw[k] = t
            self.lastw_sub[k] = sub
            self.readers[k] = {}
            self.readers_sub[k] = {}

    def op(self, e, fn, reads=(), writes=(), sub=None):
        self._wait(e, self._deps(e, reads, writes, sub))
        ins = fn()
        self.cnt[e] += 1
        ins.then_inc(self.sem[e], 1)
        t = (e, self.sem[e], "s_" + e, self.cnt[e])
        self._reg(t, reads, writes, sub)
        self.nops += 1
        return t

    def mm(self, grp, reads=(), writes=()):
        e = "pe"
        self._wait(e, self._deps(e, reads, writes))
        ins = None
        for (o, l, r, st, sp) in grp:
            ins = self.nc.tensor.matmul(o, l, r, start=st, stop=sp)
            self.nops += 1
        self.cnt[e] += 1
        ins.then_inc(self.sem[e], 1)
        t = (e, self.sem[e], "s_" + e, self.cnt[e])
        self._reg(t, reads, writes)
        return t

    def dma(self, q, out, in_, reads=(), writes=()):
        i = self.drr[q]
        self.drr[q] = (i + 1) % ND
        sem = self.dsem[q][i]
        key = "d_%s%d" % (q, i)
        deps = self._deps("dma", reads, writes)
        if self.dcnt[q][i] > 0:
            deps.append(("dma", sem, key, self.dcnt[q][i]))
        self._wait(q, deps)
        ins = self.engs[q].dma_start(out=out, in_=in_)
        self.dcnt[q][i] += 16
        ins.then_inc(sem, 16)
        t = ("dma", sem, key, self.dcnt[q][i])
        self._reg(t, reads, writes)
        self.nops += 1
        return t


class BufPool:
    def __init__(self, bufs, name):
        self.bufs = bufs
        self.name = name
        self.free = list(range(len(bufs)))

    def get(self):
        assert self.free, "pool %s exhausted" % self.name
        j = self.free.pop(0)
        return self.bufs[j], (self.name, j), j

    def put(self, j):
        assert j not in self.free
        self.free.append(j)


class Unit:
    def __init__(self, ap, keys):
        self.ap = ap
        self.keys = keys


def build_program():
    nc = bass.Bass("TRN2", target_bir_lowering=False)

    def dram(name, shape, kind="ExternalInput"):
        return nc.dram_tensor(name, shape, F32, kind=kind).ap()

    xin = dram("xT", [128, 8, TIN])
    pin = dram("pT", [2, 128, 2, TIN])
    w_in = dram("w_in", [2, 1024, 5120])
    w_pool = dram("w_pool", [2, 4, 256, 256])
    w_a = dram("w_branch_a", [2, 1024, 1024])
    w_b = dram("w_branch_b", [2, 1024, 1024])
    w_out = dram("w_out", [2, 1024, 1024])
    w_up = dram("w_up", [2, 1024, 2 * DFF])
    w_down = dram("w_down", [2, DFF, 1024])
    w_pg = dram("w_ple_gate", [2, 1024, 1024])
    w_ple = dram("w_ple", [2, 256, 1024])
    cst_in = dram("cst", [128, NCST])
    cstb_in = dram("cstb", [128, NCSTB])
    gsgu_in = dram("gsgu", [128, 2, 1024])
    bT_in = dram("bT", [128, 2, 1024])
    wsT_in = dram("wsT", [128, 2, 1024])
    mask_in = dram("mask", [128, 128])
    out = dram("outT", [128, 8, TOK], kind="ExternalOutput")

    with ExitStack() as es:
        S = Sched(nc, es)
        V = nc.vector
        G = nc.gpsimd
        A = nc.scalar

        def sb(name, shape, dt):
            return es.enter_context(nc.sbuf_tensor(name, shape, dt))

        xT = sb("xT_sb", [128, 8, TS], F32)
        hT = sb("hT_sb", [128, 8, TS], BF16)
        mixa = sb("mixa_sb", [128, 8, TS], BF16)
        cst = sb("cst_sb", [128, NCST], F32)
        cb = sb("cb_sb", [128, NCSTB], BF16)
        mask = sb("mask_sb", [128, 128], F32)
        gsgu = sb("gsgu_sb", [128, 2, 1024], F32)
        brow = sb("brow_sb", [128, 2, 1024], BF16)
        wsb = sb("wsb_sb", [128, 2, 8, 128], BF16)
        zpc = sb("zpc_sb", [128, 2, 1024], BF16)
        upc = sb("upc_sb", [128, 2, 2, NFF, 4], F32)
        bigp = BufPool([sb("pa%d" % i, [128, 8, 256], BF16) for i in range(3)], "pa")
        rsp = BufPool([sb("rs%d" % i, [128, 256], F32) for i in range(2)], "rs")
        zp_tm = sb("zp_tm", [128, 3, 1024], BF16)
        vgp = BufPool([sb("vg%d" % i, [128, 1024], F32) for i in range(2)], "vg")
        scr = BufPool([sb("sc%d" % i, [128, 2, 264], F32) for i in range(6)], "sc")
        ubuf = sb("u_sb", [128, 8, 256], F32)
        atp = BufPool([sb("at%d" % i, [128, 6, 256], BF16) for i in range(2)], "at")
        pTs = sb("pT_sb", [128, 2, TS], BF16)
        ssp = BufPool([sb("ss%d" % i, [128, 2], F32) for i in range(4)], "ss")
        ps = es.enter_context(nc.psum_tensor("ps", [128, 8, 512], F32))

        remaining = int(nc.sbuf_bytes_remaining)
        ring_pages = (remaining - 512) // 2048
        assert ring_pages >= 34, ring_pages
        wr = sb("wring", [128, ring_pages * 1024], BF16)

        bank_i = [0]

        def bank():
            b = bank_i[0] % 8
            bank_i[0] += 1
            return b

        def kp(w, l, c0, c1):
            return w[l, :, c0:c1].rearrange("(k p) c -> p k c", p=128)

        def ck(prefix, col0, n):
            return [(prefix, c) for c in range(col0 // 128, (col0 + n) // 128)]

        def q2(ap):
            return ap.rearrange("p (q t) -> p q t", q=2)

        for t3 in range(3):
            S.dma("sp", xT[:, :, t3 * 256:(t3 + 1) * 256], xin[:, :, t3 * 256:(t3 + 1) * 256],
                  writes=ck("x", t3 * 256, 256))
        S.dma("sp", cst[:], cst_in, writes=["cst"])
        S.dma("sp", mask[:], mask_in, writes=["mask"])
        S.dma("sp", gsgu[:], gsgu_in, writes=["gsgu"])
        S.dma("pool", cb[:], cstb_in, writes=["cb"])
        uflat = ubuf[:].rearrange("p a b -> p (a b)")
        S.dma("sp", uflat, wsT_in.rearrange("p l c -> p (l c)"), writes=["u"])
        for l in range(2):
            for h in range(8):
                S.op("dve", lambda l=l, h=h: V.tensor_tensor(
                    out=wsb[:, l, h, :], in0=uflat[:, l * 1024 + h * 128:l * 1024 + (h + 1) * 128],
                    in1=mask[:], op=ALU.mult), reads=["u", "mask"], writes=["wsb"])
        browf = brow[:].rearrange("p l c -> p (l c)")
        S.dma("sp", uflat, bT_in.rearrange("p l c -> p (l c)"), reads=["wsb"], writes=["u"])
        S.op("dve", lambda: V.memset(browf[0:64, :], 0.0), writes=["brow"])
        S.op("dve", lambda: V.tensor_copy(out=browf[0:1, :], in_=uflat[0:1, :]), reads=["u"], writes=["brow"])
        btmp = zp_tm[:, 0:2, :].rearrange("p a b -> p (a b)")
        S.op("dve", lambda: V.tensor_copy(out=btmp[32:33, :], in_=uflat[32:33, :]), reads=["u"],
             writes=[("zp", 0), ("zp", 1)])
        S.op("dve", lambda: V.tensor_tensor(out=uflat[32:33, :], in0=uflat[32:33, :], in1=btmp[32:33, :],
                                            op=ALU.subtract), reads=["u", ("zp", 0), ("zp", 1)], writes=["u"])
        S.op("dve", lambda: V.tensor_copy(out=browf[32:33, :], in_=uflat[32:33, :]), reads=["u"], writes=["brow"])
        S.op("dve", lambda: V.memset(zpc[:], 0.0), writes=[("zpc", 0), ("zpc", 1)])
        S.op("dve", lambda: V.memset(upc[:], 0.0),
             writes=[("upc", l, pr, j) for l in range(2) for pr in range(2) for j in range(NFF)])

        ones = cb[:, CB_ONES:CB_ONES + 128]
        one33 = cb[0:33, CB_ONE1:CB_ONE1 + 128]

        def Bm(off, g):
            return cb[:, off + g * 128: off + (g + 1) * 128]

        def emit_norm(gcol, col0, n, vcol=None, out_f32=None):
            sq, sqk, sqh = bigp.get()
            S.op("act", lambda: A.activation(out=sq[:, :, 0:n], in_=xT[:, :, col0:col0 + n], func=AF.Square),
                 reads=ck("x", col0, n), writes=[sqk])
            b = bank()
            S.mm([(ps[:, b, 0:n], ones, sq[:, k, 0:n], k == 0, k == 7) for k in range(8)],
                 reads=[sqk, "cb"], writes=[("ps", b)])
            bigp.put(sqh)
            rs, rsk, rsh = rsp.get()
            S.op("act", lambda: A.activation(out=rs[:, 0:n], in_=ps[:, b, 0:n], func=AF.Sqrt,
                                             bias=cst[:, C_EPS:C_EPS + 1], scale=1.0),
                 reads=[("ps", b), "cst"], writes=[rsk])
            S.op("dve", lambda: V.reciprocal(out=rs[:, 0:n], in_=rs[:, 0:n]), reads=[rsk], writes=[rsk])
            if vcol is not None:
                S.op("dve", lambda: V.tensor_tensor(out=rs[:, 0:n], in0=rs[:, 0:n],
                                                    in1=cst[:, C_VALID + vcol:C_VALID + vcol + n], op=ALU.mult),
                     reads=[rsk, "cst"], writes=[rsk])
            for k in range(8):
                if out_f32 is None:
                    o = hT[:, k, col0:col0 + n]
                    wk = ck("h", col0, n)
                else:
                    o = out_f32[:, k, 0:n]
                    wk = ["u"]
                if k < 5:
                    S.op("dve", lambda k=k, o=o: V.scalar_tensor_tensor(
                        out=o, in0=xT[:, k, col0:col0 + n], scalar=cst[:, gcol + k:gcol + k + 1],
                        in1=rs[:, 0:n], op0=ALU.mult, op1=ALU.mult),
                        reads=ck("x", col0, n) + [rsk, "cst"], writes=wk, sub=k)
                else:
                    tb, tbk, tbh = scr.get()
                    S.op("act", lambda k=k, tb=tb: A.activation(
                        out=tb[:, 0, 0:n], in_=xT[:, k, col0:col0 + n], func=AF.Identity,
                        scale=cst[:, gcol + k:gcol + k + 1]), reads=ck("x", col0, n) + ["cst"], writes=[tbk])
                    S.op("pool", lambda o=o, tb=tb: G.tensor_tensor(
                        out=o, in0=tb[:, 0, 0:n], in1=rs[:, 0:n], op=ALU.mult),
                        reads=[tbk, rsk], writes=wk, sub=k)
                    scr.put(tbh)
            rsp.put(rsh)

        zcount = [0]
        tcount = [0, 0]
        out_tickets = []
        bases = [-256, 512, 1280]
        phases = []
        ctl = {}

        def add_phase(units, body):
            phases.append((units, body))

        def make_sl(s, l):
            base = bases[s]
            cl = l * LW
            if s == 0 and l == 1:
                tiles = [(128, 128), (256, 256), (512, 256)]
            else:
                tiles = [(0, 256), (256, 256), (512, 256)]
            if s == 0 and l == 0:
                next_tiles = [(128, 128), (256, 256), (512, 256)]
            else:
                next_tiles = tiles
            NT = len(tiles)
            tbase = tcount[l]
            tcount[l] += NT

            def vcol_of(col0):
                return col0 if (s == 0 and col0 < 256) else None

            first_chunk = tiles[0][0] // 128
            slots = {}
            st = {"ple_norm_done": set(), "pa_norm_done": set()}

            def pa_tile(U, col0, n, is_last):
                Wpool, wp, wa, Wga = U["Wpool"], U["wp"], U["wa"], U["Wga"]
                nch = n // 128
                for ci in range(nch):
                    c = col0 // 128 + ci
                    slot = zcount[0] % 3
                    zcount[0] += 1
                    slots[c] = slot
                    for cbk in range(2):
                        b = bank()
                        S.mm([(ps[:, b, :], hT[:, k, c * 128:(c + 1) * 128],
                               Wpool.ap[:, k, cbk * 512:(cbk + 1) * 512], k == 0, k == 7) for k in range(8)],
                             reads=[("h", c)] + Wpool.keys, writes=[("ps", b)])
                        S.op("act", lambda b=b, slot=slot, cbk=cbk: A.copy(
                            out=zp_tm[:, slot, cbk * 512:(cbk + 1) * 512], in_=ps[:, b, :]),
                            reads=[("ps", b)], writes=[("zp", slot)], sub=cbk)
                if is_last:
                    ctl["release"]("Wpool")
                gates = {}

                def gate(djp):
                    bg = bank()
                    S.mm([(ps[:, bg, q * n:(q + 1) * n], Wga.ap[:, k, (djp * 2 + q) * 128:(djp * 2 + q + 1) * 128],
                           hT[:, k, col0:col0 + n], k == 0, k == 7) for q in range(2) for k in range(8)],
                         reads=ck("h", col0, n) + Wga.keys, writes=[("ps", bg)])
                    sc, sck, sch = scr.get()
                    S.op("act", lambda: A.activation(out=sc[:, :, 0:n], in_=q2(ps[:, bg, 0:2 * n]), func=AF.Sigmoid),
                         reads=[("ps", bg)], writes=[sck])
                    gates[djp] = (sc, sck, sch)

                gate(0)
                gate(1)
                pl, plk, plh = bigp.get()
                for ci in range(nch):
                    c = col0 // 128 + ci
                    slot = slots[c]
                    if c == first_chunk:
                        prev_ap, prev_key = zpc[:, l, :], ("zpc", l)
                    else:
                        prev_ap, prev_key = zp_tm[:, slots[c - 1], :], ("zp", slots[c - 1])
                    boff = CB_BFIRST if (s == 0 and c == 2) else CB_BCUR
                    for jh in range(2):
                        b = bank()
                        grp = []
                        for jj in range(4):
                            j = jh * 4 + jj
                            g = j // 2
                            o = ps[:, b, jj * 128:(jj + 1) * 128]
                            grp.append((o, zp_tm[:, slot, j * 128:(j + 1) * 128], Bm(boff, g), True, False))
                            grp.append((o, prev_ap[:, j * 128:(j + 1) * 128], Bm(CB_BPREV, g), False, True))
                        S.mm(grp, reads=[("zp", slot), prev_key, "cb"], writes=[("ps", b)])
                        S.op("act", lambda b=b, jh=jh, ci=ci: A.copy(
                            out=pl[:, jh * 4:(jh + 1) * 4, ci * 128:(ci + 1) * 128],
                            in_=ps[:, b, :].rearrange("p (j t) -> p j t", j=4)),
                            reads=[("ps", b)], writes=[plk], sub=(ci, jh))
                gate(2)
                yp, ypk, yph = bigp.get()
                for djp in range(4):
                    b = bank()
                    grp = []
                    for q in range(2):
                        dj = djp * 2 + q
                        g = dj // 2
                        for cc in range(2):
                            grp.append((ps[:, b, q * n:(q + 1) * n],
                                        wp.ap[:, g, cc, (dj % 2) * 128:(dj % 2 + 1) * 128],
                                        pl[:, 2 * g + cc, 0:n], cc == 0, cc == 1))
                    S.mm(grp, reads=[plk] + wp.keys, writes=[("ps", b)])
                    for q in range(2):
                        dj = djp * 2 + q
                        S.op("act", lambda b=b, q=q, dj=dj: A.activation(
                            out=yp[:, dj, 0:n], in_=ps[:, b, q * n:(q + 1) * n], func=AF.Identity,
                            scale=cst[:, cl + C_PSC + dj:cl + C_PSC + dj + 1]),
                            reads=[("ps", b), "cst"], writes=[ypk], sub=dj)
                bigp.put(plh)
                gate(3)
                for djp in range(4):
                    sc, sck, sch = gates[djp]
                    by = bank()
                    S.mm([(ps[:, by, q * n:(q + 1) * n], wa.ap[:, k, (djp * 2 + q) * 128:(djp * 2 + q + 1) * 128],
                           yp[:, k, 0:n], k == 0, k == 7) for q in range(2) for k in range(8)],
                         reads=[ypk] + wa.keys, writes=[("ps", by)])
                    S.op("dve", lambda by=by, sc=sc, djp=djp: V.tensor_tensor(
                        out=mixa[:, 2 * djp:2 * djp + 2, col0:col0 + n], in0=sc[:, :, 0:n],
                        in1=q2(ps[:, by, 0:2 * n]), op=ALU.mult),
                        reads=[sck, ("ps", by)], writes=ck("ma", col0, n), sub=djp)
                    scr.put(sch)
                bigp.put(yph)

            def pa_body(U):
                if tiles[0] not in st["pa_norm_done"]:
                    emit_norm(cl + C_MIXN, tiles[0][0], tiles[0][1], vcol_of(tiles[0][0]))
                for i, (col0, n) in enumerate(tiles):
                    if i + 1 < NT and tiles[i + 1] not in st["pa_norm_done"]:
                        emit_norm(cl + C_MIXN, tiles[i + 1][0], tiles[i + 1][1], vcol_of(tiles[i + 1][0]))
                    pa_tile(U, col0, n, i == NT - 1)
                last_c = (tiles[-1][0] + tiles[-1][1]) // 128 - 1
                S.op("act", lambda: A.copy(out=zpc[:, l, :], in_=zp_tm[:, slots[last_c], :]),
                     reads=[("zp", slots[last_c])], writes=[("zpc", l)])

            add_phase([("Wpool", kp(w_in, l, 0, 1024), [8, 1024]),
                       ("Wga", kp(w_in, l, 3072, 4096), [8, 1024]),
                       ("wp", w_pool[l].rearrange("g (c p) d -> p g c d", p=128), [4, 2, 256]),
                       ("wa", kp(w_a, l, 0, 1024), [8, 1024])], pa_body)

            def pb_tile(U, col0, n, is_last):
                Wu, Wv, wb, Wgb = U["Wu"], U["Wv"], U["wb"], U["Wgb"]
                nch = n // 128
                vslots = []
                vgs = []
                for ci in range(nch):
                    c = col0 // 128 + ci
                    slot = zcount[0] % 3
                    zcount[0] += 1
                    vslots.append(slot)
                    vg, vgk, vgh = vgp.get()
                    ss, ssk, ssh = ssp.get()
                    vgs.append((vg, vgk, vgh, ss, ssk, ssh, slot))
                    for cbk in range(2):
                        b = bank()
                        S.mm([(ps[:, b, :], hT[:, k, c * 128:(c + 1) * 128],
                               Wv.ap[:, k, cbk * 512:(cbk + 1) * 512], k == 0, k == 7) for k in range(8)],
                             reads=[("h", c)] + Wv.keys, writes=[("ps", b)])
                        S.op("act", lambda: A.activation(
                            out=vg[:, cbk * 512:(cbk + 1) * 512], in_=ps[:, b, :], func=AF.Gelu),
                            reads=[("ps", b)], writes=[vgk], sub=cbk)
                for (vg, vgk, vgh, ss, ssk, ssh, slot) in vgs:
                    S.op("dve", lambda: V.memset(ss[:], 0.0), writes=[ssk])
                    S.op("dve", lambda: V.scalar_tensor_tensor(
                        out=zp_tm[:, slot, :], in0=vg[:], scalar=1.0, in1=vg[:], op0=ALU.mult, op1=ALU.mult,
                        accum_out=ss[:, 0:1]), reads=[vgk, ssk], writes=[("zp", slot), ssk])
                for (vg, vgk, vgh, ss, ssk, ssh, slot) in vgs:
                    S.op("act", lambda: A.activation(
                        out=ss[:, 1:2], in_=ss[:, 0:1], func=AF.Sqrt, bias=cst[:, C_EPS:C_EPS + 1],
                        scale=1.0 / 1024.0), reads=[ssk, "cst"], writes=[ssk])
                for (vg, vgk, vgh, ss, ssk, ssh, slot) in vgs:
                    S.op("dve", lambda: V.reciprocal(out=ss[:, 1:2], in_=ss[:, 1:2]), reads=[ssk], writes=[ssk])
                for (vg, vgk, vgh, ss, ssk, ssh, slot) in vgs:
                    S.op("act", lambda: A.activation(out=vg[:], in_=vg[:], func=AF.Identity, scale=ss[:, 1:2]),
                         reads=[vgk, ssk], writes=[vgk])
                for (vg, vgk, vgh, ss, ssk, ssh, slot) in vgs:
                    S.op("pool", lambda: G.tensor_tensor(
                        out=zp_tm[:, slot, :], in0=vg[:], in1=gsgu[:, l, :], op=ALU.mult),
                        reads=[vgk, "gsgu"], writes=[("zp", slot)])
                    vgp.put(vgh)
                    ssp.put(ssh)
                if is_last:
                    ctl["release"]("Wv")
                for fp in range(4):
                    b = bank()
                    S.mm([(ps[:, b, q * n:(q + 1) * n], Wu.ap[:, k, (fp * 2 + q) * 128:(fp * 2 + q + 1) * 128],
                           hT[:, k, col0:col0 + n], k == 0, k == 7) for q in range(2) for k in range(8)],
                         reads=ck("h", col0, n) + Wu.keys, writes=[("ps", b)])
                    S.op("act", lambda b=b, fp=fp: A.activation(
                        out=ubuf[:, 2 * fp:2 * fp + 2, 0:n], in_=q2(ps[:, b, 0:2 * n]), func=AF.Gelu),
                        reads=[("ps", b)], writes=["u"], sub=fp)
                if is_last:
                    ctl["release"]("Wu")
                gt, gtk, gth = bigp.get()
                for ci in range(nch):
                    slot = vslots[ci]
                    for hh in range(2):
                        b = bank()
                        grp = []
                        for jj in range(4):
                            h = hh * 4 + jj
                            o = ps[:, b, jj * 128:(jj + 1) * 128]
                            grp.append((o, zp_tm[:, slot, h * 128:(h + 1) * 128], wsb[:, l, h, :], True, False))
                            grp.append((o, one33, brow[0:33, l, h * 128:(h + 1) * 128], False, True))
                        S.mm(grp, reads=[("zp", slot), "wsb", "brow", "cb"], writes=[("ps", b)])
                        S.op("dve", lambda b=b, hh=hh, ci=ci: V.tensor_tensor(
                            out=gt[:, hh * 4:(hh + 1) * 4, ci * 128:(ci + 1) * 128],
                            in0=ps[:, b, :].rearrange("p (j t) -> p j t", j=4),
                            in1=ubuf[:, hh * 4:(hh + 1) * 4, ci * 128:(ci + 1) * 128], op=ALU.mult),
                            reads=[("ps", b), "u"], writes=[gtk], sub=(ci, hh))
                gsc = {}
                for djp in range(4):
                    bg = bank()
                    S.mm([(ps[:, bg, q * n:(q + 1) * n], Wgb.ap[:, k, (djp * 2 + q) * 128:(djp * 2 + q + 1) * 128],
                           hT[:, k, col0:col0 + n], k == 0, k == 7) for q in range(2) for k in range(8)],
                         reads=ck("h", col0, n) + Wgb.keys, writes=[("ps", bg)])
                    sc, sck, sch = scr.get()
                    S.op("act", lambda bg=bg, sc=sc: A.activation(out=sc[:, :, 0:n], in_=q2(ps[:, bg, 0:2 * n]),
                                                                  func=AF.Sigmoid),
                         reads=[("ps", bg)], writes=[sck])
                    gsc[djp] = (sc, sck, sch)
                for djp in range(4):
                    sc, sck, sch = gsc[djp]
                    by = bank()
                    S.mm([(ps[:, by, q * n:(q + 1) * n], wb.ap[:, k, (djp * 2 + q) * 128:(djp * 2 + q + 1) * 128],
                           gt[:, k, 0:n], k == 0, k == 7) for q in range(2) for k in range(8)],
                         reads=[gtk] + wb.keys, writes=[("ps", by)])
                    S.op("dve", lambda by=by, sc=sc: V.tensor_tensor(
                        out=sc[:, :, 0:n], in0=sc[:, :, 0:n], in1=q2(ps[:, by, 0:2 * n]), op=ALU.mult),
                        reads=[sck, ("ps", by)], writes=[sck])
                    S.op("pool", lambda sc=sc, djp=djp: G.tensor_tensor(
                        out=mixa[:, 2 * djp:2 * djp + 2, col0:col0 + n], in0=sc[:, :, 0:n],
                        in1=mixa[:, 2 * djp:2 * djp + 2, col0:col0 + n], op=ALU.add),
                        reads=[sck] + ck("ma", col0, n), writes=ck("ma", col0, n), sub=djp)
                    scr.put(sch)
                bigp.put(gth)

            def pb_body(U):
                for i, (col0, n) in enumerate(tiles):
                    pb_tile(U, col0, n, i == NT - 1)

            add_phase([("Wv", kp(w_in, l, 2048, 3072), [8, 1024]),
                       ("Wu", kp(w_in, l, 1024, 2048), [8, 1024]),
                       ("Wgb", kp(w_in, l, 4096, 5120), [8, 1024]),
                       ("wb", kp(w_b, l, 0, 1024), [8, 1024])], pb_body)

            def pc_body(U):
                wo = U["wo"]
                for (col0, n) in tiles:
                    S.dma("pool", pTs[:, :, col0:col0 + n],
                          pin[l, :, :, base + 256 + col0: base + 256 + col0 + n],
                          writes=ck("pT", col0, n))

                def pc(col0, n):
                    for djp in range(4):
                        b = bank()
                        S.mm([(ps[:, b, q * n:(q + 1) * n], wo.ap[:, k, (djp * 2 + q) * 128:(djp * 2 + q + 1) * 128],
                               mixa[:, k, col0:col0 + n], k == 0, k == 7) for q in range(2) for k in range(8)],
                             reads=ck("ma", col0, n) + wo.keys, writes=[("ps", b)])
                        S.op("dve", lambda b=b, djp=djp: V.tensor_tensor(
                            out=xT[:, 2 * djp:2 * djp + 2, col0:col0 + n],
                            in0=xT[:, 2 * djp:2 * djp + 2, col0:col0 + n],
                            in1=q2(ps[:, b, 0:2 * n]), op=ALU.add),
                            reads=[("ps", b)] + ck("x", col0, n), writes=ck("x", col0, n), sub=djp)

                def nrm(col0, n):
                    emit_norm(cl + C_FFNN, col0, n, vcol_of(col0))

                pc(*tiles[0])
                for i in range(NT):
                    if i + 1 < NT:
                        pc(*tiles[i + 1])
                    nrm(*tiles[i])

            add_phase([("wo", kp(w_out, l, 0, 1024), [8, 1024])], pc_body)

            def make_ffn(bi, j0, cbn):
                last_block = (bi == len(FFN_BLOCKS) - 1)

                def up(U, col0, n, tpar):
                    upA, upB = U["upA"], U["upB"]
                    at, atk, ath = atp.get()
                    accs = {}

                    def stA(jl):
                        j = j0 + jl
                        b = bank()
                        grp = [(ps[:, b, 0:n], upA.ap[:, k, jl * 128:(jl + 1) * 128], hT[:, k, col0:col0 + n],
                                k == 0, k == 7) for k in range(8)]
                        grp += [(ps[:, b, n:2 * n], upB.ap[:, k, jl * 128:(jl + 1) * 128], hT[:, k, col0:col0 + n],
                                 k == 0, k == 7) for k in range(8)]
                        S.mm(grp, reads=ck("h", col0, n) + upA.keys + upB.keys, writes=[("ps", b)])
                        acc, acck, acch = scr.get()
                        accs[jl] = (acc, acck, acch, b)
                        S.op("dve", lambda: V.memset(acc[:, :, 0:1], 0.0), writes=[acck, (acck, 0), (acck, 1)],
                             sub="m0")
                        S.op("dve", lambda: V.memset(acc[:, :, n + 1:n + 2], 0.0), writes=[(acck, 0), (acck, 1)],
                             sub="m1")
                        for q, f in ((0, j), (1, NFF + j)):
                            w1 = cst[:, cl + C_CW + 44 + f:cl + C_CW + 44 + f + 1]
                            bb = cst[:, cl + C_CB + f:cl + C_CB + f + 1]
                            S.op("act", lambda: A.activation(
                                out=acc[:, q, 1:n + 1], in_=ps[:, b, q * n:q * n + n], func=AF.Identity,
                                scale=w1, bias=bb),
                                reads=[("ps", b), "cst"], writes=[(acck, q)])
                        S.op("pool", lambda: G.tensor_tensor(
                            out=acc[:, :, 0:2], in0=acc[:, :, 0:2],
                            in1=upc[:, l, tpar, j, :].rearrange("p (q t) -> p q t", q=2), op=ALU.add),
                            reads=[("upc", l, tpar, j), (acck, 0), (acck, 1)], writes=[(acck, 0), (acck, 1)],
                            sub="head")

                    def stB(jl):
                        j = j0 + jl
                        acc, acck, acch, b = accs[jl]
                        for q, f in ((0, j), (1, NFF + j)):
                            w2 = cst[:, cl + C_CW + 88 + f:cl + C_CW + 88 + f + 1]
                            S.op("dve", lambda: V.scalar_tensor_tensor(
                                out=acc[:, q, 0:n], in0=ps[:, b, q * n:q * n + n], scalar=w2,
                                in1=acc[:, q, 0:n], op0=ALU.mult, op1=ALU.add),
                                reads=[("ps", b), (acck, q), "cst"], writes=[(acck, q)])
                        for q, f in ((0, j), (1, NFF + j)):
                            w0 = cst[:, cl + C_CW + f:cl + C_CW + f + 1]
                            S.op("dve", lambda: V.scalar_tensor_tensor(
                                out=acc[:, q, 2:n + 2], in0=ps[:, b, q * n:q * n + n], scalar=w0,
                                in1=acc[:, q, 2:n + 2], op0=ALU.mult, op1=ALU.add),
                                reads=[("ps", b), (acck, q), "cst"], writes=[(acck, q)])
                        if "copy" in SKIP:
                            pass
                        elif os.environ.get("KV_ACTCOPY"):
                            S.op("act", lambda: A.copy(
                                out=upc[:, l, 1 - tpar, j, :].rearrange("p (q t) -> p q t", q=2), in_=acc[:, :, n:n + 2]),
                                reads=[(acck, 0), (acck, 1)], writes=[("upc", l, 1 - tpar, j)], sub="tail")
                        else:
                            S.op("pool", lambda: G.tensor_copy(
                                out=upc[:, l, 1 - tpar, j, :].rearrange("p (q t) -> p q t", q=2), in_=acc[:, :, n:n + 2]),
                                reads=[(acck, 0), (acck, 1)], writes=[("upc", l, 1 - tpar, j)], sub="tail")

                    def stC(jl):
                        acc, acck, acch, b = accs[jl]
                        S.op("act", lambda: A.activation(out=acc[:, 0, 0:n], in_=acc[:, 0, 0:n], func=AF.Gelu),
                             reads=[(acck, 0)], writes=[(acck, 0)])
                        S.op("pool", lambda: G.tensor_tensor(
                            out=at[:, jl, 0:n], in0=acc[:, 0, 0:n], in1=acc[:, 1, 0:n], op=ALU.mult),
                            reads=[acck, (acck, 0), (acck, 1)], writes=[atk])
                        scr.put(acch)

                    for step in range(cbn + 2):
                        if step < cbn:
                            stA(step)
                        if 0 <= step - 1 < cbn:
                            stB(step - 1)
                        if 0 <= step - 2 < cbn:
                            stC(step - 2)
                    return (at, atk, ath)

                def down(U, col0, n, atr):
                    dwn = U["dwn"]
                    at, atk, ath = atr
                    for djp in range(4):
                        b = bank()
                        S.mm([(ps[:, b, q * n:(q + 1) * n], dwn.ap[:, jl, (djp * 2 + q) * 128:(djp * 2 + q + 1) * 128],
                               at[:, jl, 0:n], jl == 0, jl == cbn - 1) for q in range(2) for jl in range(cbn)],
                             reads=[atk] + dwn.keys, writes=[("ps", b)])
                        S.op("dve", lambda b=b, djp=djp: V.tensor_tensor(
                            out=xT[:, 2 * djp:2 * djp + 2, col0:col0 + n],
                            in0=xT[:, 2 * djp:2 * djp + 2, col0:col0 + n],
                            in1=q2(ps[:, b, 0:2 * n]), op=ALU.add),
                            reads=[("ps", b)] + ck("x", col0, n), writes=ck("x", col0, n), sub=djp)
                    atp.put(ath)

                def body(U):
                    ats = {}
                    ats[0] = up(U, tiles[0][0], tiles[0][1], (tbase + 0) % 2)
                    for i in range(NT):
                        if i + 1 < NT:
                            ats[i + 1] = up(U, tiles[i + 1][0], tiles[i + 1][1], (tbase + i + 1) % 2)
                            if i + 1 == NT - 1:
                                ctl["release"]("upA")
                                ctl["release"]("upB")
                        if last_block and i == 1:
                            emit_norm(cl + C_PLEN, tiles[0][0], tiles[0][1], vcol_of(tiles[0][0]))
                            st["ple_norm_done"].add(tiles[0])
                        down(U, tiles[i][0], tiles[i][1], ats[i])

                add_phase([("upA", kp(w_up, l, j0 * 128, (j0 + cbn) * 128), [8, cbn * 128]),
                           ("upB", kp(w_up, l, DFF + j0 * 128, DFF + (j0 + cbn) * 128), [8, cbn * 128]),
                           ("dwn", w_down[l, j0 * 128:(j0 + cbn) * 128, :].rearrange("(j p) c -> p j c", p=128),
                            [cbn, 1024])], body)

            for bi, (j0, cbn) in enumerate(FFN_BLOCKS):
                make_ffn(bi, j0, cbn)

            def ple_body(U):
                wpg, wpl = U["wpg"], U["wpl"]
                pend = []

                def finalize(col0, n):
                    if base + col0 >= 0:
                        emit_norm(C_FIN, col0, n, None, out_f32=ubuf)
                        oc = base + col0
                        out_tickets.append(S.dma("sp", out[:, :, oc:oc + n], ubuf[:, :, 0:n], reads=["u"]))
                    if s + 1 < 3:
                        c0 = (col0 // 256) * 256
                        nb = bases[s + 1]
                        S.dma("sp", xT[:, :, c0:c0 + 256], xin[:, :, nb + 256 + c0: nb + 256 + c0 + 256],
                              writes=ck("x", c0, 256))

                for ti, (col0, n) in enumerate(tiles):
                    for tj in (ti, ti + 1):
                        if tj < NT and tiles[tj] not in st["ple_norm_done"]:
                            emit_norm(cl + C_PLEN, tiles[tj][0], tiles[tj][1], vcol_of(tiles[tj][0]))
                            st["ple_norm_done"].add(tiles[tj])
                    for djp in range(4):
                        bg = bank()
                        S.mm([(ps[:, bg, q * n:(q + 1) * n], wpg.ap[:, k, (djp * 2 + q) * 128:(djp * 2 + q + 1) * 128],
                               hT[:, k, col0:col0 + n], k == 0, k == 7) for q in range(2) for k in range(8)],
                             reads=ck("h", col0, n) + wpg.keys, writes=[("ps", bg)])
                        sc, sck, sch = scr.get()
                        S.op("act", lambda: A.activation(out=sc[:, :, 0:n], in_=q2(ps[:, bg, 0:2 * n]),
                                                         func=AF.Sigmoid),
                             reads=[("ps", bg)], writes=[sck])
                        be = bank()
                        S.mm([(ps[:, be, q * n:(q + 1) * n], wpl.ap[:, kk, (djp * 2 + q) * 128:(djp * 2 + q + 1) * 128],
                               pTs[:, kk, col0:col0 + n], kk == 0, kk == 1) for q in range(2) for kk in range(2)],
                             reads=ck("pT", col0, n) + wpl.keys, writes=[("ps", be)])
                        S.op("dve", lambda: V.tensor_tensor(
                            out=sc[:, :, 0:n], in0=sc[:, :, 0:n], in1=q2(ps[:, be, 0:2 * n]), op=ALU.mult),
                            reads=[sck, ("ps", be)], writes=[sck])
                        S.op("pool", lambda: G.tensor_tensor(
                            out=xT[:, 2 * djp:2 * djp + 2, col0:col0 + n],
                            in0=xT[:, 2 * djp:2 * djp + 2, col0:col0 + n], in1=sc[:, :, 0:n], op=ALU.add),
                            reads=[sck] + ck("x", col0, n), writes=ck("x", col0, n), sub=djp)
                        scr.put(sch)
                    if l == 1:
                        if pend:
                            finalize(*pend.pop())
                        pend.append((col0, n))
                while pend:
                    finalize(*pend.pop())

            add_phase([("wpg", kp(w_pg, l, 0, 1024), [8, 1024]),
                       ("wpl", kp(w_ple, l, 0, 1024), [2, 1024])], ple_body)

        def x_load_phase(s):
            base = bases[s]

            def body(U):
                return
            add_phase([], body)

        for s in range(3):
            x_load_phase(s)
            for l in range(2):
                make_sl(s, l)

        ring = {"ptr": 0}
        live = {}
        loaded = {}
        regions = {}

        def try_load(pi, j):
            name, src, dims = phases[pi][0][j]
            size = 1
            for d_ in dims:
                size *= d_
            assert size % 1024 == 0
            npg = size // 1024
            if os.environ.get("KV_FIFO"):
                p0 = ring["ptr"]
                if p0 + npg > ring_pages:
                    p0 = 0
                for regs in live.values():
                    for (a, b_) in regs:
                        if not (p0 + npg <= a or p0 >= b_):
                            return False
                ring["ptr"] = p0 + npg
            else:
                occ = [r for regs in live.values() for r in regs]
                if npg >= 6:
                    cands = list(range(0, ring_pages - npg + 1, npg))
                else:
                    cands = list(range(ring_pages - npg, -1, -1))
                p0 = None
                for c in cands:
                    if all(c + npg <= a or c >= b_ for (a, b_) in occ):
                        p0 = c
                        break
                if p0 is None:
                    return False
            live.setdefault(pi, []).append((p0, p0 + npg))
            regions[(pi, j)] = (p0, p0 + npg)
            if os.environ.get("KV_DBG"):
                print("LOAD phase", pi, name, "pages", p0, p0 + npg, "at_phase", cur_phase[0])
            flat = wr[:, p0 * 1024:p0 * 1024 + size]
            if len(dims) == 2:
                ap = flat.rearrange("p (a b) -> p a b", a=dims[0])
            else:
                ap = flat.rearrange("p (a b c) -> p a b c", a=dims[0], b=dims[1])
            keys = [("wr", pg) for pg in range(p0, p0 + npg)]
            S.dma("pool", ap, src, reads=(), writes=keys)
            loaded[(pi, j)] = Unit(ap, keys)
            return True

        cur_phase = [0]

        def prefetch_from(pi):
            for pj in range(pi + 1, min(pi + 5, len(phases))):
                for j in range(len(phases[pj][0])):
                    if (pj, j) not in loaded:
                        if not try_load(pj, j):
                            return

        def release_unit(name):
            pi = cur_phase[0]
            units = phases[pi][0]
            j = [u[0] for u in units].index(name)
            live[pi].remove(regions[(pi, j)])
            prefetch_from(pi)

        ctl["release"] = release_unit
        for pi in range(len(phases)):
            cur_phase[0] = pi
            units, body = phases[pi]
            for j in range(len(units)):
                if (pi, j) not in loaded:
                    ok = try_load(pi, j)
                    assert ok, "weight ring too small for phase %d" % pi
            prefetch_from(pi)
            body({units[j][0]: loaded[(pi, j)] for j in range(len(units))})
            live.pop(pi, None)

        for t in out_tickets:
            S._wait("sp", [t])
    return nc


_CACHE = {}


def _pool_mats(first):
    Bcur = np.zeros((4, 128, 128), np.float32)
    Bprev = np.zeros((4, 128, 128), np.float32)
    for g, w in enumerate((2, 4, 8, 16)):
        for t in range(128):
            cnt = min(t + 1, w) if first else w
            for sg in range(t - w + 1, t + 1):
                if sg >= 0:
                    Bcur[g, sg, t] += 1.0 / cnt
                elif not first:
                    Bprev[g, sg + 128, t] += 1.0 / cnt
            Bcur[g, t, t] -= 1.0
    return Bcur, Bprev


def kernel(x, p, mix_norm, w_in, w_pool, pool_scale, sgu_norm, w_spatial, b_spatial,
           w_branch_a, w_branch_b, w_out, ffn_norm, w_up, conv_w, conv_b, w_down,
           ple_norm, w_ple_gate, w_ple, final_norm):
    f = np.float32
    x = np.asarray(x, f)
    p = np.asarray(p, f)
    if "nc" not in _CACHE:
        _CACHE["nc"] = build_program()
    nc = _CACHE["nc"]

    def vec8(v):
        return np.asarray(v, f).reshape(8, 128).T

    Bcur, Bprev = _pool_mats(False)
    Bfirst, _ = _pool_mats(True)
    mask = (np.arange(128)[None, :] >= np.arange(128)[:, None]).astype(f)
    gsgu = np.ascontiguousarray(np.broadcast_to(np.asarray(sgu_norm, f)[None, :, :], (128, 2, 1024)))
    bTt = np.ascontiguousarray(np.broadcast_to(np.asarray(b_spatial, f).reshape(1, 2, 1024), (128, 2, 1024)))
    wsT = np.ascontiguousarray(np.transpose(np.asarray(w_spatial, f), (3, 0, 1, 2)).reshape(128, 2, 1024))
    shared = {
        "w_in": np.asarray(w_in, f), "w_pool": np.asarray(w_pool, f),
        "w_branch_a": np.asarray(w_branch_a, f), "w_branch_b": np.asarray(w_branch_b, f),
        "w_out": np.asarray(w_out, f), "w_up": np.asarray(w_up, f), "w_down": np.asarray(w_down, f),
        "w_ple_gate": np.asarray(w_ple_gate, f), "w_ple": np.asarray(w_ple, f),
        "gsgu": gsgu, "bT": bTt, "wsT": wsT, "mask": mask,
    }
    cst0 = np.zeros((128, NCST), f)
    for l in range(2):
        o = l * LW
        cst0[:, o + C_MIXN:o + C_MIXN + 8] = vec8(mix_norm[l])
        cst0[:, o + C_FFNN:o + C_FFNN + 8] = vec8(ffn_norm[l])
        cst0[:, o + C_PLEN:o + C_PLEN + 8] = vec8(ple_norm[l])
        cst0[:, o + C_PSC:o + C_PSC + 8] = vec8(pool_scale[l])
        for k in range(3):
            cst0[:, o + C_CW + k * 44:o + C_CW + (k + 1) * 44] = np.asarray(conv_w[l][k], f).reshape(44, 128).T
        cst0[:, o + C_CB:o + C_CB + 44] = np.asarray(conv_b[l], f).reshape(44, 128).T
    cst0[:, C_FIN:C_FIN + 8] = vec8(final_norm)
    cst0[:, C_EPS:C_EPS + 8] = EPS

    in_maps = []
    for core in range(8):
        b = core // 4
        t0 = (core % 4) * TOK
        xs = np.zeros((TIN, D), f)
        ps_ = np.zeros((2, TIN, 256), f)
        lo = t0 - HALO
        if lo >= 0:
            xs[:] = x[b, lo:t0 + TOK]
            ps_[:] = p[:, b, lo:t0 + TOK]
        else:
            xs[HALO:] = x[b, 0:TOK]
            ps_[:, HALO:] = p[:, b, 0:TOK]
        xTc = np.ascontiguousarray(xs.T.reshape(8, 128, TIN).transpose(1, 0, 2))
        pTc = np.ascontiguousarray(ps_.transpose(0, 2, 1).reshape(2, 2, 128, TIN).transpose(0, 2, 1, 3))
        cstc = cst0.copy()
        cstc[:, C_VALID:C_VALID + 256] = 1.0 if lo >= 0 else 0.0
        cbc = np.zeros((128, NCSTB), f)
        cbc[:, CB_ONES:CB_ONES + 128] = 1.0 / 1024.0
        cbc[:, CB_ONE1:CB_ONE1 + 128] = 1.0
        for g in range(4):
            cbc[:, CB_BCUR + g * 128:CB_BCUR + (g + 1) * 128] = Bcur[g]
            cbc[:, CB_BPREV + g * 128:CB_BPREV + (g + 1) * 128] = Bprev[g]
            cbc[:, CB_BFIRST + g * 128:CB_BFIRST + (g + 1) * 128] = (Bfirst[g] if lo < 0 else Bcur[g])
        m = dict(shared)
        m.update({"xT": xTc, "pT": pTc, "cst": cstc, "cstb": cbc})
        in_maps.append(m)

    res = run_bass_kernel_spmd(nc, in_maps, core_ids=list(range(8)))
    outp = np.zeros((2, SEQ, D), f)
    for core in range(8):
        b = core // 4
        t0 = (core % 4) * TOK
        oT = np.asarray(res.results[core]["outT"], f)
        outp[b, t0:t0 + TOK] = oT.transpose(2, 1, 0).reshape(TOK, D)
    return outp
```

```python
import os
import numpy as np
from contextlib import ExitStack
import concourse.bass as bass
import concourse.mybir as mybir
from concourse.bass_utils import run_bass_kernel_spmd

F32 = mybir.dt.float32
BF16 = mybir.dt.bfloat16
AF = mybir.ActivationFunctionType
ALU = mybir.AluOpType

D = 1024
DFF = 2816
NFF = 22
SEQ = 8192
TOK = 2048
HALO = 256
TIN = TOK + HALO
TS = 768
EPS = 1e-6
LW = 208
C_MIXN, C_FFNN, C_PLEN, C_PSC, C_CW, C_CB = 0, 8, 16, 24, 32, 164
C_FIN = 2 * LW
C_VALID = C_FIN + 8
C_EPS = C_VALID + 256
NCST = C_EPS + 8
CB_ONES, CB_BCUR, CB_BPREV, CB_BFIRST = 0, 128, 640, 1152
CB_ONE1 = 1664
NCSTB = 1792
ND = 8
SKIP = os.environ.get('KV_SKIP', '')
FFN_BLOCKS = [(0, 6), (6, 6), (12, 6), (18, 4)]


class Sched:
    def __init__(self, nc, es):
        self.nc = nc
        self.engs = {"pe": nc.tensor, "act": nc.scalar, "dve": nc.vector,
                     "pool": nc.gpsimd, "sp": nc.sync}
        self.sem = {}
        self.cnt = {}
        for e in self.engs:
            self.sem[e] = es.enter_context(nc.semaphore("s_" + e))
            self.cnt[e] = 0
        self.dsem = {}
        self.dcnt = {}
        self.drr = {}
        for q in ("pool", "sp"):
            self.dsem[q] = [es.enter_context(nc.semaphore("d_%s%d" % (q, i))) for i in range(ND)]
            self.dcnt[q] = [0] * ND
            self.drr[q] = 0
        self.seen = {e: {} for e in self.engs}
        self.lastw = {}
        self.readers = {}
        self.lastw_sub = {}
        self.readers_sub = {}
        self.nops = 0

    def _same(self, e, t, sub, tsub):
        if t[3] < self.cnt[e] - 3:
            return False
        if sub is not None and tsub is not None and sub != tsub:
            return False
        return True

    def _deps(self, e, reads, writes, sub=None):
        deps = []
        for k in reads:
            w = self.lastw.get(k)
            if w is not None:
                if w[0] != e or w[0] == "dma":
                    deps.append(w)
                elif self._same(e, w, sub, self.lastw_sub.get(k)):
                    deps.append(w)
            if isinstance(k, tuple) and k[0] == "ps":
                for r in self.readers.get(k, {}).values():
                    if r[0] != e:
                        deps.append(r)
        for k in writes:
            w = self.lastw.get(k)
            if w is not None:
                if w[0] != e or w[0] == "dma":
                    deps.append(w)
                elif self._same(e, w, sub, self.lastw_sub.get(k)):
                    deps.append(w)
            for sk, r in self.readers.get(k, {}).items():
                if r[0] != e or r[0] == "dma":
                    deps.append(r)
                elif self._same(e, r, sub, self.readers_sub.get(k, {}).get(sk)):
                    deps.append(r)
        return deps

    def _wait(self, e, deps):
        best = {}
        for d in deps:
            if d[2] not in best or best[d[2]][3] < d[3]:
                best[d[2]] = d
        for d in best.values():
            if self.seen[e].get(d[2], 0) >= d[3]:
                continue
            self.engs[e].wait_ge(d[1], d[3])
            self.seen[e][d[2]] = d[3]

    def _reg(self, t, reads, writes, sub=None):
        for k in reads:
            self.readers.setdefault(k, {})[t[2]] = t
            self.readers_sub.setdefault(k, {})[t[2]] = sub
        for k in writes:
            self.lastw[k] = t
            self.lastw_sub[k] = sub
            self.readers[k] = {}
            self.readers_sub[k] = {}

    def op(self, e, fn, reads=(), writes=(), sub=None):
        self._wait(e, self._deps(e, reads, writes, sub))
        ins = fn()
        self.cnt[e] += 1
        ins.then_inc(self.sem[e], 1)
        t = (e, self.sem[e], "s_" + e, self.cnt[e])
        self._reg(t, reads, writes, sub)
        self.nops += 1
        return t

    def mm(self, grp, reads=(), writes=()):
        e = "pe"
        self._wait(e, self._deps(e, reads, writes))
        ins = None
        for (o, l, r, st, sp) in grp:
            ins = self.nc.tensor.matmul(o, l, r, start=st, stop=sp)
            self.nops += 1
        self.cnt[e] += 1
        ins.then_inc(self.sem[e], 1)
        t = (e, self.sem[e], "s_" + e, self.cnt[e])
        self._reg(t, reads, writes)
        return t

    def dma(self, q, out, in_, reads=(), writes=()):
        i = self.drr[q]
        self.drr[q] = (i + 1) % ND
        sem = self.dsem[q][i]
        key = "d_%s%d" % (q, i)
        deps = self._deps("dma", reads, writes)
        if self.dcnt[q][i] > 0:
            deps.append(("dma", sem, key, self.dcnt[q][i]))
        self._wait(q, deps)
        ins = self.engs[q].dma_start(out=out, in_=in_)
        self.dcnt[q][i] += 16
        ins.then_inc(sem, 16)
        t = ("dma", sem, key, self.dcnt[q][i])
        self._reg(t, reads, writes)
        self.nops += 1
        return t


class BufPool:
    def __init__(self, bufs, name):
        self.bufs = bufs
        self.name = name
        self.free = list(range(len(bufs)))

    def get(self):
        assert self.free, "pool %s exhausted" % self.name
        j = self.free.pop(0)
        return self.bufs[j], (self.name, j), j

    def put(self, j):
        assert j not in self.free
        self.free.append(j)


class Unit:
    def __init__(self, ap, keys):
        self.ap = ap
        self.keys = keys


def build_program():
    nc = bass.Bass("TRN2", target_bir_lowering=False)

    def dram(name, shape, kind="ExternalInput"):
        return nc.dram_tensor(name, shape, F32, kind=kind).ap()

    xin = dram("xT", [128, 8, TIN])
    pin = dram("pT", [2, 128, 2, TIN])
    w_in = dram("w_in", [2, 1024, 5120])
    w_pool = dram("w_pool", [2, 4, 256, 256])
    w_a = dram("w_branch_a", [2, 1024, 1024])
    w_b = dram("w_branch_b", [2, 1024, 1024])
    w_out = dram("w_out", [2, 1024, 1024])
    w_up = dram("w_up", [2, 1024, 2 * DFF])
    w_down = dram("w_down", [2, DFF, 1024])
    w_pg = dram("w_ple_gate", [2, 1024, 1024])
    w_ple = dram("w_ple", [2, 256, 1024])
    cst_in = dram("cst", [128, NCST])
    cstb_in = dram("cstb", [128, NCSTB])
    gsgu_in = dram("gsgu", [128, 2, 1024])
    bT_in = dram("bT", [128, 2, 1024])
    wsT_in = dram("wsT", [128, 2, 1024])
    mask_in = dram("mask", [128, 128])
    out = dram("outT", [128, 8, TOK], kind="ExternalOutput")

    with ExitStack() as es:
        S = Sched(nc, es)
        V = nc.vector
        G = nc.gpsimd
        A = nc.scalar

        def sb(name, shape, dt):
            return es.enter_context(nc.sbuf_tensor(name, shape, dt))

        xT = sb("xT_sb", [128, 8, TS], F32)
        hT = sb("hT_sb", [128, 8, TS], BF16)
        mixa = sb("mixa_sb", [128, 8, TS], BF16)
        cst = sb("cst_sb", [128, NCST], F32)
        cb = sb("cb_sb", [128, NCSTB], BF16)
        mask = sb("mask_sb", [128, 128], F32)
        gsgu = sb("gsgu_sb", [128, 2, 1024], F32)
        brow = sb("brow_sb", [128, 2, 1024], BF16)
        wsb = sb("wsb_sb", [128, 2, 8, 128], BF16)
        zpc = sb("zpc_sb", [128, 2, 1024], BF16)
        upc = sb("upc_sb", [128, 2, 2, NFF, 4], F32)
        bigp = BufPool([sb("pa%d" % i, [128, 8, 256], BF16) for i in range(3)], "pa")
        rsp = BufPool([sb("rs%d" % i, [128, 256], F32) for i in range(2)], "rs")
        zp_tm = sb("zp_tm", [128, 3, 1024], BF16)
        vgp = BufPool([sb("vg%d" % i, [128, 1024], F32) for i in range(2)], "vg")
        scr = BufPool([sb("sc%d" % i, [128, 2, 264], F32) for i in range(6)], "sc")
        ubuf = sb("u_sb", [128, 8, 256], F32)
        atp = BufPool([sb("at%d" % i, [128, 6, 256], BF16) for i in range(2)], "at")
        pTs = sb("pT_sb", [128, 2, TS], BF16)
        ssp = BufPool([sb("ss%d" % i, [128, 2], F32) for i in range(4)], "ss")
        ps = es.enter_context(nc.psum_tensor("ps", [128, 8, 512], F32))

        remaining = int(nc.sbuf_bytes_remaining)
        ring_pages = (remaining - 512) // 2048
        assert ring_pages >= 34, ring_pages
        wr = sb("wring", [128, ring_pages * 1024], BF16)

        bank_i = [0]

        def bank():
            b = bank_i[0] % 8
            bank_i[0] += 1
            return b

        def kp(w, l, c0, c1):
            return w[l, :, c0:c1].rearrange("(k p) c -> p k c", p=128)

        def ck(prefix, col0, n):
            return [(prefix, c) for c in range(col0 // 128, (col0 + n) // 128)]

        def q2(ap):
            return ap.rearrange("p (q t) -> p q t", q=2)

        for t3 in range(3):
            S.dma("sp", xT[:, :, t3 * 256:(t3 + 1) * 256], xin[:, :, t3 * 256:(t3 + 1) * 256],
                  writes=ck("x", t3 * 256, 256))
        S.dma("sp", cst[:], cst_in, writes=["cst"])
        S.dma("sp", mask[:], mask_in, writes=["mask"])
        S.dma("sp", gsgu[:], gsgu_in, writes=["gsgu"])
        S.dma("pool", cb[:], cstb_in, writes=["cb"])
        uflat = ubuf[:].rearrange("p a b -> p (a b)")
        S.dma("sp", uflat, wsT_in.rearrange("p l c -> p (l c)"), writes=["u"])
        for l in range(2):
            for h in range(8):
                S.op("dve", lambda l=l, h=h: V.tensor_tensor(
                    out=wsb[:, l, h, :], in0=uflat[:, l * 1024 + h * 128:l * 1024 + (h + 1) * 128],
                    in1=mask[:], op=ALU.mult), reads=["u", "mask"], writes=["wsb"])
        browf = brow[:].rearrange("p l c -> p (l c)")
        S.dma("sp", uflat, bT_in.rearrange("p l c -> p (l c)"), reads=["wsb"], writes=["u"])
        S.op("dve", lambda: V.memset(browf[0:64, :], 0.0), writes=["brow"])
        S.op("dve", lambda: V.tensor_copy(out=browf[0:1, :], in_=uflat[0:1, :]), reads=["u"], writes=["brow"])
        btmp = zp_tm[:, 0:2, :].rearrange("p a b -> p (a b)")
        S.op("dve", lambda: V.tensor_copy(out=btmp[32:33, :], in_=uflat[32:33, :]), reads=["u"],
             writes=[("zp", 0), ("zp", 1)])
        S.op("dve", lambda: V.tensor_tensor(out=uflat[32:33, :], in0=uflat[32:33, :], in1=btmp[32:33, :],
                                            op=ALU.subtract), reads=["u", ("zp", 0), ("zp", 1)], writes=["u"])
        S.op("dve", lambda: V.tensor_copy(out=browf[32:33, :], in_=uflat[32:33, :]), reads=["u"], writes=["brow"])
        S.op("dve", lambda: V.memset(zpc[:], 0.0), writes=[("zpc", 0), ("zpc", 1)])
        S.op("dve", lambda: V.memset(upc[:], 0.0),
             writes=[("upc", l, pr, j) for l in range(2) for pr in range(2) for j in range(NFF)])

        ones = cb[:, CB_ONES:CB_ONES + 128]
        one33 = cb[0:33, CB_ONE1:CB_ONE1 + 128]

        def Bm(off, g):
            return cb[:, off + g * 128: off + (g + 1) * 128]

        def emit_norm(gcol, col0, n, vcol=None, out_f32=None):
            sq, sqk, sqh = bigp.get()
            S.op("act", lambda: A.activation(out=sq[:, :, 0:n], in_=xT[:, :, col0:col0 + n], func=AF.Square),
                 reads=ck("x", col0, n), writes=[sqk])
            b = bank()
            S.mm([(ps[:, b, 0:n], ones, sq[:, k, 0:n], k == 0, k == 7) for k in range(8)],
                 reads=[sqk, "cb"], writes=[("ps", b)])
            bigp.put(sqh)
            rs, rsk, rsh = rsp.get()
            S.op("act", lambda: A.activation(out=rs[:, 0:n], in_=ps[:, b, 0:n], func=AF.Sqrt,
                                             bias=cst[:, C_EPS:C_EPS + 1], scale=1.0),
                 reads=[("ps", b), "cst"], writes=[rsk])
            S.op("dve", lambda: V.reciprocal(out=rs[:, 0:n], in_=rs[:, 0:n]), reads=[rsk], writes=[rsk])
            if vcol is not None:
                S.op("dve", lambda: V.tensor_tensor(out=rs[:, 0:n], in0=rs[:, 0:n],
                                                    in1=cst[:, C_VALID + vcol:C_VALID + vcol + n], op=ALU.mult),
                     reads=[rsk, "cst"], writes=[rsk])
            for k in range(8):
                if out_f32 is None:
                    o = hT[:, k, col0:col0 + n]
                    wk = ck("h", col0, n)
                else:
                    o = out_f32[:, k, 0:n]
                    wk = ["u"]
                if k < 5:
                    S.op("dve", lambda k=k, o=o: V.scalar_tensor_tensor(
                        out=o, in0=xT[:, k, col0:col0 + n], scalar=cst[:, gcol + k:gcol + k + 1],
                        in1=rs[:, 0:n], op0=ALU.mult, op1=ALU.mult),
                        reads=ck("x", col0, n) + [rsk, "cst"], writes=wk, sub=k)
                else:
                    tb, tbk, tbh = scr.get()
                    S.op("act", lambda k=k, tb=tb: A.activation(
                        out=tb[:, 0, 0:n], in_=xT[:, k, col0:col0 + n], func=AF.Identity,
                        scale=cst[:, gcol + k:gcol + k + 1]), reads=ck("x", col0, n) + ["cst"], writes=[tbk])
                    S.op("pool", lambda o=o, tb=tb: G.tensor_tensor(
                        out=o, in0=tb[:, 0, 0:n], in1=rs[:, 0:n], op=ALU.mult),
                        reads=[tbk, rsk], writes=wk, sub=k)
                    scr.put(tbh)
            rsp.put(rsh)

        zcount = [0]
        tcount = [0, 0]
        out_tickets = []
        bases = [-256, 512, 1280]
        phases = []
        ctl = {}

        def add_phase(units, body):
            phases.append((units, body))

        def make_sl(s, l):
            base = bases[s]
            cl = l * LW
            if s == 0 and l == 1:
                tiles = [(128, 128), (256, 256), (512, 256)]
            else:
                tiles = [(0, 256), (256, 256), (512, 256)]
            if s == 0 and l == 0:
                next_tiles = [(128, 128), (256, 256), (512, 256)]
            else:
                next_tiles = tiles
            NT = len(tiles)
            tbase = tcount[l]
            tcount[l] += NT

            def vcol_of(col0):
                return col0 if (s == 0 and col0 < 256) else None

            first_chunk = tiles[0][0] // 128
            slots = {}
            st = {"ple_norm_done": set(), "pa_norm_done": set()}

            def pa_tile(U, col0, n, is_last):
                Wpool, wp, wa, Wga = U["Wpool"], U["wp"], U["wa"], U["Wga"]
                nch = n // 128
                for ci in range(nch):
                    c = col0 // 128 + ci
                    slot = zcount[0] % 3
                    zcount[0] += 1
                    slots[c] = slot
                    for cbk in range(2):
                        b = bank()
                        S.mm([(ps[:, b, :], hT[:, k, c * 128:(c + 1) * 128],
                               Wpool.ap[:, k, cbk * 512:(cbk + 1) * 512], k == 0, k == 7) for k in range(8)],
                             reads=[("h", c)] + Wpool.keys, writes=[("ps", b)])
                        S.op("act", lambda b=b, slot=slot, cbk=cbk: A.copy(
                            out=zp_tm[:, slot, cbk * 512:(cbk + 1) * 512], in_=ps[:, b, :]),
                            reads=[("ps", b)], writes=[("zp", slot)], sub=cbk)
                if is_last:
                    ctl["release"]("Wpool")
                gates = {}

                def gate(djp):
                    bg = bank()
                    S.mm([(ps[:, bg, q * n:(q + 1) * n], Wga.ap[:, k, (djp * 2 + q) * 128:(djp * 2 + q + 1) * 128],
                           hT[:, k, col0:col0 + n], k == 0, k == 7) for q in range(2) for k in range(8)],
                         reads=ck("h", col0, n) + Wga.keys, writes=[("ps", bg)])
                    sc, sck, sch = scr.get()
                    S.op("act", lambda: A.activation(out=sc[:, :, 0:n], in_=q2(ps[:, bg, 0:2 * n]), func=AF.Sigmoid),
                         reads=[("ps", bg)], writes=[sck])
                    gates[djp] = (sc, sck, sch)

                gate(0)
                gate(1)
                pl, plk, plh = bigp.get()
                for ci in range(nch):
                    c = col0 // 128 + ci
                    slot = slots[c]
                    if c == first_chunk:
                        prev_ap, prev_key = zpc[:, l, :], ("zpc", l)
                    else:
                        prev_ap, prev_key = zp_tm[:, slots[c - 1], :], ("zp", slots[c - 1])
                    boff = CB_BFIRST if (s == 0 and c == 2) else CB_BCUR
                    for jh in range(2):
                        b = bank()
                        grp = []
                        for jj in range(4):
                            j = jh * 4 + jj
                            g = j // 2
                            o = ps[:, b, jj * 128:(jj + 1) * 128]
                            grp.append((o, zp_tm[:, slot, j * 128:(j + 1) * 128], Bm(boff, g), True, False))
                            grp.append((o, prev_ap[:, j * 128:(j + 1) * 128], Bm(CB_BPREV, g), False, True))
                        S.mm(grp, reads=[("zp", slot), prev_key, "cb"], writes=[("ps", b)])
                        S.op("act", lambda b=b, jh=jh, ci=ci: A.copy(
                            out=pl[:, jh * 4:(jh + 1) * 4, ci * 128:(ci + 1) * 128],
                            in_=ps[:, b, :].rearrange("p (j t) -> p j t", j=4)),
                            reads=[("ps", b)], writes=[plk], sub=(ci, jh))
                gate(2)
                yp, ypk, yph = bigp.get()
                for djp in range(4):
                    b = bank()
                    grp = []
                    for q in range(2):
                        dj = djp * 2 + q
                        g = dj // 2
                        for cc in range(2):
                            grp.append((ps[:, b, q * n:(q + 1) * n],
                                        wp.ap[:, g, cc, (dj % 2) * 128:(dj % 2 + 1) * 128],
                                        pl[:, 2 * g + cc, 0:n], cc == 0, cc == 1))
                    S.mm(grp, reads=[plk] + wp.keys, writes=[("ps", b)])
                    for q in range(2):
                        dj = djp * 2 + q
                        S.op("act", lambda b=b, q=q, dj=dj: A.activation(
                            out=yp[:, dj, 0:n], in_=ps[:, b, q * n:(q + 1) * n], func=AF.Identity,
                            scale=cst[:, cl + C_PSC + dj:cl + C_PSC + dj + 1]),
                            reads=[("ps", b), "cst"], writes=[ypk], sub=dj)
                bigp.put(plh)
                if is_last:
                    ctl["release"]("wp")
                gate(3)
                if is_last:
                    ctl["release"]("Wga")
                for djp in range(4):
                    sc, sck, sch = gates[djp]
                    by = bank()
                    S.mm([(ps[:, by, q * n:(q + 1) * n], wa.ap[:, k, (djp * 2 + q) * 128:(djp * 2 + q + 1) * 128],
                           yp[:, k, 0:n], k == 0, k == 7) for q in range(2) for k in range(8)],
                         reads=[ypk] + wa.keys, writes=[("ps", by)])
                    S.op("dve", lambda by=by, sc=sc, djp=djp: V.tensor_tensor(
                        out=mixa[:, 2 * djp:2 * djp + 2, col0:col0 + n], in0=sc[:, :, 0:n],
                        in1=q2(ps[:, by, 0:2 * n]), op=ALU.mult),
                        reads=[sck, ("ps", by)], writes=ck("ma", col0, n), sub=djp)
                    scr.put(sch)
                bigp.put(yph)

            def pa_body(U):
                if tiles[0] not in st["pa_norm_done"]:
                    emit_norm(cl + C_MIXN, tiles[0][0], tiles[0][1], vcol_of(tiles[0][0]))
                for i, (col0, n) in enumerate(tiles):
                    if i + 1 < NT and tiles[i + 1] not in st["pa_norm_done"]:
                        emit_norm(cl + C_MIXN, tiles[i + 1][0], tiles[i + 1][1], vcol_of(tiles[i + 1][0]))
                    pa_tile(U, col0, n, i == NT - 1)
                last_c = (tiles[-1][0] + tiles[-1][1]) // 128 - 1
                S.op("act", lambda: A.copy(out=zpc[:, l, :], in_=zp_tm[:, slots[last_c], :]),
                     reads=[("zp", slots[last_c])], writes=[("zpc", l)])

            add_phase([("Wpool", kp(w_in, l, 0, 1024), [8, 1024]),
                       ("Wga", kp(w_in, l, 3072, 4096), [8, 1024]),
                       ("wp", w_pool[l].rearrange("g (c p) d -> p g c d", p=128), [4, 2, 256]),
                       ("wa", kp(w_a, l, 0, 1024), [8, 1024])], pa_body)

            def pb_tile(U, col0, n, is_last):
                Wu, Wv, wb, Wgb = U["Wu"], U["Wv"], U["wb"], U["Wgb"]
                nch = n // 128
                vslots = []
                vgs = []
                for ci in range(nch):
                    c = col0 // 128 + ci
                    slot = zcount[0] % 3
                    zcount[0] += 1
                    vslots.append(slot)
                    vg, vgk, vgh = vgp.get()
                    ss, ssk, ssh = ssp.get()
                    vgs.append((vg, vgk, vgh, ss, ssk, ssh, slot))
                    for cbk in range(2):
                        b = bank()
                        S.mm([(ps[:, b, :], hT[:, k, c * 128:(c + 1) * 128],
                               Wv.ap[:, k, cbk * 512:(cbk + 1) * 512], k == 0, k == 7) for k in range(8)],
                             reads=[("h", c)] + Wv.keys, writes=[("ps", b)])
                        S.op("act", lambda: A.activation(
                            out=vg[:, cbk * 512:(cbk + 1) * 512], in_=ps[:, b, :], func=AF.Gelu),
                            reads=[("ps", b)], writes=[vgk], sub=cbk)
                for (vg, vgk, vgh, ss, ssk, ssh, slot) in vgs:
                    S.op("dve", lambda: V.memset(ss[:], 0.0), writes=[ssk])
                    S.op("dve", lambda: V.scalar_tensor_tensor(
                        out=zp_tm[:, slot, :], in0=vg[:], scalar=1.0, in1=vg[:], op0=ALU.mult, op1=ALU.mult,
                        accum_out=ss[:, 0:1]), reads=[vgk, ssk], writes=[("zp", slot), ssk])
                for (vg, vgk, vgh, ss, ssk, ssh, slot) in vgs:
                    S.op("act", lambda: A.activation(
                        out=ss[:, 1:2], in_=ss[:, 0:1], func=AF.Sqrt, bias=cst[:, C_EPS:C_EPS + 1],
                        scale=1.0 / 1024.0), reads=[ssk, "cst"], writes=[ssk])
                for (vg, vgk, vgh, ss, ssk, ssh, slot) in vgs:
                    S.op("dve", lambda: V.reciprocal(out=ss[:, 1:2], in_=ss[:, 1:2]), reads=[ssk], writes=[ssk])
                for (vg, vgk, vgh, ss, ssk, ssh, slot) in vgs:
                    S.op("act", lambda: A.activation(out=vg[:], in_=vg[:], func=AF.Identity, scale=ss[:, 1:2]),
                         reads=[vgk, ssk], writes=[vgk])
                for (vg, vgk, vgh, ss, ssk, ssh, slot) in vgs:
                    S.op("pool", lambda: G.tensor_tensor(
                        out=zp_tm[:, slot, :], in0=vg[:], in1=gsgu[:, l, :], op=ALU.mult),
                        reads=[vgk, "gsgu"], writes=[("zp", slot)])
                    vgp.put(vgh)
                    ssp.put(ssh)
                if is_last:
                    ctl["release"]("Wv")
                for fp in range(4):
                    b = bank()
                    S.mm([(ps[:, b, q * n:(q + 1) * n], Wu.ap[:, k, (fp * 2 + q) * 128:(fp * 2 + q + 1) * 128],
                           hT[:, k, col0:col0 + n], k == 0, k == 7) for q in range(2) for k in range(8)],
                         reads=ck("h", col0, n) + Wu.keys, writes=[("ps", b)])
                    S.op("act", lambda b=b, fp=fp: A.activation(
                        out=ubuf[:, 2 * fp:2 * fp + 2, 0:n], in_=q2(ps[:, b, 0:2 * n]), func=AF.Gelu),
                        reads=[("ps", b)], writes=["u"], sub=fp)
                if is_last:
                    ctl["release"]("Wu")
                gt, gtk, gth = bigp.get()
                for ci in range(nch):
                    slot = vslots[ci]
                    for hh in range(2):
                        b = bank()
                        grp = []
                        for jj in range(4):
                            h = hh * 4 + jj
                            o = ps[:, b, jj * 128:(jj + 1) * 128]
                            grp.append((o, zp_tm[:, slot, h * 128:(h + 1) * 128], wsb[:, l, h, :], True, False))
                            grp.append((o, one33, brow[0:33, l, h * 128:(h + 1) * 128], False, True))
                        S.mm(grp, reads=[("zp", slot), "wsb", "brow", "cb"], writes=[("ps", b)])
                        S.op("dve", lambda b=b, hh=hh, ci=ci: V.tensor_tensor(
                            out=gt[:, hh * 4:(hh + 1) * 4, ci * 128:(ci + 1) * 128],
                            in0=ps[:, b, :].rearrange("p (j t) -> p j t", j=4),
                            in1=ubuf[:, hh * 4:(hh + 1) * 4, ci * 128:(ci + 1) * 128], op=ALU.mult),
                            reads=[("ps", b), "u"], writes=[gtk], sub=(ci, hh))
                gsc = {}
                for djp in range(4):
                    bg = bank()
                    S.mm([(ps[:, bg, q * n:(q + 1) * n], Wgb.ap[:, k, (djp * 2 + q) * 128:(djp * 2 + q + 1) * 128],
                           hT[:, k, col0:col0 + n], k == 0, k == 7) for q in range(2) for k in range(8)],
                         reads=ck("h", col0, n) + Wgb.keys, writes=[("ps", bg)])
                    sc, sck, sch = scr.get()
                    S.op("act", lambda bg=bg, sc=sc: A.activation(out=sc[:, :, 0:n], in_=q2(ps[:, bg, 0:2 * n]),
                                                                  func=AF.Sigmoid),
                         reads=[("ps", bg)], writes=[sck])
                    gsc[djp] = (sc, sck, sch)
                if is_last:
                    ctl["release"]("Wgb")
                for djp in range(4):
                    sc, sck, sch = gsc[djp]
                    by = bank()
                    S.mm([(ps[:, by, q * n:(q + 1) * n], wb.ap[:, k, (djp * 2 + q) * 128:(djp * 2 + q + 1) * 128],
                           gt[:, k, 0:n], k == 0, k == 7) for q in range(2) for k in range(8)],
                         reads=[gtk] + wb.keys, writes=[("ps", by)])
                    S.op("dve", lambda by=by, sc=sc: V.tensor_tensor(
                        out=sc[:, :, 0:n], in0=sc[:, :, 0:n], in1=q2(ps[:, by, 0:2 * n]), op=ALU.mult),
                        reads=[sck, ("ps", by)], writes=[sck])
                    S.op("pool", lambda sc=sc, djp=djp: G.tensor_tensor(
                        out=mixa[:, 2 * djp:2 * djp + 2, col0:col0 + n], in0=sc[:, :, 0:n],
                        in1=mixa[:, 2 * djp:2 * djp + 2, col0:col0 + n], op=ALU.add),
                        reads=[sck] + ck("ma", col0, n), writes=ck("ma", col0, n), sub=djp)
                    scr.put(sch)
                bigp.put(gth)

            def pb_body(U):
                for i, (col0, n) in enumerate(tiles):
                    pb_tile(U, col0, n, i == NT - 1)

            add_phase([("Wv", kp(w_in, l, 2048, 3072), [8, 1024]),
                       ("Wu", kp(w_in, l, 1024, 2048), [8, 1024]),
                       ("Wgb", kp(w_in, l, 4096, 5120), [8, 1024]),
                       ("wb", kp(w_b, l, 0, 1024), [8, 1024])], pb_body)

            def pc_body(U):
                wo = U["wo"]
                for (col0, n) in tiles:
                    S.dma("pool", pTs[:, :, col0:col0 + n],
                          pin[l, :, :, base + 256 + col0: base + 256 + col0 + n],
                          writes=ck("pT", col0, n))

                def pc(col0, n):
                    for djp in range(4):
                        b = bank()
                        S.mm([(ps[:, b, q * n:(q + 1) * n], wo.ap[:, k, (djp * 2 + q) * 128:(djp * 2 + q + 1) * 128],
                               mixa[:, k, col0:col0 + n], k == 0, k == 7) for q in range(2) for k in range(8)],
                             reads=ck("ma", col0, n) + wo.keys, writes=[("ps", b)])
                        S.op("dve", lambda b=b, djp=djp: V.tensor_tensor(
                            out=xT[:, 2 * djp:2 * djp + 2, col0:col0 + n],
                            in0=xT[:, 2 * djp:2 * djp + 2, col0:col0 + n],
                            in1=q2(ps[:, b, 0:2 * n]), op=ALU.add),
                            reads=[("ps", b)] + ck("x", col0, n), writes=ck("x", col0, n), sub=djp)

                def nrm(col0, n):
                    emit_norm(cl + C_FFNN, col0, n, vcol_of(col0))

                pc(*tiles[0])
                for i in range(NT):
                    if i + 1 < NT:
                        pc(*tiles[i + 1])
                    nrm(*tiles[i])

            add_phase([("wo", kp(w_out, l, 0, 1024), [8, 1024])], pc_body)

            def make_ffn(bi, j0, cbn):
                last_block = (bi == len(FFN_BLOCKS) - 1)

                def up(U, col0, n, tpar):
                    upA, upB = U["upA"], U["upB"]
                    at, atk, ath = atp.get()
                    accs = {}

                    def stA(jl):
                        j = j0 + jl
                        b = bank()
                        grp = [(ps[:, b, 0:n], upA.ap[:, k, jl * 128:(jl + 1) * 128], hT[:, k, col0:col0 + n],
                                k == 0, k == 7) for k in range(8)]
                        grp += [(ps[:, b, n:2 * n], upB.ap[:, k, jl * 128:(jl + 1) * 128], hT[:, k, col0:col0 + n],
                                 k == 0, k == 7) for k in range(8)]
                        S.mm(grp, reads=ck("h", col0, n) + upA.keys + upB.keys, writes=[("ps", b)])
                        acc, acck, acch = scr.get()
                        accs[jl] = (acc, acck, acch, b)
                        S.op("dve", lambda: V.memset(acc[:, :, 0:1], 0.0), writes=[acck, (acck, 0), (acck, 1)],
                             sub="m0")
                        S.op("dve", lambda: V.memset(acc[:, :, n + 1:n + 2], 0.0), writes=[(acck, 0), (acck, 1)],
                             sub="m1")
                        for q, f in ((0, j), (1, NFF + j)):
                            w1 = cst[:, cl + C_CW + 44 + f:cl + C_CW + 44 + f + 1]
                            bb = cst[:, cl + C_CB + f:cl + C_CB + f + 1]
                            S.op("act", lambda: A.activation(
                                out=acc[:, q, 1:n + 1], in_=ps[:, b, q * n:q * n + n], func=AF.Identity,
                                scale=w1, bias=bb),
                                reads=[("ps", b), "cst"], writes=[(acck, q)])
                        S.op("pool", lambda: G.tensor_tensor(
                            out=acc[:, :, 0:2], in0=acc[:, :, 0:2],
                            in1=upc[:, l, tpar, j, :].rearrange("p (q t) -> p q t", q=2), op=ALU.add),
                            reads=[("upc", l, tpar, j), (acck, 0), (acck, 1)], writes=[(acck, 0), (acck, 1)],
                            sub="head")

                    def stB(jl):
                        j = j0 + jl
                        acc, acck, acch, b = accs[jl]
                        for q, f in ((0, j), (1, NFF + j)):
                            w2 = cst[:, cl + C_CW + 88 + f:cl + C_CW + 88 + f + 1]
                            S.op("dve", lambda: V.scalar_tensor_tensor(
                                out=acc[:, q, 0:n], in0=ps[:, b, q * n:q * n + n], scalar=w2,
                                in1=acc[:, q, 0:n], op0=ALU.mult, op1=ALU.add),
                                reads=[("ps", b), (acck, q), "cst"], writes=[(acck, q)])
                        for q, f in ((0, j), (1, NFF + j)):
                            w0 = cst[:, cl + C_CW + f:cl + C_CW + f + 1]
                            S.op("dve", lambda: V.scalar_tensor_tensor(
                                out=acc[:, q, 2:n + 2], in0=ps[:, b, q * n:q * n + n], scalar=w0,
                                in1=acc[:, q, 2:n + 2], op0=ALU.mult, op1=ALU.add),
                                reads=[("ps", b), (acck, q), "cst"], writes=[(acck, q)])
                        if "copy" in SKIP:
                            pass
                        elif os.environ.get("KV_ACTCOPY"):
                            S.op("act", lambda: A.copy(
                                out=upc[:, l, 1 - tpar, j, :].rearrange("p (q t) -> p q t", q=2), in_=acc[:, :, n:n + 2]),
                                reads=[(acck, 0), (acck, 1)], writes=[("upc", l, 1 - tpar, j)], sub="tail")
                        else:
                            S.op("pool", lambda: G.tensor_copy(
                                out=upc[:, l, 1 - tpar, j, :].rearrange("p (q t) -> p q t", q=2), in_=acc[:, :, n:n + 2]),
                                reads=[(acck, 0), (acck, 1)], writes=[("upc", l, 1 - tpar, j)], sub="tail")

                    def stC(jl):
                        acc, acck, acch, b = accs[jl]
                        S.op("act", lambda: A.activation(out=acc[:, 0, 0:n], in_=acc[:, 0, 0:n], func=AF.Gelu),
                             reads=[(acck, 0)], writes=[(acck, 0)])
                        S.op("pool", lambda: G.tensor_tensor(
                            out=at[:, jl, 0:n], in0=acc[:, 0, 0:n], in1=acc[:, 1, 0:n], op=ALU.mult),
                            reads=[acck, (acck, 0), (acck, 1)], writes=[atk])
                        scr.put(acch)

                    for step in range(cbn + 2):
                        if step < cbn:
                            stA(step)
                        if 0 <= step - 1 < cbn:
                            stB(step - 1)
                        if 0 <= step - 2 < cbn:
                            stC(step - 2)
                    return (at, atk, ath)

                def down(U, col0, n, atr):
                    dwn = U["dwn"]
                    at, atk, ath = atr
                    for djp in range(4):
                        b = bank()
                        S.mm([(ps[:, b, q * n:(q + 1) * n], dwn.ap[:, jl, (djp * 2 + q) * 128:(djp * 2 + q + 1) * 128],
                               at[:, jl, 0:n], jl == 0, jl == cbn - 1) for q in range(2) for jl in range(cbn)],
                             reads=[atk] + dwn.keys, writes=[("ps", b)])
                        S.op("dve", lambda b=b, djp=djp: V.tensor_tensor(
                            out=xT[:, 2 * djp:2 * djp + 2, col0:col0 + n],
                            in0=xT[:, 2 * djp:2 * djp + 2, col0:col0 + n],
                            in1=q2(ps[:, b, 0:2 * n]), op=ALU.add),
                            reads=[("ps", b)] + ck("x", col0, n), writes=ck("x", col0, n), sub=djp)
                    atp.put(ath)

                def body(U):
                    ats = {}
                    ats[0] = up(U, tiles[0][0], tiles[0][1], (tbase + 0) % 2)
                    for i in range(NT):
                        if i + 1 < NT:
                            ats[i + 1] = up(U, tiles[i + 1][0], tiles[i + 1][1], (tbase + i + 1) % 2)
                            if i + 1 == NT - 1:
                                ctl["release"]("upA")
                                ctl["release"]("upB")
                        if last_block and i == 1:
                            emit_norm(cl + C_PLEN, tiles[0][0], tiles[0][1], vcol_of(tiles[0][0]))
                            st["ple_norm_done"].add(tiles[0])
                        down(U, tiles[i][0], tiles[i][1], ats[i])

                add_phase([("upA", kp(w_up, l, j0 * 128, (j0 + cbn) * 128), [8, cbn * 128]),
                           ("upB", kp(w_up, l, DFF + j0 * 128, DFF + (j0 + cbn) * 128), [8, cbn * 128]),
                           ("dwn", w_down[l, j0 * 128:(j0 + cbn) * 128, :].rearrange("(j p) c -> p j c", p=128),
                            [cbn, 1024])], body)

            for bi, (j0, cbn) in enumerate(FFN_BLOCKS):
                make_ffn(bi, j0, cbn)

            def ple_body(U):
                wpg, wpl = U["wpg"], U["wpl"]
                pend = []

                def finalize(col0, n):
                    if base + col0 >= 0:
                        emit_norm(C_FIN, col0, n, None, out_f32=ubuf)
                        oc = base + col0
                        out_tickets.append(S.dma("sp", out[:, :, oc:oc + n], ubuf[:, :, 0:n], reads=["u"]))
                    if s + 1 < 3:
                        c0 = (col0 // 256) * 256
                        nb = bases[s + 1]
                        S.dma("sp", xT[:, :, c0:c0 + 256], xin[:, :, nb + 256 + c0: nb + 256 + c0 + 256],
                              writes=ck("x", c0, 256))

                for ti, (col0, n) in enumerate(tiles):
                    for tj in (ti, ti + 1):
                        if tj < NT and tiles[tj] not in st["ple_norm_done"]:
                            emit_norm(cl + C_PLEN, tiles[tj][0], tiles[tj][1], vcol_of(tiles[tj][0]))
                            st["ple_norm_done"].add(tiles[tj])
                    for djp in range(4):
                        bg = bank()
                        S.mm([(ps[:, bg, q * n:(q + 1) * n], wpg.ap[:, k, (djp * 2 + q) * 128:(djp * 2 + q + 1) * 128],
                               hT[:, k, col0:col0 + n], k == 0, k == 7) for q in range(2) for k in range(8)],
                             reads=ck("h", col0, n) + wpg.keys, writes=[("ps", bg)])
                        sc, sck, sch = scr.get()
                        S.op("act", lambda: A.activation(out=sc[:, :, 0:n], in_=q2(ps[:, bg, 0:2 * n]),
                                                         func=AF.Sigmoid),
                             reads=[("ps", bg)], writes=[sck])
                        be = bank()
                        S.mm([(ps[:, be, q * n:(q + 1) * n], wpl.ap[:, kk, (djp * 2 + q) * 128:(djp * 2 + q + 1) * 128],
                               pTs[:, kk, col0:col0 + n], kk == 0, kk == 1) for q in range(2) for kk in range(2)],
                             reads=ck("pT", col0, n) + wpl.keys, writes=[("ps", be)])
                        S.op("dve", lambda: V.tensor_tensor(
                            out=sc[:, :, 0:n], in0=sc[:, :, 0:n], in1=q2(ps[:, be, 0:2 * n]), op=ALU.mult),
                            reads=[sck, ("ps", be)], writes=[sck])
                        S.op("pool", lambda: G.tensor_tensor(
                            out=xT[:, 2 * djp:2 * djp + 2, col0:col0 + n],
                            in0=xT[:, 2 * djp:2 * djp + 2, col0:col0 + n], in1=sc[:, :, 0:n], op=ALU.add),
                            reads=[sck] + ck("x", col0, n), writes=ck("x", col0, n), sub=djp)
                        scr.put(sch)
                    if l == 1:
                        if pend:
                            finalize(*pend.pop())
                        pend.append((col0, n))
                while pend:
                    finalize(*pend.pop())

            add_phase([("wpg", kp(w_pg, l, 0, 1024), [8, 1024]),
                       ("wpl", kp(w_ple, l, 0, 1024), [2, 1024])], ple_body)

        def x_load_phase(s):
            base = bases[s]

            def body(U):
                return
            add_phase([], body)

        for s in range(3):
            x_load_phase(s)
            for l in range(2):
                make_sl(s, l)

        ring = {"ptr": 0}
        live = {}
        loaded = {}
        regions = {}

        def try_load(pi, j):
            name, src, dims = phases[pi][0][j]
            size = 1
            for d_ in dims:
                size *= d_
            assert size % 1024 == 0
            npg = size // 1024
            if os.environ.get("KV_FIFO"):
                p0 = ring["ptr"]
                if p0 + npg > ring_pages:
                    p0 = 0
                for regs in live.values():
                    for (a, b_) in regs:
                        if not (p0 + npg <= a or p0 >= b_):
                            return False
                ring["ptr"] = p0 + npg
            else:
                occ = [r for regs in live.values() for r in regs]
                if npg >= 6:
                    cands = list(range(0, ring_pages - npg + 1, npg))
                else:
                    cands = list(range(ring_pages - npg, -1, -1))
                p0 = None
                for c in cands:
                    if all(c + npg <= a or c >= b_ for (a, b_) in occ):
                        p0 = c
                        break
                if p0 is None:
                    return False
            live.setdefault(pi, []).append((p0, p0 + npg))
            regions[(pi, j)] = (p0, p0 + npg)
            if os.environ.get("KV_DBG"):
                print("LOAD phase", pi, name, "pages", p0, p0 + npg, "at_phase", cur_phase[0])
            flat = wr[:, p0 * 1024:p0 * 1024 + size]
            if len(dims) == 2:
                ap = flat.rearrange("p (a b) -> p a b", a=dims[0])
            else:
                ap = flat.rearrange("p (a b c) -> p a b c", a=dims[0], b=dims[1])
            keys = [("wr", pg) for pg in range(p0, p0 + npg)]
            S.dma("pool", ap, src, reads=(), writes=keys)
            loaded[(pi, j)] = Unit(ap, keys)
            return True

        cur_phase = [0]

        def prefetch_from(pi):
            for pj in range(pi + 1, min(pi + 5, len(phases))):
                for j in range(len(phases[pj][0])):
                    if (pj, j) not in loaded:
                        if not try_load(pj, j):
                            return

        def release_unit(name):
            pi = cur_phase[0]
            units = phases[pi][0]
            j = [u[0] for u in units].index(name)
            live[pi].remove(regions[(pi, j)])
            prefetch_from(pi)

        ctl["release"] = release_unit
        for pi in range(len(phases)):
            cur_phase[0] = pi
            units, body = phases[pi]
            for j in range(len(units)):
                if (pi, j) not in loaded:
                    ok = try_load(pi, j)
                    assert ok, "weight ring too small for phase %d" % pi
            prefetch_from(pi)
            body({units[j][0]: loaded[(pi, j)] for j in range(len(units))})
            live.pop(pi, None)

        for t in out_tickets:
            S._wait("sp", [t])
    return nc


_CACHE = {}


def _pool_mats(first):
    Bcur = np.zeros((4, 128, 128), np.float32)
    Bprev = np.zeros((4, 128, 128), np.float32)
    for g, w in enumerate((2, 4, 8, 16)):
        for t in range(128):
            cnt = min(t + 1, w) if first else w
            for sg in range(t - w + 1, t + 1):
                if sg >= 0:
                    Bcur[g, sg, t] += 1.0 / cnt
                elif not first:
                    Bprev[g, sg + 128, t] += 1.0 / cnt
            Bcur[g, t, t] -= 1.0
    return Bcur, Bprev


def kernel(x, p, mix_norm, w_in, w_pool, pool_scale, sgu_norm, w_spatial, b_spatial,
           w_branch_a, w_branch_b, w_out, ffn_norm, w_up, conv_w, conv_b, w_down,
           ple_norm, w_ple_gate, w_ple, final_norm):
    f = np.float32
    x = np.asarray(x, f)
    p = np.asarray(p, f)
    if "nc" not in _CACHE:
        _CACHE["nc"] = build_program()
    nc = _CACHE["nc"]

    def vec8(v):
        return np.asarray(v, f).reshape(8, 128).T

    Bcur, Bprev = _pool_mats(False)
    Bfirst, _ = _pool_mats(True)
    mask = (np.arange(128)[None, :] >= np.arange(128)[:, None]).astype(f)
    gsgu = np.ascontiguousarray(np.broadcast_to(np.asarray(sgu_norm, f)[None, :, :], (128, 2, 1024)))
    bTt = np.ascontiguousarray(np.broadcast_to(np.asarray(b_spatial, f).reshape(1, 2, 1024), (128, 2, 1024)))
    wsT = np.ascontiguousarray(np.transpose(np.asarray(w_spatial, f), (3, 0, 1, 2)).reshape(128, 2, 1024))
    shared = {
        "w_in": np.asarray(w_in, f), "w_pool": np.asarray(w_pool, f),
        "w_branch_a": np.asarray(w_branch_a, f), "w_branch_b": np.asarray(w_branch_b, f),
        "w_out": np.asarray(w_out, f), "w_up": np.asarray(w_up, f), "w_down": np.asarray(w_down, f),
        "w_ple_gate": np.asarray(w_ple_gate, f), "w_ple": np.asarray(w_ple, f),
        "gsgu": gsgu, "bT": bTt, "wsT": wsT, "mask": mask,
    }
    cst0 = np.zeros((128, NCST), f)
    for l in range(2):
        o = l * LW
        cst0[:, o + C_MIXN:o + C_MIXN + 8] = vec8(mix_norm[l])
        cst0[:, o + C_FFNN:o + C_FFNN + 8] = vec8(ffn_norm[l])
        cst0[:, o + C_PLEN:o + C_PLEN + 8] = vec8(ple_norm[l])
        cst0[:, o + C_PSC:o + C_PSC + 8] = vec8(pool_scale[l])
        for k in range(3):
            cst0[:, o + C_CW + k * 44:o + C_CW + (k + 1) * 44] = np.asarray(conv_w[l][k], f).reshape(44, 128).T
        cst0[:, o + C_CB:o + C_CB + 44] = np.asarray(conv_b[l], f).reshape(44, 128).T
    cst0[:, C_FIN:C_FIN + 8] = vec8(final_norm)
    cst0[:, C_EPS:C_EPS + 8] = EPS

    in_maps = []
    for core in range(8):
        b = core // 4
        t0 = (core % 4) * TOK
        xs = np.zeros((TIN, D), f)
        ps_ = np.zeros((2, TIN, 256), f)
        lo = t0 - HALO
        if lo >= 0:
            xs[:] = x[b, lo:t0 + TOK]
            ps_[:] = p[:, b, lo:t0 + TOK]
        else:
            xs[HALO:] = x[b, 0:TOK]
            ps_[:, HALO:] = p[:, b, 0:TOK]
        xTc = np.ascontiguousarray(xs.T.reshape(8, 128, TIN).transpose(1, 0, 2))
        pTc = np.ascontiguousarray(ps_.transpose(0, 2, 1).reshape(2, 2, 128, TIN).transpose(0, 2, 1, 3))
        cstc = cst0.copy()
        cstc[:, C_VALID:C_VALID + 256] = 1.0 if lo >= 0 else 0.0
        cbc = np.zeros((128, NCSTB), f)
        cbc[:, CB_ONES:CB_ONES + 128] = 1.0 / 1024.0
        cbc[:, CB_ONE1:CB_ONE1 + 128] = 1.0
        for g in range(4):
            cbc[:, CB_BCUR + g * 128:CB_BCUR + (g + 1) * 128] = Bcur[g]
            cbc[:, CB_BPREV + g * 128:CB_BPREV + (g + 1) * 128] = Bprev[g]
            cbc[:, CB_BFIRST + g * 128:CB_BFIRST + (g + 1) * 128] = (Bfirst[g] if lo < 0 else Bcur[g])
        m = dict(shared)
        m.update({"xT": xTc, "pT": pTc, "cst": cstc, "cstb": cbc})
        in_maps.append(m)

    res = run_bass_kernel_spmd(nc, in_maps, core_ids=list(range(8)))
    outp = np.zeros((2, SEQ, D), f)
    for core in range(8):
        b = core // 4
        t0 = (core % 4) * TOK
        oT = np.asarray(res.results[core]["outT"], f)
        outp[b, t0:t0 + TOK] = oT.transpose(2, 1, 0).reshape(TOK, D)
    return outp
```
